# Optimizing a Trainium2 kernel written in Bass

```python
import math
import jax, jax.numpy as jnp
from jax import lax
import numpy as np

D_MODEL = 4096
BATCH = 4
SEQ = 4096
DEPTH = 2
DEC_BATCH = 32
DEC_SEQ = 32
PAST_LEN = 4096

CHUNK = 64
N_HEADS = 32
HEAD_DIM = D_MODEL // N_HEADS
N_KV = 8
GROUP = N_HEADS // N_KV
ATTN_WIDTH = N_HEADS * HEAD_DIM
H_IDX = 32
D_IDX = 64
TOPK_MAX = 256
Q_BLOCK = 128
NUM_BUCKETS = 32
MAX_DISTANCE = 128
POOL_WINDOWS = (2, 4, 8, 16)
N_POOL_GROUPS = len(POOL_WINDOWS)
POOL_WIDTH = D_MODEL
GROUP_W = POOL_WIDTH // N_POOL_GROUPS
POOL_HIST = max(POOL_WINDOWS) - 1
EPS = 1e-6
N_ATTN_LAYERS = (DEPTH + 1) // 2
N_POOL_LAYERS = DEPTH // 2

Q_W = ATTN_WIDTH
KV_W = N_KV * HEAD_DIM
QI_W = H_IDX * D_IDX
KI_W = D_IDX
WI_W = H_IDX
ATTN_IN = Q_W + 2 * KV_W + QI_W + KI_W + WI_W + ATTN_WIDTH
ATTN_SPLITS = (Q_W, Q_W + KV_W, Q_W + 2 * KV_W, Q_W + 2 * KV_W + QI_W,
               Q_W + 2 * KV_W + QI_W + KI_W, Q_W + 2 * KV_W + QI_W + KI_W + WI_W)

kernel_name = "hybrid_dsa_pool_streaming_step"

f32 = jnp.float32


def rmsnorm(x, w):
    xf = x.astype(f32)
    y = xf * lax.rsqrt(jnp.mean(xf * xf, axis=-1, keepdims=True) + EPS)
    return (y * w.astype(f32)).astype(x.dtype)


def rel_bucket(rel):
    nb = NUM_BUCKETS // 2
    ret = jnp.where(rel > 0, nb, 0)
    n = jnp.abs(rel)
    max_exact = nb // 2
    nf = jnp.maximum(n, 1).astype(f32)
    large = max_exact + (jnp.log(nf / max_exact) / math.log(MAX_DISTANCE / max_exact)
                         * (nb - max_exact)).astype(jnp.int32)
    large = jnp.minimum(large, nb - 1)
    return ret + jnp.where(n < max_exact, n, large)


def dsa_select_attend(q, qi, wi, q_pos, k_all, v_all, ki_all, k_pos, k_sel, rel_bias):
    B, Q = q.shape[:2]
    dots = jnp.einsum('bqhd,bsd->bqhs', qi, ki_all, preferred_element_type=f32) * (D_IDX ** -0.5)
    score = jnp.einsum('bqhs,bqh->bqs', jax.nn.relu(dots), wi.astype(f32) * (H_IDX ** -0.5))
    adm = (k_pos[None, :] // CHUNK) <= (q_pos[:, None] // CHUNK)
    score = jnp.where(adm[None], score, -jnp.inf)
    _, idx = lax.top_k(score, k_sel)
    take = jax.vmap(lambda a, i: a[i])
    k_g = take(k_all, idx)
    v_g = take(v_all, idx)
    sel_pos = k_pos[idx]
    valid = (sel_pos // CHUNK) <= (q_pos[None, :, None] // CHUNK)
    bias = rel_bias[rel_bucket(sel_pos - q_pos[None, :, None])]
    bias = bias.reshape(B, Q, k_sel, N_KV, GROUP).transpose(0, 1, 3, 4, 2)
    qg = q.reshape(B, Q, N_KV, GROUP, HEAD_DIM)
    logits = jnp.einsum('bqkgd,bqjkd->bqkgj', qg, k_g, preferred_element_type=f32) * (HEAD_DIM ** -0.5)
    logits = logits + bias.astype(f32)
    logits = jnp.where(valid[:, :, None, None, :], logits, -jnp.inf)
    p = jax.nn.softmax(logits, axis=-1)
    out = jnp.einsum('bqkgj,bqjkd->bqkgd', p.astype(v_g.dtype), v_g, preferred_element_type=f32)
    return out.reshape(B, Q, ATTN_WIDTH).astype(q.dtype)


def attn_project(h, w_in):
    B, T, _ = h.shape
    proj = h @ w_in
    q, k, v, qi, ki, wi, gate = jnp.split(proj, ATTN_SPLITS, axis=-1)
    return (q.reshape(B, T, N_HEADS, HEAD_DIM), k.reshape(B, T, N_KV, HEAD_DIM),
            v.reshape(B, T, N_KV, HEAD_DIM), qi.reshape(B, T, H_IDX, D_IDX), ki, wi, gate)


def prompt_attend(q, qi, wi, k, v, ki, rel_bias):
    B, S = q.shape[:2]
    nb = S // Q_BLOCK
    pos = jnp.arange(S)
    k_sel = min(TOPK_MAX, S // 4)

    def to_blocks(a):
        return jnp.moveaxis(a.reshape((B, nb, Q_BLOCK) + a.shape[2:]), 1, 0)

    def blk(args):
        qb, qib, wib, pb = args
        return dsa_select_attend(qb, qib, wib, pb, k, v, ki, pos, k_sel, rel_bias)

    out = lax.map(blk, (to_blocks(q), to_blocks(qi), to_blocks(wi), pos.reshape(nb, Q_BLOCK)))
    return jnp.moveaxis(out, 0, 1).reshape(B, S, ATTN_WIDTH)


def sample_attend(q, qi, wi, k, v, ki, ck, cv, cki, rel_bias):
    T = q.shape[1]
    P = ck.shape[1]
    k_all = jnp.concatenate([ck.astype(k.dtype), k], axis=1)
    v_all = jnp.concatenate([cv.astype(v.dtype), v], axis=1)
    ki_all = jnp.concatenate([cki.astype(ki.dtype), ki], axis=1)
    k_pos = jnp.arange(P + T)
    q_pos = P + jnp.arange(T)
    k_sel = min(TOPK_MAX, (P + T) // 4)
    return dsa_select_attend(q, qi, wi, q_pos, k_all, v_all, ki_all, k_pos, k_sel, rel_bias)


def pool_mix(u, hist, start):
    B, T, E = u.shape
    P = POOL_HIST
    ext = jnp.concatenate([jnp.zeros((B, 1, E), f32), hist.astype(f32), u.astype(f32)], axis=1)
    c = jnp.cumsum(ext, axis=1)
    ends = c[:, P + 1:P + 1 + T]
    pos = start + jnp.arange(T)
    outs = []
    for g, w in enumerate(POOL_WINDOWS):
        lo, hi = g * GROUP_W, (g + 1) * GROUP_W
        s = ends[..., lo:hi] - c[:, P + 1 - w:P + 1 - w + T, lo:hi]
        cnt = jnp.minimum(pos + 1, w).astype(f32)
        outs.append(s / cnt[None, :, None])
    mean = jnp.concatenate(outs, axis=-1)
    return (mean - u.astype(f32)).astype(u.dtype)


def pool_layer(h, hist, start, w_in, group_w, scale, w_out):
    B, T, _ = h.shape
    proj = h @ w_in
    u, gate = proj[..., :POOL_WIDTH], proj[..., POOL_WIDTH:]
    m = pool_mix(u, hist, start)
    z = jnp.einsum('btgc,gcd->btgd', m.reshape(B, T, N_POOL_GROUPS, GROUP_W), group_w).reshape(B, T, POOL_WIDTH)
    z = z * scale
    out = (z * jax.nn.silu(gate)) @ w_out
    new_hist = jnp.concatenate([hist.astype(u.dtype), u], axis=1)[:, -POOL_HIST:]
    return out, new_hist


def setup_inputs(seed: int = 0) -> dict:
    key = jax.random.key(seed)
    ks = jax.random.split(key, 16)
    NA, NP = N_ATTN_LAYERS, N_POOL_LAYERS
    nrm = jax.random.normal
    return {
        "x_prompt": nrm(ks[0], (BATCH, SEQ, D_MODEL), f32),
        "x_sample": nrm(ks[1], (DEC_BATCH, DEC_SEQ, D_MODEL), f32),
        "cache_k": nrm(ks[2], (NA, DEC_BATCH, PAST_LEN, N_KV, HEAD_DIM), f32),
        "cache_v": nrm(ks[3], (NA, DEC_BATCH, PAST_LEN, N_KV, HEAD_DIM), f32),
        "cache_kidx": nrm(ks[4], (NA, DEC_BATCH, PAST_LEN, D_IDX), f32),
        "state_pool": nrm(ks[5], (NP, DEC_BATCH, POOL_HIST, POOL_WIDTH), f32),
        "norm_w": 1.0 + 0.02 * nrm(ks[6], (DEPTH, D_MODEL), f32),
        "final_norm_w": 1.0 + 0.02 * nrm(ks[7], (D_MODEL,), f32),
        "attn_w_in": nrm(ks[8], (NA, D_MODEL, ATTN_IN), f32) * D_MODEL ** -0.5,
        "attn_w_out": nrm(ks[9], (NA, ATTN_WIDTH, D_MODEL), f32) * ATTN_WIDTH ** -0.5,
        "rel_bias": 0.2 * nrm(ks[10], (NUM_BUCKETS, N_HEADS), f32),
        "pool_w_in": nrm(ks[11], (NP, D_MODEL, 2 * POOL_WIDTH), f32) * D_MODEL ** -0.5,
        "pool_group_w": nrm(ks[12], (NP, N_POOL_GROUPS, GROUP_W, GROUP_W), f32) * GROUP_W ** -0.5,
        "pool_scale": 1.0 + 0.05 * nrm(ks[13], (NP, POOL_WIDTH), f32),
        "pool_w_out": nrm(ks[14], (NP, POOL_WIDTH, D_MODEL), f32) * POOL_WIDTH ** -0.5,
    }


def reference(x_prompt, x_sample, cache_k, cache_v, cache_kidx, state_pool, norm_w, final_norm_w,
              attn_w_in, attn_w_out, rel_bias, pool_w_in, pool_group_w, pool_scale, pool_w_out):
    xp, xs = x_prompt, x_sample
    kp, vp, kip, poolp = [], [], [], []
    ksm, vsm, kism, poolsm = [], [], [], []
    for i in range(DEPTH):
        hp = rmsnorm(xp, norm_w[i])
        hs = rmsnorm(xs, norm_w[i])
        if i % 2 == 0:
            a = i // 2
            q, k, v, qi, ki, wi, g = attn_project(hp, attn_w_in[a])
            o = prompt_attend(q, qi, wi, k, v, ki, rel_bias)
            xp = xp + (o * jax.nn.silu(g)) @ attn_w_out[a]
            kp.append(k); vp.append(v); kip.append(ki)
            q, k, v, qi, ki, wi, g = attn_project(hs, attn_w_in[a])
            o = sample_attend(q, qi, wi, k, v, ki, cache_k[a], cache_v[a], cache_kidx[a], rel_bias)
            xs = xs + (o * jax.nn.silu(g)) @ attn_w_out[a]
            ksm.append(k); vsm.append(v); kism.append(ki)
        else:
            p = i // 2
            zero_hist = jnp.zeros((xp.shape[0], POOL_HIST, POOL_WIDTH), xp.dtype)
            out, hist = pool_layer(hp, zero_hist, 0, pool_w_in[p], pool_group_w[p], pool_scale[p], pool_w_out[p])
            xp = xp + out
            poolp.append(hist)
            out, hist = pool_layer(hs, state_pool[p], PAST_LEN, pool_w_in[p], pool_group_w[p], pool_scale[p], pool_w_out[p])
            xs = xs + out
            poolsm.append(hist)
    y_prompt = rmsnorm(xp, final_norm_w)
    y_sample = rmsnorm(xs, final_norm_w)
    return (y_prompt, y_sample, jnp.stack(kp), jnp.stack(vp), jnp.stack(kip), jnp.stack(poolp),
            jnp.stack(ksm), jnp.stack(vsm), jnp.stack(kism), jnp.stack(poolsm))
```

```python
import math
from contextlib import ExitStack
import numpy as np
import ml_dtypes
import concourse.bass as bass
import concourse.mybir as mybir
from concourse.bass_utils import run_bass_kernel_spmd

F32 = mybir.dt.float32
BF16 = mybir.dt.bfloat16
AF = mybir.ActivationFunctionType
ALU = mybir.AluOpType
AX = mybir.AxisListType

D = 4096
KC = 32
NFULL = 18
NKV = 15
TOKF = NFULL * 128
NKEY = 4224
ATTN_IN = 12384
NIT = 18
NEG = -30000.0
BIGNEG = -1.0e30
EPS = 1e-6
C_Q, C_K, C_V, C_QI, C_KI, C_WI, C_G = 0, 4096, 5120, 6144, 8192, 8256, 8288
POOL_WINDOWS = (2, 4, 8, 16)


class Buf:
    def __init__(self, name):
        self.name = name
        self.last_w = None
        self.readers = []
        self.sem = None
        self.semcount = 0


class Prog:
    ENG = ("sp", "act", "pool", "dve", "pe")

    def __init__(self, nc, stack):
        self.nc = nc
        self.stack = stack
        self.engs = {"sp": nc.sync, "act": nc.scalar, "pool": nc.gpsimd, "dve": nc.vector, "pe": nc.tensor}
        self.esem = {k: stack.enter_context(nc.semaphore("es_" + k)) for k in self.ENG}
        self.ecount = {k: 0 for k in self.ENG}
        self.waited = {k: {} for k in self.ENG}
        self.q = {k: [] for k in self.ENG}
        self.dma_events = {}
        self.nbuf = 0
        self.free_sems = []

    def buf(self, name=None):
        self.nbuf += 1
        return Buf(name or "b%d" % self.nbuf)

    def _bufsem(self, b):
        if b.sem is None:
            if self.free_sems:
                b.sem, b.semcount = self.free_sems.pop()
            else:
                b.sem = self.stack.enter_context(self.nc.semaphore("bs%d" % self.nbuf + b.name))
                b.semcount = 0
        return b.sem

    def release(self, bufs):
        return

    def op(self, eng, emit, reads=(), writes=(), dma=None, ndma=1):
        waits = {}

        def need(ev):
            if ev is None:
                return
            s, v = ev
            if eng == "pe" and s is self.esem["pe"]:
                return
            k = id(s)
            if k not in waits or waits[k][1] < v:
                waits[k] = (s, v)

        for b in reads:
            need(b.last_w)
        for b in writes:
            need(b.last_w)
            for r in b.readers:
                need(r)
        wl = []
        for k, (s, v) in waits.items():
            if self.waited[eng].get(k, 0) >= v:
                continue
            self.waited[eng][k] = v
            wl.append((s, v))
        if dma is not None:
            s = self._bufsem(dma)
            dma.semcount += 16 * ndma
            ev = (s, dma.semcount)
            self.dma_events[id(s)] = ev
        else:
            self.ecount[eng] += 1
            ev = (self.esem[eng], self.ecount[eng])
        for b in writes:
            b.last_w = ev
            b.readers = []
        for b in reads:
            if b not in writes:
                b.readers.append(ev)
                if len(b.readers) > 24:
                    b.readers = b.readers[-24:]
        self.q[eng].append((wl, emit, dma is not None, ev, ndma))
        return ev

    def drain(self, eng="sp"):
        wl = list(self.dma_events.values())
        for k in self.ENG:
            if k != eng and self.ecount[k] > 0:
                wl.append((self.esem[k], self.ecount[k]))
        wl2 = []
        for (s, v) in wl:
            if self.waited[eng].get(id(s), 0) >= v:
                continue
            self.waited[eng][id(s)] = v
            wl2.append((s, v))
        self.q[eng].append((wl2, None, False, None, 0))
        self.dma_events = {}

    def emit_block(self):
        nc = self.nc
        with nc.Block() as block:
            for k in self.ENG:
                if not self.q[k]:
                    continue
                ql = self.q[k]

                def body(e, ql=ql):
                    for (wl, emit, isdma, ev, ndma) in ql:
                        for (s, v) in wl:
                            e.wait_ge(s, v)
                        if emit is None:
                            continue
                        r = emit(e)
                        if isdma:
                            if not isinstance(r, (list, tuple)):
                                r = [r]
                            assert len(r) == ndma, (len(r), ndma)
                            for ins in r:
                                ins.then_inc(ev[0], 16)
                        else:
                            r.then_inc(ev[0], 1)

                getattr(block, {"sp": "sync", "act": "scalar", "pool": "gpsimd", "dve": "vector", "pe": "tensor"}[k])(body)
        self.q = {k: [] for k in self.ENG}


class T:
    def __init__(self, P, st, name, shape, dt, psum=False):
        nc = P.nc
        self.t = st.enter_context((nc.psum_tensor if psum else nc.sbuf_tensor)(name, shape, dt))
        self.b = P.buf(name)

    def __getitem__(self, k):
        return self.t[k]


def alt_evac(i):
    return "act" if i % 2 == 0 else "dve"


def build_program(stop=99, skip=()):
    nc = bass.Bass("TRN2", target_bir_lowering=False)

    def din(name, shape, dt=F32):
        return nc.dram_tensor(name, list(shape), dt, kind="ExternalInput").ap()

    def dout(name, shape, dt=F32):
        return nc.dram_tensor(name, list(shape), dt, kind="ExternalOutput").ap()

    def dscr(name, shape, dt):
        return nc.dram_tensor(name, list(shape), dt, kind="Internal").ap()

    xf = din("xf", [TOKF, D])
    xkv = din("xkv", [NKV * 128, D])
    w_in = din("w_in", [D, ATTN_IN])
    w_out = din("w_out", [D, D])
    pw_in = din("pw_in", [D, 2 * D])
    gw = din("gw", [4, 1024, 1024])
    pw_out = din("pw_out", [D, D])
    norm_w = din("norm_w", [2, 128, KC])
    fnorm_w = din("fnorm_w", [D])
    pscale = din("pscale", [128, KC])
    rel_bias = din("rel_bias", [32, 32])
    ck = din("ck", [4, 4096, 1024])
    cv = din("cv", [4, 4096, 1024])
    cki = din("cki", [4, 4096, 64])
    spool = din("spool", [4, 15, D])
    c_ident = din("c_ident", [128, 128], BF16)
    c_identf = din("c_identf", [128, 128], F32)
    c_i4 = din("c_i4", [128, 512], BF16)
    c_i4s = din("c_i4s", [32, 128], BF16)
    c_oh = din("c_oh", [32, 384], F32)
    c_adm = din("c_adm", [128, 128], F32)
    c_kbias = din("c_kbias", [NKEY], BF16)
    c_invcnt = din("c_invcnt", [4, 128], F32)

    y_o = dout("y_o", [TOKF, D])
    k_o = dout("k_o", [TOKF, 1024])
    v_o = dout("v_o", [TOKF, 1024])
    ki_o = dout("ki_o", [TOKF, 64])
    u_o = dout("u_o", [256, D])

    QT = dscr("QT", [D, TOKF], BF16)
    KT = dscr("KT", [5, 1024, NKEY], BF16)
    VB = dscr("VB", [5, 8, 128, 33, 128], BF16)
    QIT = dscr("QIT", [2048, TOKF], BF16)
    KIT = dscr("KIT", [5, 64, NKEY], BF16)
    WI = dscr("WI", [TOKF, 32], F32)
    SGT = dscr("SGT", [D, TOKF], BF16)
    OGT = dscr("OGT", [D, TOKF], BF16)
    X1 = dscr("X1", [TOKF, D], F32)
    M1T = dscr("M1T", [D, TOKF], BF16)
    SG1T = dscr("SG1T", [D, TOKF], BF16)
    X2 = dscr("X2", [TOKF, D], F32)

    with ExitStack() as gst:
        P = Prog(nc, gst)

        def load_w(Wt, w_ap, c0, ncols, k0=0, nk=KC):
            src = w_ap[k0 * 128:(k0 + nk) * 128, c0:c0 + ncols].rearrange("(kc p) n -> p kc n", p=128)
            P.op("pool", lambda e: [e.dma_start(out=Wt[:, 0:nk, 0:ncols], in_=src)], writes=[Wt.b], dma=Wt.b)

        def norm_tiles(src_ap, row0, ntile, xb, xn, ss, rstd):
            for j in range(ntile):
                r0 = row0 + j * 128
                P.op("sp", lambda e, r0=r0: [e.dma_start(out=xb[:, :], in_=src_ap[r0:r0 + 128, :])],
                     writes=[xb.b], dma=xb.b)
                P.op("act", lambda e, j=j: e.activation(out=xn[:, j, :], in_=xb[:, :], func=AF.Square,
                                                         accum_out=ss[:, j:j + 1]),
                     reads=[xb.b], writes=[xn.b, ss.b])
                P.op("act", lambda e, j=j: e.activation(out=rstd[:, j:j + 1], in_=ss[:, j:j + 1], func=AF.Sqrt,
                                                         scale=1.0 / D, bias=epst[:, 0:1]),
                     reads=[ss.b, epst.b], writes=[rstd.b])
                P.op("dve", lambda e, j=j: e.reciprocal(out=rstd[:, j:j + 1], in_=rstd[:, j:j + 1]),
                     reads=[rstd.b], writes=[rstd.b])
                P.op("dve", lambda e, j=j: e.tensor_scalar(out=xn[:, j, :], in0=xb[:, :], scalar1=rstd[:, j:j + 1],
                                                            scalar2=None, op0=ALU.mult),
                     reads=[xb.b, rstd.b], writes=[xn.b])

        def transpose_to_hT(xn, ntile, hT, nwcol, pbanks, bank0):
            ntok = ntile * 128
            for kp in range(KC // 2):
                pb = pbanks[(bank0 + kp) % len(pbanks)]
                pbv = pb.t[:, :].bitcast(BF16)

                def em(e, kp=kp, pbv=pbv):
                    r = None
                    for i in range(2):
                        kc = kp * 2 + i
                        for j in range(ntile):
                            r = e.transpose(out=pbv[:, i * 512 + j * 128:i * 512 + (j + 1) * 128],
                                            in_=xn[:, j, kc * 128:(kc + 1) * 128], identity=ident[:, :])
                    return r
                P.op("pe", em, reads=[xn.b, ident.b], writes=[pb.b])
                for i in range(2):
                    kc = kp * 2 + i
                    eng = alt_evac(kp)
                    if eng == "act":
                        P.op("act", lambda e, kc=kc, i=i, pbv=pbv: e.activation(
                            out=hT[:, kc, 0:ntok], in_=pbv[:, i * 512:i * 512 + ntok], func=AF.Copy,
                            scale=nwcol[:, kc:kc + 1]), reads=[pb.b, nwcol.b], writes=[hT.b])
                    else:
                        P.op("dve", lambda e, kc=kc, i=i, pbv=pbv: e.tensor_scalar(
                            out=hT[:, kc, 0:ntok], in0=pbv[:, i * 512:i * 512 + ntok], scalar1=nwcol[:, kc:kc + 1],
                            scalar2=None, op0=ALU.mult), reads=[pb.b, nwcol.b], writes=[hT.b])

        def fm_mm(Wt, ncols, act, ntok, pbs, nk=KC, kofs=0):
            nsub = (ncols + 127) // 128
            for s in range(nsub):
                cw = min(128, ncols - s * 128)

                def em(e, s=s, cw=cw):
                    r = None
                    for kc in range(nk):
                        r = e.matmul(out=pbs[s].t[0:cw, 0:ntok], lhsT=Wt[:, kc, s * 128:s * 128 + cw],
                                     rhs=act[:, kofs + kc, 0:ntok], start=(kc == 0), stop=(kc == nk - 1))
                    return r
                P.op("pe", em, reads=[Wt.b, act.b], writes=[pbs[s].b])

        def tm_mm(Wt, ncols, act, ntile, pbs):
            for j in range(ntile):
                def em(e, j=j):
                    r = None
                    for kc in range(KC):
                        r = e.matmul(out=pbs[j].t[:, 0:ncols], lhsT=act[:, kc, j * 128:(j + 1) * 128],
                                     rhs=Wt[:, kc, 0:ncols], start=(kc == 0), stop=(kc == KC - 1))
                    return r
                P.op("pe", em, reads=[Wt.b, act.b], writes=[pbs[j].b])

        full_groups = [(0, 4), (4, 4), (8, 4), (12, 4), (16, 2)]
        kv_groups = [(0, 4), (4, 4), (8, 4), (12, 3)]

        def key_dst(ft):
            if ft < 17:
                return [(0, (15 + ft) * 128, 0, 128)]
            return [(1 + sb, 4096, sb * 32, 32) for sb in range(4)]

        with ExitStack() as st:
            xb = T(P, st, "xb", [128, D], F32)
            xn = T(P, st, "xn", [128, 4, D], BF16)
            hT = T(P, st, "hT", [128, KC, 512], BF16)
            Wts = [T(P, st, "wt%d" % i, [128, KC, 512], BF16) for i in range(2)]
            sfm = [T(P, st, "sfm%d" % i, [128, 4, 512], BF16) for i in range(2)]
            stm = [T(P, st, "stm%d" % i, [128, 4, 512], F32) for i in range(2)]
            svb = [T(P, st, "svb%d" % i, [128, 4, 512], BF16) for i in range(2)]
            ss = T(P, st, "ss", [128, 4], F32)
            rstd = T(P, st, "rstd", [128, 4], F32)
            nwcol = T(P, st, "nwcol", [128, KC], F32)
            epst = T(P, st, "epst", [128, 1], F32)
            ident = T(P, st, "ident", [128, 128], BF16)
            pbanks = [T(P, st, "pb%d" % i, [128, 512], F32, psum=True) for i in range(8)]
            cnt = {"w": 0, "fm": 0, "tm": 0, "vb": 0, "blk": 0}

            P.op("sp", lambda e: [e.dma_start(out=ident[:, :], in_=c_ident[:, :])], writes=[ident.b], dma=ident.b)
            P.op("sp", lambda e: [e.dma_start(out=nwcol[:, :], in_=norm_w[0, :, :])],
                 writes=[nwcol.b], dma=nwcol.b)
            P.op("dve", lambda e: e.memset(epst[:, :], EPS), writes=[epst.b])

            def nextW():
                w = Wts[cnt["w"] % 2]
                cnt["w"] += 1
                return w

            def banks4():
                b = cnt["blk"] % 2
                cnt["blk"] += 1
                return pbanks[b * 4:(b + 1) * 4]

            def evac_fm_to(pbs, nsub, ntok, scr_rows_ap_fn, func=None, scale=None):
                sg = sfm[cnt["fm"] % 2]
                cnt["fm"] += 1
                for s in range(nsub):
                    eng = "act" if func is not None else alt_evac(s)
                    if eng == "act":
                        P.op("act", lambda e, s=s: e.activation(out=sg[:, s, 0:ntok], in_=pbs[s].t[:, 0:ntok],
                                                                 func=(func or AF.Copy),
                                                                 scale=(scale if scale is not None else 1.0)),
                             reads=[pbs[s].b], writes=[sg.b])
                    else:
                        P.op("dve", lambda e, s=s: e.tensor_scalar(out=sg[:, s, 0:ntok], in0=pbs[s].t[:, 0:ntok],
                                                                    scalar1=(scale if scale is not None else 1.0),
                                                                    scalar2=None, op0=ALU.mult),
                             reads=[pbs[s].b], writes=[sg.b])
                return sg

            def proj_group(src_ap, t0, ntile, kind, do_norm=True, nxt=None):
                ntok = ntile * 128
                if do_norm:
                    norm_tiles(src_ap, t0 * 128, ntile, xb, xn, ss, rstd)
                if 'normonly' in skip:
                    return
                transpose_to_hT(xn, ntile, hT, nwcol, pbanks, 0)
                if 'tronly' in skip:
                    return
                if nxt is not None:
                    norm_tiles(nxt[0], nxt[1] * 128, nxt[2], xb, xn, ss, rstd)
                full = kind == "full"
                if full:
                    pieces = []
                    for j in range(ntile):
                        for (sq, k0, tk0, n) in key_dst(t0 + j):
                            pieces.append((sq, k0, j * 128 + tk0, n))
                    qc0 = t0 * 128
                else:
                    pieces = [(0, (t0 + j) * 128, j * 128, 128) for j in range(ntile)]
                merged = []
                for pc in pieces:
                    if merged and merged[-1][0] == pc[0] and merged[-1][1] + merged[-1][3] == pc[1] \
                            and merged[-1][2] + merged[-1][3] == pc[2]:
                        m = merged[-1]
                        merged[-1] = (m[0], m[1], m[2], m[3] + pc[3])
                    else:
                        merged.append(pc)

                if full:
                    for blk in (range(8) if 'noq' not in skip else []):
                        Wt = nextW()
                        load_w(Wt, w_in, C_Q + blk * 512, 512)
                        pbs = banks4()
                        fm_mm(Wt, 512, hT, ntok, pbs)
                        sg = evac_fm_to(pbs, 4, ntok, None, scale=128.0 ** -0.5)
                        dst = QT[blk * 512:(blk + 1) * 512, qc0:qc0 + ntok].rearrange("(s p) t -> p s t", p=128)
                        P.op("sp", lambda e, sg=sg, dst=dst: [e.dma_start(out=dst, in_=sg[:, :, 0:ntok])],
                             reads=[sg.b], dma=sg.b)
                for blk in (range(2) if 'nok' not in skip else []):
                    Wt = nextW()
                    load_w(Wt, w_in, C_K + blk * 512, 512)
                    pbs = banks4()
                    fm_mm(Wt, 512, hT, ntok, pbs)
                    sg = evac_fm_to(pbs, 4, ntok, None)
                    dl = []
                    for (sq, k0, tk0, n) in merged:
                        dl.append((KT[sq, blk * 512:(blk + 1) * 512, k0:k0 + n].rearrange("(s p) t -> p s t", p=128),
                                   tk0, n))
                    P.op("sp", lambda e, sg=sg, dl=dl: [e.dma_start(out=d, in_=sg[:, :, a:a + n]) for (d, a, n) in dl],
                         reads=[sg.b], dma=sg.b, ndma=len(dl))
                    if full:
                        pbs = banks4()
                        tm_mm(Wt, 512, hT, ntile, pbs)
                        sg = stm[cnt["tm"] % 2]
                        cnt["tm"] += 1
                        for j in range(ntile):
                            eng = alt_evac(j)
                            if eng == "act":
                                P.op("act", lambda e, j=j, sg=sg, pbs=pbs: e.activation(out=sg[:, j, :], in_=pbs[j].t[:, :], func=AF.Copy),
                                     reads=[pbs[j].b], writes=[sg.b])
                            else:
                                P.op("dve", lambda e, j=j, sg=sg, pbs=pbs: e.tensor_copy(out=sg[:, j, :], in_=pbs[j].t[:, :]),
                                     reads=[pbs[j].b], writes=[sg.b])
                        dst = k_o[t0 * 128:t0 * 128 + ntok, blk * 512:(blk + 1) * 512].rearrange("(j p) c -> p j c", p=128)
                        P.op("sp", lambda e, sg=sg, dst=dst: [e.dma_start(out=dst, in_=sg[:, 0:ntile, :])],
                             reads=[sg.b], dma=sg.b)
                for blk in (range(2) if 'nov' not in skip else []):
                    Wt = nextW()
                    load_w(Wt, w_in, C_V + blk * 512, 512)
                    pbs = banks4()
                    tm_mm(Wt, 512, hT, ntile, pbs)
                    sv = svb[cnt["vb"] % 2]
                    cnt["vb"] += 1
                    if full:
                        sg = stm[cnt["tm"] % 2]
                        cnt["tm"] += 1
                    for j in range(ntile):
                        if full:
                            P.op("dve", lambda e, j=j, sg=sg, pbs=pbs: e.tensor_copy(out=sg[:, j, :], in_=pbs[j].t[:, :]),
                                 reads=[pbs[j].b], writes=[sg.b])
                            P.op("act", lambda e, j=j, sv=sv, sg=sg: e.activation(out=sv[:, j, :], in_=sg[:, j, :], func=AF.Copy),
                                 reads=[sg.b], writes=[sv.b])
                        else:
                            P.op("act", lambda e, j=j, sv=sv, pbs=pbs: e.activation(out=sv[:, j, :], in_=pbs[j].t[:, :], func=AF.Copy),
                                 reads=[pbs[j].b], writes=[sv.b])
                    if full:
                        dst = v_o[t0 * 128:t0 * 128 + ntok, blk * 512:(blk + 1) * 512].rearrange("(j p) c -> p j c", p=128)
                        P.op("sp", lambda e, sg=sg, dst=dst: [e.dma_start(out=dst, in_=sg[:, 0:ntile, :])],
                             reads=[sg.b], dma=sg.b)
                    dl = []
                    for j in range(ntile):
                        if full:
                            pcs = key_dst(t0 + j)
                        else:
                            pcs = [(0, (t0 + j) * 128, 0, 128)]
                        for (sq, k0, tk0, n) in pcs:
                            c = k0 // 128
                            p0 = k0 % 128
                            d = VB[sq, blk * 4:(blk + 1) * 4, p0:p0 + n, c, :].rearrange("g p d -> p g d")
                            dl.append((d, j, tk0, n))
                    P.op("sp", lambda e, sv=sv, dl=dl: [
                        e.dma_start(out=d, in_=sv[tk0:tk0 + n, j, :].rearrange("p (g d) -> p g d", g=4))
                        for (d, j, tk0, n) in dl], reads=[sv.b], dma=sv.b, ndma=len(dl))
                if full:
                    for blk in (range(4) if 'noqi' not in skip else []):
                        Wt = nextW()
                        load_w(Wt, w_in, C_QI + blk * 512, 512)
                        pbs = banks4()
                        fm_mm(Wt, 512, hT, ntok, pbs)
                        sg = evac_fm_to(pbs, 4, ntok, None)
                        dst = QIT[blk * 512:(blk + 1) * 512, qc0:qc0 + ntok].rearrange("(s p) t -> p s t", p=128)
                        P.op("sp", lambda e, sg=sg, dst=dst: [e.dma_start(out=dst, in_=sg[:, :, 0:ntok])],
                             reads=[sg.b], dma=sg.b)
                if 'noki' in skip:
                    return
                Wt = nextW()
                load_w(Wt, w_in, C_KI, 128)
                pbs = banks4()
                fm_mm(Wt, 128, hT, ntok, pbs[0:1])
                sg = sfm[cnt["fm"] % 2]
                cnt["fm"] += 1
                P.op("act", lambda e, sg=sg, pbs=pbs: e.activation(out=sg[0:64, 0, 0:ntok], in_=pbs[0].t[0:64, 0:ntok], func=AF.Copy),
                     reads=[pbs[0].b], writes=[sg.b])
                dl = [(KIT[sq, :, k0:k0 + n], tk0, n) for (sq, k0, tk0, n) in merged]
                P.op("sp", lambda e, sg=sg, dl=dl: [e.dma_start(out=d, in_=sg[0:64, 0, a:a + n]) for (d, a, n) in dl],
                     reads=[sg.b], dma=sg.b, ndma=len(dl))
                if full:
                    pbs = banks4()
                    tm_mm(Wt, 128, hT, ntile, pbs)
                    sg = stm[cnt["tm"] % 2]
                    cnt["tm"] += 1
                    for j in range(ntile):
                        P.op("dve", lambda e, j=j, sg=sg, pbs=pbs: e.tensor_copy(out=sg[:, j, 0:64], in_=pbs[j].t[:, 0:64]),
                             reads=[pbs[j].b], writes=[sg.b])
                        P.op("dve", lambda e, j=j, sg=sg, pbs=pbs: e.tensor_scalar(out=sg[:, j, 64:96], in0=pbs[j].t[:, 64:96],
                                                                                    scalar1=1.0 / (8.0 * math.sqrt(32.0)), scalar2=None,
                                                                                    op0=ALU.mult),
                             reads=[pbs[j].b], writes=[sg.b])
                    d1 = ki_o[t0 * 128:t0 * 128 + ntok, :].rearrange("(j p) c -> p j c", p=128)
                    d2 = WI[t0 * 128:t0 * 128 + ntok, :].rearrange("(j p) c -> p j c", p=128)
                    P.op("sp", lambda e, sg=sg, d1=d1, d2=d2: [e.dma_start(out=d1, in_=sg[:, 0:ntile, 0:64]),
                                                              e.dma_start(out=d2, in_=sg[:, 0:ntile, 64:96])],
                         reads=[sg.b], dma=sg.b, ndma=2)
                    for blk in (range(8) if 'nogate' not in skip else []):
                        Wt = nextW()
                        load_w(Wt, w_in, C_G + blk * 512, 512)
                        pbs = banks4()
                        fm_mm(Wt, 512, hT, ntok, pbs)
                        sg = evac_fm_to(pbs, 4, ntok, None, func=AF.Silu)
                        dst = SGT[blk * 512:(blk + 1) * 512, qc0:qc0 + ntok].rearrange("(s p) t -> p s t", p=128)
                        P.op("sp", lambda e, sg=sg, dst=dst: [e.dma_start(out=dst, in_=sg[:, :, 0:ntok])],
                             reads=[sg.b], dma=sg.b)

            allg = [(xkv, t0, n, "kv") for (t0, n) in kv_groups] + [(xf, t0, n, "full") for (t0, n) in full_groups]
            for gi_, (src_, t0, n, kind_) in enumerate(allg):
                nxt_ = allg[gi_ + 1][0:3] if gi_ + 1 < len(allg) else None
                proj_group(src_, t0, n, kind_, do_norm=(gi_ == 0), nxt=nxt_)

            vdma = P.buf("vdma")
            ckt = Wts[0]
            ckit = Wts[1]
            for sb in (range(4) if 'S' not in skip else []):
                for g in range(8):
                    src = cv[sb, :, g * 128:(g + 1) * 128].rearrange("(c p) d -> p c d", p=128)
                    dst = VB[1 + sb, g, :, 0:32, :]
                    P.op("pool", lambda e, src=src, dst=dst: [e.dma_start(out=dst, in_=src)], dma=vdma)
                srck = cki[sb, :, :].rearrange("(c p) d -> p c d", p=128)
                P.op("pool", lambda e, srck=srck: [e.dma_start(out=ckit[:, 0:32, 0:64], in_=srck)], writes=[ckit.b], dma=ckit.b)
                for c8 in range(4):
                    pb = pbanks[c8 % 8]
                    pbv = pb.t[:, :].bitcast(BF16)

                    def em(e, c8=c8, pbv=pbv):
                        r = None
                        for i in range(8):
                            r = e.transpose(out=pbv[0:64, i * 128:(i + 1) * 128], in_=ckit[:, c8 * 8 + i, 0:64],
                                            identity=ident[:, :])
                        return r
                    P.op("pe", em, reads=[ckit.b, ident.b], writes=[pb.b])
                    sg = sfm[cnt["fm"] % 2]
                    cnt["fm"] += 1
                    P.op("act", lambda e, sg=sg, pbv=pbv: e.activation(out=sg[0:64, :, :].rearrange("p a b -> p (a b)")[:, 0:1024],
                                                                        in_=pbv[0:64, :], func=AF.Copy),
                         reads=[pb.b], writes=[sg.b])
                    dst = KIT[1 + sb, :, c8 * 1024:(c8 + 1) * 1024]
                    P.op("sp", lambda e, sg=sg, dst=dst: [e.dma_start(out=dst, in_=sg[0:64, :, :].rearrange("p a b -> p (a b)")[:, 0:1024])],
                         reads=[sg.b], dma=sg.b)
                for c4 in range(8):
                    srcc = ck[sb, c4 * 512:(c4 + 1) * 512, :].rearrange("(c p) n -> p c n", p=128)
                    ckv = ckt[:, :, :].rearrange("p a b -> p (a b)")
                    P.op("pool", lambda e, srcc=srcc, ckv=ckv: [e.dma_start(
                        out=ckv[:, 0:4096].rearrange("p (c n) -> p c n", c=4), in_=srcc)], writes=[ckt.b], dma=ckt.b)
                    for half in range(2):
                        sg = sfm[cnt["fm"] % 2]
                        cnt["fm"] += 1
                        for gg in range(4):
                            g = half * 4 + gg
                            pb = pbanks[(c4 * 8 + g) % 8]
                            pbv = pb.t[:, :].bitcast(BF16)

                            def em(e, g=g, pbv=pbv, ckv=ckv):
                                r = None
                                for c in range(4):
                                    r = e.transpose(out=pbv[:, c * 128:(c + 1) * 128],
                                                    in_=ckv[:, c * 1024 + g * 128:c * 1024 + (g + 1) * 128],
                                                    identity=ident[:, :])
                                return r
                            P.op("pe", em, reads=[ckt.b, ident.b], writes=[pb.b])
                            eng = alt_evac(gg)
                            if eng == "act":
                                P.op("act", lambda e, sg=sg, gg=gg, pbv=pbv: e.activation(out=sg[:, gg, :], in_=pbv[:, 0:512], func=AF.Copy),
                                     reads=[pb.b], writes=[sg.b])
                            else:
                                P.op("dve", lambda e, sg=sg, gg=gg, pbv=pbv: e.tensor_copy(out=sg[:, gg, :], in_=pbv[:, 0:512]),
                                     reads=[pb.b], writes=[sg.b])
                        dst = KT[1 + sb, half * 512:(half + 1) * 512, c4 * 512:(c4 + 1) * 512].rearrange("(g p) k -> p g k", p=128)
                        P.op("sp", lambda e, sg=sg, dst=dst: [e.dma_start(out=dst, in_=sg[:, :, :])], reads=[sg.b], dma=sg.b)
            P.drain("sp")
            P.emit_block()
            P.release([t.b for t in [xb, xn, hT] + Wts + sfm + stm + svb + [ident, nwcol]] + [vdma])
            if stop == 1:
                return nc

        with ExitStack() as st:
            SC = T(P, st, "SC", [128, NKEY], F32)
            MBs = [T(P, st, "MB%d" % i, [128, NKEY], BF16) for i in range(2)]
            KBI = T(P, st, "KBI", [128, NKEY], BF16)
            junk = T(P, st, "junk", [128, NKEY], BF16)
            QIs = [T(P, st, "QI%d" % i, [64, 32, 128], BF16) for i in range(2)]
            KIs = [T(P, st, "KI%d" % i, [64, NKEY], BF16) for i in range(2)]
            WIs = [T(P, st, "WIs%d" % i, [128, 32], F32) for i in range(2)]
            QTs = [T(P, st, "QTt%d" % i, [128, 32, 128], BF16) for i in range(2)]
            KTg = [T(P, st, "KTg%d" % i, [128, NKEY], BF16) for i in range(2)]
            Vg = [T(P, st, "Vg%d" % i, [128, 33, 129], BF16) for i in range(2)]
            rls = [T(P, st, "rl%d" % i, [128, 512], BF16) for i in range(4)]
            pTs = [T(P, st, "pT%d" % i, [128, 512], BF16) for i in range(3)]
            Ot = T(P, st, "Ot", [128, D], BF16)
            SGt = T(P, st, "SGt", [128, 32, 128], BF16)
            OGs = T(P, st, "OGs", [128, 32, 128], BF16)
            BT = T(P, st, "BT", [128, 2, 32, 128], BF16)
            ident = T(P, st, "ident2", [128, 128], BF16)
            i4 = T(P, st, "i4", [128, 512], BF16)
            i4s = T(P, st, "i4s", [32, 128], BF16)
            oh = T(P, st, "oh", [32, 384], F32)
            rb = T(P, st, "rb", [32, 32], F32)
            rb15 = T(P, st, "rb15", [32, 32], F32)
            adm = T(P, st, "adm", [128, 128], F32)
            sm = T(P, st, "sm", [128, 16], F32)
            den = T(P, st, "den", [128, 8], F32)
            pix = [T(P, st, "pix%d" % i, [128, 512], F32, psum=True) for i in range(2)]
            plg = [T(P, st, "plg%d" % i, [128, 512], F32, psum=True) for i in range(2)]
            poa = [T(P, st, "poa%d" % i, [128, 512], F32, psum=True) for i in range(2)]
            ptr = [T(P, st, "ptr%d" % i, [128, 512], F32, psum=True) for i in range(2)]

            for (t_, src) in ((ident, c_ident), (i4, c_i4), (i4s, c_i4s), (oh, c_oh), (rb, rel_bias), (adm, c_adm)):
                P.op("sp", lambda e, t_=t_, src=src: [e.dma_start(out=t_[:, :], in_=src[:, :])], writes=[t_.b], dma=t_.b)
            P.op("sp", lambda e: [e.dma_start(out=rb15[:, :], in_=rel_bias[15, :].partition_broadcast(32))],
                 writes=[rb15.b], dma=rb15.b)
            P.op("sp", lambda e: [e.dma_start(out=KBI[:, :], in_=c_kbias.partition_broadcast(128))],
                 writes=[KBI.b], dma=KBI.b)
            P.op("dve", lambda e: e.tensor_tensor(out=rb[:, :], in0=rb[:, :], in1=rb15[:, :], op=ALU.subtract),
                 reads=[rb15.b, rb.b], writes=[rb.b])
            P.op("dve", lambda e: e.memset(sm[:, 7:8], 0.5), writes=[sm.b])
            for v_ in Vg:
                P.op("dve", lambda e, v_=v_: e.memset(v_[:, :, 128:129], 1.0), writes=[v_.b])
            for dl in range(2):
                for q16 in range(8):
                    pb = ptr[(dl * 8 + q16) % 2]

                    def em(e, dl=dl, q16=q16, pb=pb):
                        r = None
                        for i in range(16):
                            ql = q16 * 16 + i
                            start = (-128 if dl == 1 else 0) - ql + 255
                            r = e.matmul(out=pb.t[:, i * 32:(i + 1) * 32], lhsT=oh[:, start:start + 128], rhs=rb[:, :],
                                         start=True, stop=True)
                        return r
                    P.op("pe", em, reads=[oh.b, rb.b], writes=[pb.b])
                    P.op("act", lambda e, dl=dl, q16=q16, pb=pb: e.activation(
                        out=BT[:, dl, :, q16 * 16:(q16 + 1) * 16],
                        in_=pb.t[:, :].rearrange("p (q h) -> p h q", h=32), func=AF.Copy),
                        reads=[pb.b], writes=[BT.b])

            tiles = []
            for ft in range(17):
                tiles.append(dict(seq=0, nq=128, qc=ft * 128, nch=16 + ft, last=128, prompt=True))
            for sb in range(4):
                tiles.append(dict(seq=1 + sb, nq=32, qc=17 * 128 + sb * 32, nch=33, last=32, prompt=False))
            cnt2 = {"rl": 0, "pix": 0, "pT": 0, "plg": 0, "kv": 0}
            def indexer(ti):
                tl = tiles[ti]
                nq, qc, nch, seq = tl["nq"], tl["qc"], tl["nch"], tl["seq"]
                nkeys = (nch - 1) * 128 + tl["last"]
                QI, KI, WIt, QTt, MB = QIs[ti % 2], KIs[ti % 2], WIs[ti % 2], QTs[ti % 2], MBs[ti % 2]
                P.op("sp", lambda e, QI=QI, qc=qc, nq=nq: [e.dma_start(
                    out=QI[:, :, 0:nq], in_=QIT[:, qc:qc + nq].rearrange("(h d) q -> d h q", d=64))],
                    writes=[QI.b], dma=QI.b)
                P.op("sp", lambda e, KI=KI, seq=seq, nkeys=nkeys: [e.dma_start(out=KI[:, 0:nkeys], in_=KIT[seq, :, 0:nkeys])],
                     writes=[KI.b], dma=KI.b)
                P.op("sp", lambda e, WIt=WIt, qc=qc, nq=nq: [e.dma_start(out=WIt[0:nq, :], in_=WI[qc:qc + nq, :])],
                     writes=[WIt.b], dma=WIt.b)
                P.op("sp", lambda e, QTt=QTt, qc=qc, nq=nq: [e.dma_start(
                    out=QTt[:, :, 0:nq], in_=QT[:, qc:qc + nq].rearrange("(h d) q -> d h q", d=128))],
                    writes=[QTt.b], dma=QTt.b)
                nblk = (nkeys + 511) // 512
                for kb in range(nblk):
                    k0 = kb * 512
                    bs = min(512, nkeys - k0)
                    for h in range(32):
                        pb = pix[cnt2["pix"] % 2]
                        cnt2["pix"] += 1
                        rl = rls[cnt2["rl"] % 4]
                        cnt2["rl"] += 1
                        P.op("pe", lambda e, pb=pb, QI=QI, KI=KI, h=h, k0=k0, bs=bs, nq=nq: e.matmul(
                            out=pb.t[0:nq, 0:bs], lhsT=QI[:, h, 0:nq], rhs=KI[:, k0:k0 + bs], start=True, stop=True),
                            reads=[QI.b, KI.b], writes=[pb.b])
                        P.op("act", lambda e, pb=pb, rl=rl, bs=bs, nq=nq: e.activation(
                            out=rl[0:nq, 0:bs], in_=pb.t[0:nq, 0:bs], func=AF.Relu), reads=[pb.b], writes=[rl.b])
                        if h == 0:
                            P.op("dve", lambda e, rl=rl, WIt=WIt, k0=k0, bs=bs, nq=nq: e.tensor_scalar(
                                out=SC[0:nq, k0:k0 + bs], in0=rl[0:nq, 0:bs], scalar1=WIt[0:nq, 0:1], scalar2=None,
                                op0=ALU.mult), reads=[rl.b, WIt.b], writes=[SC.b])
                        else:
                            P.op("dve", lambda e, rl=rl, WIt=WIt, h=h, k0=k0, bs=bs, nq=nq: e.scalar_tensor_tensor(
                                out=SC[0:nq, k0:k0 + bs], in0=rl[0:nq, 0:bs], scalar=WIt[0:nq, h:h + 1],
                                in1=SC[0:nq, k0:k0 + bs], op0=ALU.mult, op1=ALU.add),
                                reads=[rl.b, WIt.b, SC.b], writes=[SC.b])
                P.op("dve", lambda e, nq=nq, nkeys=nkeys: e.tensor_reduce(out=sm[0:nq, 1:2], in_=SC[0:nq, 0:nkeys],
                                                                          axis=AX.X, op=ALU.max),
                     reads=[SC.b], writes=[sm.b])
                P.op("dve", lambda e, nq=nq, nkeys=nkeys: e.tensor_reduce(out=sm[0:nq, 0:1], in_=SC[0:nq, 0:nkeys],
                                                                          axis=AX.X, op=ALU.min),
                     reads=[SC.b], writes=[sm.b])
                P.op("dve", lambda e, nq=nq: e.tensor_scalar(out=sm[0:nq, 1:2], in0=sm[0:nq, 1:2], scalar1=1.0, scalar2=None,
                                                              op0=ALU.add), reads=[sm.b], writes=[sm.b])
                P.op("dve", lambda e, nq=nq: e.tensor_scalar(out=sm[0:nq, 0:1], in0=sm[0:nq, 0:1], scalar1=-1.0, scalar2=None,
                                                              op0=ALU.add), reads=[sm.b], writes=[sm.b])
                if tl["prompt"]:
                    P.op("dve", lambda e, nkeys=nkeys: e.tensor_tensor(out=SC[:, 0:nkeys], in0=SC[:, 0:nkeys],
                                                                        in1=KBI[:, 0:nkeys], op=ALU.add),
                         reads=[SC.b, KBI.b], writes=[SC.b])
                    P.op("dve", lambda e, nkeys=nkeys: e.tensor_tensor(out=SC[:, nkeys - 128:nkeys], in0=SC[:, nkeys - 128:nkeys],
                                                                        in1=adm[:, :], op=ALU.add),
                         reads=[SC.b, adm.b], writes=[SC.b])
                for it in range(NIT):
                    P.op("dve", lambda e, nq=nq: e.scalar_tensor_tensor(out=sm[0:nq, 2:3], in0=sm[0:nq, 0:1], scalar=sm[0:nq, 1:2],
                                                                         in1=sm[0:nq, 7:8], op0=ALU.add, op1=ALU.mult),
                         reads=[sm.b], writes=[sm.b])
                    P.op("dve", lambda e, nq=nq, nkeys=nkeys: e.tensor_scalar(out=junk[0:nq, 0:nkeys], in0=SC[0:nq, 0:nkeys],
                                                                               scalar1=sm[0:nq, 2:3], scalar2=None, op0=ALU.is_ge,
                                                                               op1=ALU.add, accum_out=sm[0:nq, 3:4]),
                         reads=[sm.b, SC.b], writes=[sm.b, junk.b])
                    P.op("dve", lambda e, nq=nq: e.tensor_single_scalar(out=sm[0:nq, 4:5], in_=sm[0:nq, 3:4], scalar=255.5, op=ALU.is_ge),
                         reads=[sm.b], writes=[sm.b])
                    P.op("dve", lambda e, nq=nq: e.tensor_tensor(out=sm[0:nq, 5:6], in0=sm[0:nq, 2:3], in1=sm[0:nq, 0:1], op=ALU.subtract),
                         reads=[sm.b], writes=[sm.b])
                    P.op("dve", lambda e, nq=nq: e.tensor_tensor(out=sm[0:nq, 6:7], in0=sm[0:nq, 1:2], in1=sm[0:nq, 2:3], op=ALU.subtract),
                         reads=[sm.b], writes=[sm.b])
                    P.op("dve", lambda e, nq=nq: e.scalar_tensor_tensor(out=sm[0:nq, 0:1], in0=sm[0:nq, 5:6], scalar=sm[0:nq, 4:5],
                                                                         in1=sm[0:nq, 0:1], op0=ALU.mult, op1=ALU.add),
                         reads=[sm.b], writes=[sm.b])
                    P.op("dve", lambda e, nq=nq: e.scalar_tensor_tensor(out=sm[0:nq, 1:2], in0=sm[0:nq, 6:7], scalar=sm[0:nq, 4:5],
                                                                         in1=sm[0:nq, 2:3], op0=ALU.mult, op1=ALU.add),
                         reads=[sm.b], writes=[sm.b])
                P.op("dve", lambda e, nq=nq, nkeys=nkeys, MB=MB: e.tensor_scalar(out=MB[0:nq, 0:nkeys], in0=SC[0:nq, 0:nkeys],
                                                                                  scalar1=sm[0:nq, 0:1], scalar2=NEG, op0=ALU.is_lt,
                                                                                  op1=ALU.mult),
                     reads=[sm.b, SC.b], writes=[MB.b])
            def attend(ti):
                tl = tiles[ti]
                nq, qc, nch, seq = tl["nq"], tl["qc"], tl["nch"], tl["seq"]
                nkeys = (nch - 1) * 128 + tl["last"]
                QI, KI, WIt, QTt, MB = QIs[ti % 2], KIs[ti % 2], WIs[ti % 2], QTs[ti % 2], MBs[ti % 2]
                P.op("sp", lambda e, qc=qc, nq=nq: [e.dma_start(out=SGt[:, :, 0:nq],
                                                               in_=SGT[:, qc:qc + nq].rearrange("(kc p) q -> p kc q", p=128))],
                     writes=[SGt.b], dma=SGt.b)
                for g in range(8):
                    Kt, Vt = KTg[cnt2["kv"] % 2], Vg[cnt2["kv"] % 2]
                    cnt2["kv"] += 1
                    P.op("sp", lambda e, Kt=Kt, seq=seq, g=g, nkeys=nkeys: [e.dma_start(
                        out=Kt[:, 0:nkeys], in_=KT[seq, g * 128:(g + 1) * 128, 0:nkeys])], writes=[Kt.b], dma=Kt.b)
                    if tl["last"] == 128:
                        P.op("sp", lambda e, Vt=Vt, seq=seq, g=g, nch=nch: [e.dma_start(
                            out=Vt[:, 0:nch, 0:128], in_=VB[seq, g, :, 0:nch, :])], writes=[Vt.b], dma=Vt.b)
                    else:
                        P.op("sp", lambda e, Vt=Vt, seq=seq, g=g, nch=nch: [
                            e.dma_start(out=Vt[:, 0:nch - 1, 0:128], in_=VB[seq, g, :, 0:nch - 1, :]),
                            e.dma_start(out=Vt[0:32, nch - 1, 0:128], in_=VB[seq, g, 0:32, nch - 1, :])],
                            writes=[Vt.b], dma=Vt.b, ndma=2)
                    def issue_qk(c, Kt=Kt, g=g):
                            ksz = tl["last"] if c == nch - 1 else 128
                            near = c >= nch - 2
                            dl = 0 if c == nch - 1 else 1
                            lg = plg[cnt2["plg"] % 2]
                            cnt2["plg"] += 1
                            pT = pTs[cnt2["pT"] % 3]
                            cnt2["pT"] += 1
                            w4 = 4 * nq

                            def em(e, lg=lg, Kt=Kt, QTt=QTt, MB=MB, c=c, ksz=ksz, near=near, dl=dl, g=g, nq=nq, w4=w4):
                                e.matmul(out=lg.t[0:ksz, 0:w4], lhsT=Kt[:, c * 128:c * 128 + ksz],
                                         rhs=QTt[:, 4 * g:4 * g + 4, 0:nq], start=True, stop=False)
                                r = e.matmul(out=lg.t[0:ksz, 0:w4], lhsT=MB[0:nq, c * 128:c * 128 + ksz],
                                             rhs=(i4[:, :] if nq == 128 else i4s[:, :]), start=False, stop=(not near))
                                if near:
                                    r = e.matmul(out=lg.t[0:ksz, 0:w4], lhsT=ident[0:ksz, 0:ksz],
                                                 rhs=BT[0:ksz, dl, 4 * g:4 * g + 4, 0:nq], start=False, stop=True)
                                return r
                            P.op("pe", em, reads=[Kt.b, QTt.b, MB.b, i4.b, i4s.b, ident.b, BT.b], writes=[lg.b])
                            return (lg, pT, ksz, w4)

                    def issue_exp_pv(c, lg, pT, ksz, w4, Vt=Vt):
                            P.op("act", lambda e, lg=lg, pT=pT, ksz=ksz, w4=w4: e.activation(
                                out=pT[0:ksz, 0:w4], in_=lg.t[0:ksz, 0:w4], func=AF.Exp), reads=[lg.b], writes=[pT.b])

                            def em2(e, pT=pT, Vt=Vt, c=c, ksz=ksz, nq=nq, nch=nch):
                                r = None
                                for j in range(4):
                                    r = e.matmul(out=poa[j // 2].t[0:nq, (j % 2) * 129:(j % 2) * 129 + 129],
                                                 lhsT=pT[0:ksz, j * nq:(j + 1) * nq], rhs=Vt[0:ksz, c, 0:129],
                                                 start=(c == 0 and j % 2 == 0), stop=(c == nch - 1), skip_group_check=True)
                                return r
                            P.op("pe", em2, reads=[pT.b, Vt.b], writes=[poa[0].b, poa[1].b])

                    st_ = issue_qk(0)
                    for c in range(nch):
                        nx_ = issue_qk(c + 1) if c + 1 < nch else None
                        issue_exp_pv(c, *st_)
                        st_ = nx_
                    for j2 in range(2):
                        P.op("dve", lambda e, j2=j2, nq=nq: e.tensor_scalar(
                            out=den[0:nq, j2 * 2:j2 * 2 + 2], in0=poa[j2].t[0:nq, 0:258].rearrange("p (j d) -> p j d", d=129)[:, :, 128],
                            scalar1=1e-30, scalar2=None, op0=ALU.max), reads=[poa[j2].b], writes=[den.b])
                    P.op("dve", lambda e, nq=nq: e.reciprocal(out=den[0:nq, 0:4], in_=den[0:nq, 0:4]), reads=[den.b], writes=[den.b])
                    for j in range(4):
                        eng = "act" if j // 2 == 0 else "dve"
                        col = (4 * g + j) * 128
                        if eng == "act":
                            P.op("act", lambda e, j=j, col=col, nq=nq: e.activation(
                                out=Ot[0:nq, col:col + 128], in_=poa[j // 2].t[0:nq, (j % 2) * 129:(j % 2) * 129 + 128],
                                func=AF.Copy, scale=den[0:nq, j:j + 1]), reads=[poa[j // 2].b, den.b], writes=[Ot.b])
                        else:
                            P.op("dve", lambda e, j=j, col=col, nq=nq: e.tensor_scalar(
                                out=Ot[0:nq, col:col + 128], in0=poa[j // 2].t[0:nq, (j % 2) * 129:(j % 2) * 129 + 128],
                                scalar1=den[0:nq, j:j + 1], scalar2=None, op0=ALU.mult), reads=[poa[j // 2].b, den.b], writes=[Ot.b])
                for k8 in range(4):
                    pb = ptr[k8 % 2]
                    pbv = pb.t[:, :].bitcast(BF16)

                    def em(e, k8=k8, pbv=pbv, nq=nq):
                        r = None
                        for i in range(8):
                            kc = k8 * 8 + i
                            r = e.transpose(out=pbv[:, i * nq:(i + 1) * nq], in_=Ot[0:nq, kc * 128:(kc + 1) * 128],
                                            identity=ident[0:nq, 0:nq])
                        return r
                    P.op("pe", em, reads=[Ot.b, ident.b], writes=[pb.b])
                    P.op("dve", lambda e, k8=k8, pbv=pbv, nq=nq: e.tensor_tensor(
                        out=OGs[:, k8 * 8:(k8 + 1) * 8, 0:nq], in0=pbv[:, 0:8 * nq].rearrange("p (i q) -> p i q", q=nq),
                        in1=SGt[:, k8 * 8:(k8 + 1) * 8, 0:nq], op=ALU.mult), reads=[pb.b, SGt.b], writes=[OGs.b])
                P.op("sp", lambda e, qc=qc, nq=nq: [e.dma_start(out=OGT[:, qc:qc + nq].rearrange("(kc p) q -> p kc q", p=128),
                                                               in_=OGs[:, :, 0:nq])], reads=[OGs.b], dma=OGs.b)
            indexer(0)
            for ti in range(len(tiles)):
                if ti + 1 < len(tiles):
                    indexer(ti + 1)
                attend(ti)
            P.drain("sp")
            P.emit_block()
            P.release([t.b for t in [SC, KBI, junk, Ot, SGt, OGs, BT, ident, i4, i4s, oh, rb, rb15, adm] + MBs + QIs + KIs
                       + WIs + QTs + KTg + Vg + rls + pTs])
            if stop == 2:
                return nc

        with ExitStack() as st:
            OGt = T(P, st, "OGt", [128, KC, 512], BF16)
            Wts = [T(P, st, "w3t%d" % i, [128, KC, 512], BF16) for i in range(2)]
            xs = [T(P, st, "xs%d" % i, [128, 4, 512], F32) for i in range(2)]
            so = [T(P, st, "so%d" % i, [128, 4, 512], F32) for i in range(2)]
            pbanks = [T(P, st, "p3b%d" % i, [128, 512], F32, psum=True) for i in range(8)]
            c3d = {'n': 0}
            def grp_fn0(t0, ntile):
                ntok = ntile * 128
                P.op("sp", lambda e, t0=t0, ntok=ntok: [e.dma_start(
                    out=OGt[:, :, 0:ntok], in_=OGT[:, t0 * 128:t0 * 128 + ntok].rearrange("(kc p) q -> p kc q", p=128))],
                    writes=[OGt.b], dma=OGt.b)
                for blk in range(8):
                    Wt = Wts[c3d['n'] % 2]
                    xsb = xs[c3d['n'] % 2]
                    sob = so[c3d['n'] % 2]
                    pbs = pbanks[(c3d['n'] % 2) * 4:(c3d['n'] % 2) * 4 + 4]
                    c3d['n'] += 1
                    load_w(Wt, w_out, blk * 512, 512)
                    P.op("sp", lambda e, xsb=xsb, t0=t0, ntok=ntok, ntile=ntile, blk=blk: [e.dma_start(
                        out=xsb[:, 0:ntile, :], in_=xf[t0 * 128:t0 * 128 + ntok, blk * 512:(blk + 1) * 512].rearrange(
                            "(j p) c -> p j c", p=128))], writes=[xsb.b], dma=xsb.b)
                    tm_mm(Wt, 512, OGt, ntile, pbs)
                    for j in range(ntile):
                        P.op("dve", lambda e, j=j, sob=sob, xsb=xsb, pbs=pbs: e.tensor_tensor(
                            out=sob[:, j, :], in0=pbs[j].t[:, :], in1=xsb[:, j, :], op=ALU.add),
                            reads=[pbs[j].b, xsb.b], writes=[sob.b])
                    P.op("sp", lambda e, sob=sob, t0=t0, ntok=ntok, ntile=ntile, blk=blk: [e.dma_start(
                        out=X1[t0 * 128:t0 * 128 + ntok, blk * 512:(blk + 1) * 512].rearrange("(j p) c -> p j c", p=128),
                        in_=sob[:, 0:ntile, :])], reads=[sob.b], dma=sob.b)
            for (t0, ntile) in full_groups:
                grp_fn0(t0, ntile)
            P.drain("sp")
            P.emit_block()
            P.release([t.b for t in [OGt] + Wts + xs + so])
            if stop == 3:
                return nc

        with ExitStack() as st:
            xb = T(P, st, "xb4", [128, D], F32)
            xn = T(P, st, "xn4", [128, 4, D], BF16)
            hT = T(P, st, "hT4", [128, KC, 512], BF16)
            Wts = [T(P, st, "w4t%d" % i, [128, KC, 256], BF16) for i in range(2)]
            UF = T(P, st, "UF", [128, 2, 527], F32)
            T1 = T(P, st, "T1", [128, 2, 527], F32)
            T2 = T(P, st, "T2", [128, 2, 527], F32)
            CR = T(P, st, "CR", [128, KC, 15], F32)
            HS = T(P, st, "HS", [64, D], F32)
            HST = T(P, st, "HST", [128, KC, 4, 15], F32)
            smt = [T(P, st, "smt%d" % i, [128, 2, 512], BF16) for i in range(2)]
            stm = [T(P, st, "stm4%d" % i, [128, 2, 256], F32) for i in range(2)]
            ss = T(P, st, "ss4", [128, 4], F32)
            rstd = T(P, st, "rstd4", [128, 4], F32)
            nwcol = T(P, st, "nwcol4", [128, KC], F32)
            epst = T(P, st, "epst4", [128, 1], F32)
            ident = T(P, st, "ident4", [128, 128], BF16)
            identf = T(P, st, "identf4", [128, 128], F32)
            icn = T(P, st, "icn", [128, 4, 128], F32)
            pbanks = [T(P, st, "p4b%d" % i, [128, 512], F32, psum=True) for i in range(8)]
            c4 = {"w": 0, "blk": 0, "smt": 0, "tm": 0}

            P.op("sp", lambda e: [e.dma_start(out=ident[:, :], in_=c_ident[:, :])], writes=[ident.b], dma=ident.b)
            P.op("sp", lambda e: [e.dma_start(out=identf[:, :], in_=c_identf[:, :])], writes=[identf.b], dma=identf.b)
            P.op("sp", lambda e: [e.dma_start(out=nwcol[:, :], in_=norm_w[1, :, :])],
                 writes=[nwcol.b], dma=nwcol.b)
            P.op("sp", lambda e: [e.dma_start(out=icn[:, :, :].rearrange("p a b -> p (a b)"),
                                              in_=c_invcnt.rearrange("a b -> (a b)").partition_broadcast(128))],
                 writes=[icn.b], dma=icn.b)
            P.op("sp", lambda e: [e.dma_start(out=HS[0:60, :], in_=spool.rearrange("s i d -> (s i) d"))], writes=[HS.b], dma=HS.b)
            P.op("dve", lambda e: e.memset(epst[:, :], EPS), writes=[epst.b])
            P.op("dve", lambda e: e.memset(CR[:, :, :], 0.0), writes=[CR.b])
            for k4 in range(8):
                pb = pbanks[k4 % 8]

                def em(e, k4=k4, pb=pb):
                    r = None
                    for i in range(4):
                        kc = k4 * 4 + i
                        r = e.transpose(out=pb.t[:, i * 60:(i + 1) * 60], in_=HS[0:60, kc * 128:(kc + 1) * 128],
                                        identity=identf[0:60, 0:60])
                    return r
                P.op("pe", em, reads=[HS.b, identf.b], writes=[pb.b])
                P.op("dve", lambda e, k4=k4, pb=pb: e.tensor_copy(
                    out=HST[:, k4 * 4:(k4 + 1) * 4, :, :].rearrange("p k s i -> p k (s i)"),
                    in_=pb.t[:, 0:240].rearrange("p (k x) -> p k x", k=4)), reads=[pb.b], writes=[HST.b])

            def pool_mix(seg_views, w, widx, ntokseg, invc_first):
                V = seg_views
                L = ntokseg
                tot = L + 15
                cur = UF
                bufs = [T1, T2]
                bi = 0
                sh = 1
                lo = 0
                while sh < w:
                    dst = bufs[bi % 2]
                    lo2 = lo + sh
                    P.op("dve", lambda e, cur=cur, dst=dst, lo2=lo2, sh=sh, tot=tot: e.tensor_tensor(
                        out=V(dst, lo2, tot), in0=V(cur, lo2, tot), in1=V(cur, lo2 - sh, tot - sh), op=ALU.add),
                        reads=[cur.b], writes=[dst.b])
                    cur = dst
                    bi += 1
                    lo = lo2
                    sh *= 2
                return cur

            def group4a(gi, t0, ntile):
                ntok = ntile * 128
                has_sample = (t0 + ntile - 1) == 17
                npt = ntile - (1 if has_sample else 0)
                Lp = npt * 128
                if gi == 0:
                    norm_tiles(X1, t0 * 128, ntile, xb, xn, ss, rstd)
                transpose_to_hT(xn, ntile, hT, nwcol, pbanks, 0)
                if gi + 1 < len(full_groups):
                    norm_tiles(X1, full_groups[gi + 1][0] * 128, full_groups[gi + 1][1], xb, xn, ss, rstd)
                for blk in range(32):
                    isu = blk < 16
                    Wt = Wts[c4["w"] % 2]
                    c4["w"] += 1
                    load_w(Wt, pw_in, blk * 256, 256)
                    pbs = pbanks[(c4["blk"] % 4) * 2:(c4["blk"] % 4) * 2 + 2]
                    c4["blk"] += 1
                    fm_mm(Wt, 256, hT, ntok, pbs)
                    sg = smt[c4["smt"] % 2]
                    c4["smt"] += 1
                    if not isu:
                        for s in range(2):
                            P.op("act", lambda e, s=s, sg=sg, pbs=pbs: e.activation(out=sg[:, s, 0:ntok], in_=pbs[s].t[:, 0:ntok],
                                                                                     func=AF.Silu), reads=[pbs[s].b], writes=[sg.b])
                        r0 = (blk - 16) * 256
                        dst = SG1T[r0:r0 + 256, t0 * 128:t0 * 128 + ntok].rearrange("(s p) t -> p s t", p=128)
                        P.op("sp", lambda e, sg=sg, dst=dst: [e.dma_start(out=dst, in_=sg[:, :, 0:ntok])], reads=[sg.b], dma=sg.b)
                        continue
                    fc0 = blk * 2
                    widx = blk // 4
                    w = POOL_WINDOWS[widx]
                    if Lp > 0:
                        P.op("dve", lambda e, fc0=fc0: e.tensor_copy(out=UF[:, :, 0:15], in_=CR[:, fc0:fc0 + 2, :]),
                             reads=[CR.b], writes=[UF.b])
                        for s in range(2):
                            P.op("act", lambda e, s=s, pbs=pbs, Lp=Lp: e.activation(out=UF[:, s, 15:15 + Lp], in_=pbs[s].t[:, 0:Lp],
                                                                                     func=AF.Copy), reads=[pbs[s].b], writes=[UF.b])
                        P.op("dve", lambda e, fc0=fc0, Lp=Lp: e.tensor_copy(out=CR[:, fc0:fc0 + 2, :], in_=UF[:, :, Lp:Lp + 15]),
                             reads=[UF.b], writes=[CR.b])
                        Vp = lambda b, lo, hi: b[:, :, lo:hi]
                        S = pool_mix(Vp, w, widx, Lp, None)
                        if gi == 0:
                            for s in range(2):
                                P.op("dve", lambda e, S=S, widx=widx, s=s: e.tensor_tensor(
                                    out=S[:, s, 15 + 128:15 + 256], in0=S[:, s, 15 + 128:15 + 256],
                                    in1=icn[:, widx, :], op=ALU.mult),
                                    reads=[S.b, icn.b], writes=[S.b])
                            P.op("dve", lambda e, S=S, w=w: e.tensor_scalar(
                                out=S[:, :, 15:15 + 128], in0=S[:, :, 15:15 + 128], scalar1=1.0 / w, scalar2=None, op0=ALU.mult),
                                reads=[S.b], writes=[S.b])
                            P.op("dve", lambda e, S=S, w=w, Lp=Lp: e.tensor_scalar(
                                out=S[:, :, 15 + 256:15 + Lp], in0=S[:, :, 15 + 256:15 + Lp], scalar1=1.0 / w, scalar2=None,
                                op0=ALU.mult), reads=[S.b], writes=[S.b])
                            P.op("dve", lambda e, S=S, sg=sg, Lp=Lp: e.tensor_tensor(
                                out=sg[:, :, 0:Lp], in0=S[:, :, 15:15 + Lp], in1=UF[:, :, 15:15 + Lp], op=ALU.subtract),
                                reads=[S.b, UF.b], writes=[sg.b])
                        else:
                            P.op("dve", lambda e, S=S, sg=sg, Lp=Lp, w=w: e.scalar_tensor_tensor(
                                out=sg[:, :, 0:Lp], in0=S[:, :, 15:15 + Lp], scalar=1.0 / w, in1=UF[:, :, 15:15 + Lp],
                                op0=ALU.mult, op1=ALU.subtract), reads=[S.b, UF.b], writes=[sg.b])
                    if has_sample:
                        o0 = Lp
                        U4 = lambda b, lo, hi: b[:, :, 0:188].rearrange("p s (b x) -> p s b x", x=47)[:, :, :, lo:hi]
                        P.op("dve", lambda e, fc0=fc0, U4=U4: e.tensor_copy(out=U4(UF, 0, 15), in_=HST[:, fc0:fc0 + 2, :, :]),
                             reads=[HST.b], writes=[UF.b])
                        for s in range(2):
                            P.op("act", lambda e, s=s, pbs=pbs, o0=o0: e.activation(
                                out=UF[:, s, 0:188].rearrange("p (b x) -> p b x", x=47)[:, :, 15:47],
                                in_=pbs[s].t[:, o0:o0 + 128].rearrange("p (b x) -> p b x", x=32), func=AF.Copy),
                                reads=[pbs[s].b], writes=[UF.b])
                        cur = UF
                        bufs = [T1, T2]
                        bi = 0
                        sh = 1
                        lo = 0
                        while sh < w:
                            dstb = bufs[bi % 2]
                            lo2 = lo + sh
                            P.op("dve", lambda e, cur=cur, dstb=dstb, lo2=lo2, sh=sh, U4=U4: e.tensor_tensor(
                                out=U4(dstb, lo2, 47), in0=U4(cur, lo2, 47), in1=U4(cur, lo2 - sh, 47 - sh), op=ALU.add),
                                reads=[cur.b], writes=[dstb.b])
                            cur = dstb
                            bi += 1
                            lo = lo2
                            sh *= 2
                        for s in range(2):
                            P.op("dve", lambda e, s=s, cur=cur, sg=sg, o0=o0, w=w: e.scalar_tensor_tensor(
                                out=sg[:, s, o0:o0 + 128].rearrange("p (b x) -> p b x", x=32),
                                in0=cur[:, s, 0:188].rearrange("p (b x) -> p b x", x=47)[:, :, 15:47], scalar=1.0 / w,
                                in1=UF[:, s, 0:188].rearrange("p (b x) -> p b x", x=47)[:, :, 15:47],
                                op0=ALU.mult, op1=ALU.subtract), reads=[cur.b, UF.b], writes=[sg.b])
                    r0 = blk * 256
                    dst = M1T[r0:r0 + 256, t0 * 128:t0 * 128 + ntok].rearrange("(s p) t -> p s t", p=128)
                    P.op("sp", lambda e, sg=sg, dst=dst: [e.dma_start(out=dst, in_=sg[:, :, 0:ntok])], reads=[sg.b], dma=sg.b)
                    if has_sample:
                        pb2 = pbanks[(c4["blk"] % 4) * 2:(c4["blk"] % 4) * 2 + 2]
                        c4["blk"] += 1
                        for j in range(2):
                            def em(e, j=j, Wt=Wt, pb2=pb2):
                                r = None
                                for kc in range(KC):
                                    r = e.matmul(out=pb2[j].t[:, 0:256], lhsT=hT[:, kc, j * 128:(j + 1) * 128],
                                                 rhs=Wt[:, kc, 0:256], start=(kc == 0), stop=(kc == KC - 1))
                                return r
                            P.op("pe", em, reads=[Wt.b, hT.b], writes=[pb2[j].b])
                        so_ = stm[c4["tm"] % 2]
                        c4["tm"] += 1
                        for j in range(2):
                            P.op("act", lambda e, j=j, so_=so_, pb2=pb2: e.activation(out=so_[:, j, :], in_=pb2[j].t[:, 0:256], func=AF.Copy),
                                 reads=[pb2[j].b], writes=[so_.b])
                        P.op("sp", lambda e, so_=so_, blk=blk: [e.dma_start(
                            out=u_o[:, blk * 256:(blk + 1) * 256].rearrange("(j p) c -> p j c", p=128), in_=so_[:, :, :])],
                            reads=[so_.b], dma=so_.b)
            for gi, (t0, ntile) in enumerate(full_groups):
                group4a(gi, t0, ntile)
            P.drain("sp")
            P.emit_block()
            P.release([t.b for t in [xb, xn, hT, UF, T1, T2, CR, HS, HST, ident, identf, icn, nwcol] + Wts + smt + stm])
            if stop == 4:
                return nc

        with ExitStack() as st:
            MT = T(P, st, "MT", [128, KC, 512], BF16)
            SG1 = T(P, st, "SG1", [128, KC, 512], BF16)
            Wts = [T(P, st, "w5t%d" % i, [128, KC, 512], BF16) for i in range(2)]
            GWt = [T(P, st, "gwt%d" % i, [128, 8, 512], BF16) for i in range(2)]
            xs = [T(P, st, "xs5%d" % i, [128, 4, 512], F32) for i in range(2)]
            so = [T(P, st, "so5%d" % i, [128, 4, 512], F32) for i in range(2)]
            psc = T(P, st, "psc", [128, KC], F32)
            pbanks = [T(P, st, "p5b%d" % i, [128, 512], F32, psum=True) for i in range(8)]
            c5 = {"g": 0, "w": 0}
            P.op("sp", lambda e: [e.dma_start(out=psc[:, :], in_=pscale[:, :])],
                 writes=[psc.b], dma=psc.b)
            def grp_fn1(t0, ntile):
                ntok = ntile * 128
                P.op("sp", lambda e, t0=t0, ntok=ntok: [e.dma_start(
                    out=MT[:, :, 0:ntok], in_=M1T[:, t0 * 128:t0 * 128 + ntok].rearrange("(kc p) q -> p kc q", p=128))],
                    writes=[MT.b], dma=MT.b)
                P.op("sp", lambda e, t0=t0, ntok=ntok: [e.dma_start(
                    out=SG1[:, :, 0:ntok], in_=SG1T[:, t0 * 128:t0 * 128 + ntok].rearrange("(kc p) q -> p kc q", p=128))],
                    writes=[SG1.b], dma=SG1.b)
                for g in range(4):
                    for half in range(2):
                        Gt = GWt[c5["g"] % 2]
                        pbs = pbanks[(c5["g"] % 2) * 4:(c5["g"] % 2) * 4 + 4]
                        c5["g"] += 1
                        src = gw[g, :, half * 512:(half + 1) * 512].rearrange("(kc p) n -> p kc n", p=128)
                        P.op("pool", lambda e, Gt=Gt, src=src: [e.dma_start(out=Gt[:, :, :], in_=src)], writes=[Gt.b], dma=Gt.b)
                        fm_mm(Gt, 512, MT, ntok, pbs, nk=8, kofs=g * 8)
                        for s in range(4):
                            fc = g * 8 + half * 4 + s
                            P.op("dve", lambda e, s=s, fc=fc, pbs=pbs, ntok=ntok: e.scalar_tensor_tensor(
                                out=SG1[:, fc, 0:ntok], in0=pbs[s].t[:, 0:ntok], scalar=psc[:, fc:fc + 1],
                                in1=SG1[:, fc, 0:ntok], op0=ALU.mult, op1=ALU.mult),
                                reads=[pbs[s].b, psc.b, SG1.b], writes=[SG1.b])
                for blk in range(8):
                    Wt = Wts[c5["w"] % 2]
                    xsb = xs[c5["w"] % 2]
                    sob = so[c5["w"] % 2]
                    pbs = pbanks[(c5["w"] % 2) * 4:(c5["w"] % 2) * 4 + 4]
                    c5["w"] += 1
                    load_w(Wt, pw_out, blk * 512, 512)
                    P.op("sp", lambda e, xsb=xsb, t0=t0, ntok=ntok, ntile=ntile, blk=blk: [e.dma_start(
                        out=xsb[:, 0:ntile, :], in_=X1[t0 * 128:t0 * 128 + ntok, blk * 512:(blk + 1) * 512].rearrange(
                            "(j p) c -> p j c", p=128))], writes=[xsb.b], dma=xsb.b)
                    tm_mm(Wt, 512, SG1, ntile, pbs)
                    for j in range(ntile):
                        P.op("dve", lambda e, j=j, sob=sob, xsb=xsb, pbs=pbs: e.tensor_tensor(
                            out=sob[:, j, :], in0=pbs[j].t[:, :], in1=xsb[:, j, :], op=ALU.add),
                            reads=[pbs[j].b, xsb.b], writes=[sob.b])
                    P.op("sp", lambda e, sob=sob, t0=t0, ntok=ntok, ntile=ntile, blk=blk: [e.dma_start(
                        out=X2[t0 * 128:t0 * 128 + ntok, blk * 512:(blk + 1) * 512].rearrange("(j p) c -> p j c", p=128),
                        in_=sob[:, 0:ntile, :])], reads=[sob.b], dma=sob.b)
            for (t0, ntile) in full_groups:
                grp_fn1(t0, ntile)
            P.drain("sp")
            P.emit_block()
            P.release([t.b for t in [MT, SG1, psc] + Wts + GWt + xs + so])
            if stop == 5:
                return nc

        with ExitStack() as st:
            xbs = [T(P, st, "x6b%d" % i, [128, D], F32) for i in range(2)]
            ybs = [T(P, st, "y6b%d" % i, [128, D], F32) for i in range(2)]
            fw = T(P, st, "fw", [128, D], F32)
            jk = T(P, st, "jk6", [128, D], BF16)
            ss = T(P, st, "ss6", [128, 2], F32)
            rs = T(P, st, "rs6", [128, 2], F32)
            epst = T(P, st, "epst6", [128, 1], F32)
            P.op("sp", lambda e: [e.dma_start(out=fw[:, :], in_=fnorm_w.partition_broadcast(128))], writes=[fw.b], dma=fw.b)
            P.op("dve", lambda e: e.memset(epst[:, :], EPS), writes=[epst.b])
            for t in range(NFULL):
                xb_, yb_ = xbs[t % 2], ybs[t % 2]
                j = t % 2
                P.op("sp", lambda e, xb_=xb_, t=t: [e.dma_start(out=xb_[:, :], in_=X2[t * 128:(t + 1) * 128, :])],
                     writes=[xb_.b], dma=xb_.b)
                P.op("act", lambda e, xb_=xb_, j=j: e.activation(out=jk[:, :], in_=xb_[:, :], func=AF.Square, accum_out=ss[:, j:j + 1]),
                     reads=[xb_.b], writes=[jk.b, ss.b])
                P.op("act", lambda e, j=j: e.activation(out=rs[:, j:j + 1], in_=ss[:, j:j + 1], func=AF.Sqrt, scale=1.0 / D,
                                                         bias=epst[:, 0:1]), reads=[ss.b, epst.b], writes=[rs.b])
                P.op("dve", lambda e, j=j: e.reciprocal(out=rs[:, j:j + 1], in_=rs[:, j:j + 1]), reads=[rs.b], writes=[rs.b])
                P.op("dve", lambda e, xb_=xb_, yb_=yb_, j=j: e.scalar_tensor_tensor(
                    out=yb_[:, :], in0=xb_[:, :], scalar=rs[:, j:j + 1], in1=fw[:, :], op0=ALU.mult, op1=ALU.mult),
                    reads=[xb_.b, rs.b, fw.b], writes=[yb_.b])
                P.op("sp", lambda e, yb_=yb_, t=t: [e.dma_start(out=y_o[t * 128:(t + 1) * 128, :], in_=yb_[:, :])],
                     reads=[yb_.b], dma=yb_.b)
            P.drain("sp")
            P.emit_block()
    return nc


def _rel_bucket_np(rel):
    nb = 16
    ret = np.where(rel > 0, nb, 0)
    n = np.abs(rel)
    max_exact = nb // 2
    nf = np.maximum(n, 1).astype(np.float32)
    large = max_exact + (np.log(nf / np.float32(max_exact)) / np.float32(math.log(128 / max_exact))
                         * np.float32(nb - max_exact)).astype(np.int32)
    large = np.minimum(large, nb - 1)
    return ret + np.where(n < max_exact, n, large)


def _consts():
    bf = ml_dtypes.bfloat16
    ident = np.eye(128, dtype=np.float32)
    i4 = np.concatenate([ident] * 4, axis=1)
    i4s = np.concatenate([np.eye(32, dtype=np.float32)] * 4, axis=1)
    rel = np.arange(384) - 255
    bk = _rel_bucket_np(rel.astype(np.int32))
    oh = (bk[None, :] == np.arange(32)[:, None]).astype(np.float32)
    adm = np.zeros((128, 128), np.float32)
    adm[:64, 64:] = BIGNEG
    return dict(c_ident=ident.astype(bf), c_identf=ident, c_i4=i4.astype(bf), c_i4s=i4s.astype(bf), c_oh=oh, c_adm=adm)


_NC_CACHE = {}


def make_in_maps(inputs, cores=range(8)):
    bf = ml_dtypes.bfloat16
    cs = _consts()
    x_prompt = inputs["x_prompt"]
    x_sample = inputs["x_sample"]
    maps = []
    for c in cores:
        b, h = c // 2, c % 2
        xf = np.zeros((TOKF, D), np.float32)
        xkv = np.zeros((NKV * 128, D), np.float32)
        if h == 1:
            xf[0:128] = x_prompt[b, 1920:2048]
            xkv[:] = x_prompt[b, 0:1920]
        xf[128:128 + 2048] = x_prompt[b, h * 2048:(h + 1) * 2048]
        xf[17 * 128:] = x_sample[4 * c:4 * c + 4].reshape(128, D)
        kb = np.zeros((NKEY,), np.float32)
        if h == 0:
            kb[0:2048] = BIGNEG
        ic = np.zeros((4, 128), np.float32)
        pos = h * 2048 + np.arange(128)
        for wi_, w in enumerate(POOL_WINDOWS):
            ic[wi_] = 1.0 / np.minimum(pos + 1, w)
        m = dict(
            xf=xf, xkv=xkv,
            w_in=inputs["attn_w_in"][0], w_out=inputs["attn_w_out"][0], pw_in=inputs["pool_w_in"][0],
            gw=inputs["pool_group_w"][0], pw_out=inputs["pool_w_out"][0],
            norm_w=inputs["norm_w"].reshape(2, KC, 128).transpose(0, 2, 1), fnorm_w=inputs["final_norm_w"],
            pscale=inputs["pool_scale"][0].reshape(KC, 128).T,
            rel_bias=inputs["rel_bias"],
            ck=inputs["cache_k"][0, 4 * c:4 * c + 4].reshape(4, 4096, 1024),
            cv=inputs["cache_v"][0, 4 * c:4 * c + 4].reshape(4, 4096, 1024),
            cki=inputs["cache_kidx"][0, 4 * c:4 * c + 4],
            spool=inputs["state_pool"][0, 4 * c:4 * c + 4],
            c_kbias=kb.astype(bf), c_invcnt=ic,
        )
        m.update(cs)
        maps.append({k: np.ascontiguousarray(v) for k, v in m.items()})
    return maps


def assemble(results):
    y_p = np.zeros((4, 4096, D), np.float32)
    y_s = np.zeros((32, 32, D), np.float32)
    k_p = np.zeros((1, 4, 4096, 8, 128), np.float32)
    v_p = np.zeros((1, 4, 4096, 8, 128), np.float32)
    ki_p = np.zeros((1, 4, 4096, 64), np.float32)
    pool_p = np.zeros((1, 4, 15, D), np.float32)
    k_s = np.zeros((1, 32, 32, 8, 128), np.float32)
    v_s = np.zeros((1, 32, 32, 8, 128), np.float32)
    ki_s = np.zeros((1, 32, 32, 64), np.float32)
    pool_s = np.zeros((1, 32, 15, D), np.float32)
    for c, r in enumerate(results):
        b, h = c // 2, c % 2
        sl = slice(h * 2048, (h + 1) * 2048)
        y_p[b, sl] = r["y_o"][128:128 + 2048]
        y_s[4 * c:4 * c + 4] = r["y_o"][17 * 128:].reshape(4, 32, D)
        k_p[0, b, sl] = r["k_o"][128:128 + 2048].reshape(2048, 8, 128)
        v_p[0, b, sl] = r["v_o"][128:128 + 2048].reshape(2048, 8, 128)
        ki_p[0, b, sl] = r["ki_o"][128:128 + 2048]
        k_s[0, 4 * c:4 * c + 4] = r["k_o"][17 * 128:].reshape(4, 32, 8, 128)
        v_s[0, 4 * c:4 * c + 4] = r["v_o"][17 * 128:].reshape(4, 32, 8, 128)
        ki_s[0, 4 * c:4 * c + 4] = r["ki_o"][17 * 128:].reshape(4, 32, 64)
        if h == 1:
            pool_p[0, b] = r["u_o"][128 - 15:128]
        pool_s[0, 4 * c:4 * c + 4] = r["u_o"][128:].reshape(4, 32, D)[:, 17:]
    return (y_p, y_s, k_p, v_p, ki_p, pool_p, k_s, v_s, ki_s, pool_s)


def kernel(**inputs):
    inputs = {k: np.asarray(v) for k, v in inputs.items()}
    if "nc" not in _NC_CACHE:
        _NC_CACHE["nc"] = build_program()
    nc = _NC_CACHE["nc"]
    in_maps = make_in_maps(inputs)
    res = run_bass_kernel_spmd(nc, in_maps, core_ids=list(range(8)))
    return assemble(res.results)
```

```python
import math
from contextlib import ExitStack
import numpy as np
import ml_dtypes
import concourse.bass as bass
import concourse.mybir as mybir
from concourse.bass_utils import run_bass_kernel_spmd

F32 = mybir.dt.float32
BF16 = mybir.dt.bfloat16
AF = mybir.ActivationFunctionType
ALU = mybir.AluOpType
AX = mybir.AxisListType

D = 4096
KC = 32
NFULL = 18
NKV = 15
TOKF = NFULL * 128
NKEY = 4224
ATTN_IN = 12384
NIT = 18
NEG = -30000.0
BIGNEG = -1.0e30
EPS = 1e-6
C_Q, C_K, C_V, C_QI, C_KI, C_WI, C_G = 0, 4096, 5120, 6144, 8192, 8256, 8288
POOL_WINDOWS = (2, 4, 8, 16)


class Buf:
    def __init__(self, name):
        self.name = name
        self.last_w = None
        self.readers = []
        self.sem = None
        self.semcount = 0


class Prog:
    ENG = ("sp", "act", "pool", "dve", "pe")

    def __init__(self, nc, stack):
        self.nc = nc
        self.stack = stack
        self.engs = {"sp": nc.sync, "act": nc.scalar, "pool": nc.gpsimd, "dve": nc.vector, "pe": nc.tensor}
        self.esem = {k: stack.enter_context(nc.semaphore("es_" + k)) for k in self.ENG}
        self.ecount = {k: 0 for k in self.ENG}
        self.waited = {k: {} for k in self.ENG}
        self.q = {k: [] for k in self.ENG}
        self.dma_events = {}
        self.nbuf = 0
        self.free_sems = []

    def buf(self, name=None):
        self.nbuf += 1
        return Buf(name or "b%d" % self.nbuf)

    def _bufsem(self, b):
        if b.sem is None:
            if self.free_sems:
                b.sem, b.semcount = self.free_sems.pop()
            else:
                b.sem = self.stack.enter_context(self.nc.semaphore("bs%d" % self.nbuf + b.name))
                b.semcount = 0
        return b.sem

    def release(self, bufs):
        return

    def op(self, eng, emit, reads=(), writes=(), dma=None, ndma=1):
        waits = {}

        def need(ev):
            if ev is None:
                return
            s, v = ev
            if eng == "pe" and s is self.esem["pe"]:
                return
            k = id(s)
            if k not in waits or waits[k][1] < v:
                waits[k] = (s, v)

        for b in reads:
            need(b.last_w)
        for b in writes:
            need(b.last_w)
            for r in b.readers:
                need(r)
        wl = []
        for k, (s, v) in waits.items():
            if self.waited[eng].get(k, 0) >= v:
                continue
            self.waited[eng][k] = v
            wl.append((s, v))
        if dma is not None:
            s = self._bufsem(dma)
            dma.semcount += 16 * ndma
            ev = (s, dma.semcount)
            self.dma_events[id(s)] = ev
        else:
            self.ecount[eng] += 1
            ev = (self.esem[eng], self.ecount[eng])
        for b in writes:
            b.last_w = ev
            b.readers = []
        for b in reads:
            if b not in writes:
                b.readers.append(ev)
                if len(b.readers) > 24:
                    b.readers = b.readers[-24:]
        self.q[eng].append((wl, emit, dma is not None, ev, ndma))
        return ev

    def drain(self, eng="sp"):
        wl = list(self.dma_events.values())
        for k in self.ENG:
            if k != eng and self.ecount[k] > 0:
                wl.append((self.esem[k], self.ecount[k]))
        wl2 = []
        for (s, v) in wl:
            if self.waited[eng].get(id(s), 0) >= v:
                continue
            self.waited[eng][id(s)] = v
            wl2.append((s, v))
        self.q[eng].append((wl2, None, False, None, 0))
        self.dma_events = {}

    def emit_block(self):
        nc = self.nc
        with nc.Block() as block:
            for k in self.ENG:
                if not self.q[k]:
                    continue
                ql = self.q[k]

                def body(e, ql=ql):
                    for (wl, emit, isdma, ev, ndma) in ql:
                        for (s, v) in wl:
                            e.wait_ge(s, v)
                        if emit is None:
                            continue
                        r = emit(e)
                        if isdma:
                            if not isinstance(r, (list, tuple)):
                                r = [r]
                            assert len(r) == ndma, (len(r), ndma)
                            for ins in r:
                                ins.then_inc(ev[0], 16)
                        else:
                            r.then_inc(ev[0], 1)

                getattr(block, {"sp": "sync", "act": "scalar", "pool": "gpsimd", "dve": "vector", "pe": "tensor"}[k])(body)
        self.q = {k: [] for k in self.ENG}


class T:
    def __init__(self, P, st, name, shape, dt, psum=False):
        nc = P.nc
        self.t = st.enter_context((nc.psum_tensor if psum else nc.sbuf_tensor)(name, shape, dt))
        self.b = P.buf(name)

    def __getitem__(self, k):
        return self.t[k]


def alt_evac(i):
    return "act" if i % 2 == 0 else "dve"


def build_program(stop=99, skip=()):
    nc = bass.Bass("TRN2", target_bir_lowering=False)

    def din(name, shape, dt=F32):
        return nc.dram_tensor(name, list(shape), dt, kind="ExternalInput").ap()

    def dout(name, shape, dt=F32):
        return nc.dram_tensor(name, list(shape), dt, kind="ExternalOutput").ap()

    def dscr(name, shape, dt):
        return nc.dram_tensor(name, list(shape), dt, kind="Internal").ap()

    xf = din("xf", [TOKF, D])
    xkv = din("xkv", [NKV * 128, D])
    w_in = din("w_in", [D, ATTN_IN])
    w_out = din("w_out", [D, D])
    pw_in = din("pw_in", [D, 2 * D])
    gw = din("gw", [4, 1024, 1024])
    pw_out = din("pw_out", [D, D])
    norm_w = din("norm_w", [2, 128, KC])
    fnorm_w = din("fnorm_w", [D])
    pscale = din("pscale", [128, KC])
    rel_bias = din("rel_bias", [32, 32])
    ck = din("ck", [4, 4096, 1024])
    cv = din("cv", [4, 4096, 1024])
    cki = din("cki", [4, 4096, 64])
    spool = din("spool", [4, 15, D])
    c_ident = din("c_ident", [128, 128], BF16)
    c_identf = din("c_identf", [128, 128], F32)
    c_i4 = din("c_i4", [128, 512], BF16)
    c_i4s = din("c_i4s", [32, 128], BF16)
    c_oh = din("c_oh", [32, 384], F32)
    c_adm = din("c_adm", [128, 128], F32)
    c_kbias = din("c_kbias", [NKEY], BF16)
    c_invcnt = din("c_invcnt", [4, 128], F32)

    y_o = dout("y_o", [TOKF, D])
    k_o = dout("k_o", [TOKF, 1024])
    v_o = dout("v_o", [TOKF, 1024])
    ki_o = dout("ki_o", [TOKF, 64])
    u_o = dout("u_o", [256, D])

    QT = dscr("QT", [D, TOKF], BF16)
    KT = dscr("KT", [5, 1024, NKEY], BF16)
    VB = dscr("VB", [5, 8, 128, 33, 128], BF16)
    QIT = dscr("QIT", [2048, TOKF], BF16)
    KIT = dscr("KIT", [5, 64, NKEY], BF16)
    WI = dscr("WI", [TOKF, 32], F32)
    SGT = dscr("SGT", [D, TOKF], BF16)
    OGT = dscr("OGT", [D, TOKF], BF16)
    X1 = dscr("X1", [TOKF, D], F32)
    M1T = dscr("M1T", [D, TOKF], BF16)
    SG1T = dscr("SG1T", [D, TOKF], BF16)
    X2 = dscr("X2", [TOKF, D], F32)

    with ExitStack() as gst:
        P = Prog(nc, gst)

        def load_w(Wt, w_ap, c0, ncols, k0=0, nk=KC):
            src = w_ap[k0 * 128:(k0 + nk) * 128, c0:c0 + ncols].rearrange("(kc p) n -> p kc n", p=128)
            P.op("pool", lambda e: [e.dma_start(out=Wt[:, 0:nk, 0:ncols], in_=src)], writes=[Wt.b], dma=Wt.b)

        def norm_tiles(src_ap, row0, ntile, xb, xn, ss, rstd):
            for j in range(ntile):
                r0 = row0 + j * 128
                P.op("sp", lambda e, r0=r0: [e.dma_start(out=xb[:, :], in_=src_ap[r0:r0 + 128, :])],
                     writes=[xb.b], dma=xb.b)
                P.op("act", lambda e, j=j: e.activation(out=xn[:, j, :], in_=xb[:, :], func=AF.Square,
                                                         accum_out=ss[:, j:j + 1]),
                     reads=[xb.b], writes=[xn.b, ss.b])
                P.op("act", lambda e, j=j: e.activation(out=rstd[:, j:j + 1], in_=ss[:, j:j + 1], func=AF.Sqrt,
                                                         scale=1.0 / D, bias=epst[:, 0:1]),
                     reads=[ss.b, epst.b], writes=[rstd.b])
                P.op("dve", lambda e, j=j: e.reciprocal(out=rstd[:, j:j + 1], in_=rstd[:, j:j + 1]),
                     reads=[rstd.b], writes=[rstd.b])
                P.op("dve", lambda e, j=j: e.tensor_scalar(out=xn[:, j, :], in0=xb[:, :], scalar1=rstd[:, j:j + 1],
                                                            scalar2=None, op0=ALU.mult),
                     reads=[xb.b, rstd.b], writes=[xn.b])

        def transpose_to_hT(xn, ntile, hT, nwcol, pbanks, bank0):
            ntok = ntile * 128
            for kp in range(KC // 2):
                pb = pbanks[(bank0 + kp) % len(pbanks)]
                pbv = pb.t[:, :].bitcast(BF16)

                def em(e, kp=kp, pbv=pbv):
                    r = None
                    for i in range(2):
                        kc = kp * 2 + i
                        for j in range(ntile):
                            r = e.transpose(out=pbv[:, i * 512 + j * 128:i * 512 + (j + 1) * 128],
                                            in_=xn[:, j, kc * 128:(kc + 1) * 128], identity=ident[:, :])
                    return r
                P.op("pe", em, reads=[xn.b, ident.b], writes=[pb.b])
                for i in range(2):
                    kc = kp * 2 + i
                    eng = alt_evac(kp)
                    if eng == "act":
                        P.op("act", lambda e, kc=kc, i=i, pbv=pbv: e.activation(
                            out=hT[:, kc, 0:ntok], in_=pbv[:, i * 512:i * 512 + ntok], func=AF.Copy,
                            scale=nwcol[:, kc:kc + 1]), reads=[pb.b, nwcol.b], writes=[hT.b])
                    else:
                        P.op("dve", lambda e, kc=kc, i=i, pbv=pbv: e.tensor_scalar(
                            out=hT[:, kc, 0:ntok], in0=pbv[:, i * 512:i * 512 + ntok], scalar1=nwcol[:, kc:kc + 1],
                            scalar2=None, op0=ALU.mult), reads=[pb.b, nwcol.b], writes=[hT.b])

        def fm_mm(Wt, ncols, act, ntok, pbs, nk=KC, kofs=0):
            nsub = (ncols + 127) // 128
            for s in range(nsub):
                cw = min(128, ncols - s * 128)

                def em(e, s=s, cw=cw):
                    r = None
                    for kc in range(nk):
                        r = e.matmul(out=pbs[s].t[0:cw, 0:ntok], lhsT=Wt[:, kc, s * 128:s * 128 + cw],
                                     rhs=act[:, kofs + kc, 0:ntok], start=(kc == 0), stop=(kc == nk - 1))
                    return r
                P.op("pe", em, reads=[Wt.b, act.b], writes=[pbs[s].b])

        def tm_mm(Wt, ncols, act, ntile, pbs):
            for j in range(ntile):
                def em(e, j=j):
                    r = None
                    for kc in range(KC):
                        r = e.matmul(out=pbs[j].t[:, 0:ncols], lhsT=act[:, kc, j * 128:(j + 1) * 128],
                                     rhs=Wt[:, kc, 0:ncols], start=(kc == 0), stop=(kc == KC - 1))
                    return r
                P.op("pe", em, reads=[Wt.b, act.b], writes=[pbs[j].b])

        full_groups = [(0, 4), (4, 4), (8, 4), (12, 4), (16, 2)]
        kv_groups = [(0, 4), (4, 4), (8, 4), (12, 3)]

        def key_dst(ft):
            if ft < 17:
                return [(0, (15 + ft) * 128, 0, 128)]
            return [(1 + sb, 4096, sb * 32, 32) for sb in range(4)]

        with ExitStack() as st:
            xb = T(P, st, "xb", [128, D], F32)
            xn = T(P, st, "xn", [128, 4, D], BF16)
            hT = T(P, st, "hT", [128, KC, 512], BF16)
            Wts = [T(P, st, "wt%d" % i, [128, KC, 512], BF16) for i in range(2)]
            sfm = [T(P, st, "sfm%d" % i, [128, 4, 512], BF16) for i in range(2)]
            stm = [T(P, st, "stm%d" % i, [128, 4, 512], F32) for i in range(2)]
            svb = [T(P, st, "svb%d" % i, [128, 4, 512], BF16) for i in range(2)]
            ss = T(P, st, "ss", [128, 4], F32)
            rstd = T(P, st, "rstd", [128, 4], F32)
            nwcol = T(P, st, "nwcol", [128, KC], F32)
            epst = T(P, st, "epst", [128, 1], F32)
            ident = T(P, st, "ident", [128, 128], BF16)
            pbanks = [T(P, st, "pb%d" % i, [128, 512], F32, psum=True) for i in range(8)]
            cnt = {"w": 0, "fm": 0, "tm": 0, "vb": 0, "blk": 0}

            P.op("sp", lambda e: [e.dma_start(out=ident[:, :], in_=c_ident[:, :])], writes=[ident.b], dma=ident.b)
            P.op("sp", lambda e: [e.dma_start(out=nwcol[:, :], in_=norm_w[0, :, :])],
                 writes=[nwcol.b], dma=nwcol.b)
            P.op("dve", lambda e: e.memset(epst[:, :], EPS), writes=[epst.b])

            def nextW():
                w = Wts[cnt["w"] % 2]
                cnt["w"] += 1
                return w

            def banks4():
                b = cnt["blk"] % 2
                cnt["blk"] += 1
                return pbanks[b * 4:(b + 1) * 4]

            def evac_fm_to(pbs, nsub, ntok, scr_rows_ap_fn, func=None, scale=None):
                sg = sfm[cnt["fm"] % 2]
                cnt["fm"] += 1
                for s in range(nsub):
                    eng = "act" if func is not None else alt_evac(s)
                    if eng == "act":
                        P.op("act", lambda e, s=s: e.activation(out=sg[:, s, 0:ntok], in_=pbs[s].t[:, 0:ntok],
                                                                 func=(func or AF.Copy),
                                                                 scale=(scale if scale is not None else 1.0)),
                             reads=[pbs[s].b], writes=[sg.b])
                    else:
                        P.op("dve", lambda e, s=s: e.tensor_scalar(out=sg[:, s, 0:ntok], in0=pbs[s].t[:, 0:ntok],
                                                                    scalar1=(scale if scale is not None else 1.0),
                                                                    scalar2=None, op0=ALU.mult),
                             reads=[pbs[s].b], writes=[sg.b])
                return sg

            def proj_group(src_ap, t0, ntile, kind, do_norm=True, nxt=None):
                ntok = ntile * 128
                if do_norm:
                    norm_tiles(src_ap, t0 * 128, ntile, xb, xn, ss, rstd)
                if 'normonly' in skip:
                    return
                transpose_to_hT(xn, ntile, hT, nwcol, pbanks, 0)
                if 'tronly' in skip:
                    return
                if nxt is not None:
                    norm_tiles(nxt[0], nxt[1] * 128, nxt[2], xb, xn, ss, rstd)
                full = kind == "full"
                if full:
                    pieces = []
                    for j in range(ntile):
                        for (sq, k0, tk0, n) in key_dst(t0 + j):
                            pieces.append((sq, k0, j * 128 + tk0, n))
                    qc0 = t0 * 128
                else:
                    pieces = [(0, (t0 + j) * 128, j * 128, 128) for j in range(ntile)]
                merged = []
                for pc in pieces:
                    if merged and merged[-1][0] == pc[0] and merged[-1][1] + merged[-1][3] == pc[1] \
                            and merged[-1][2] + merged[-1][3] == pc[2]:
                        m = merged[-1]
                        merged[-1] = (m[0], m[1], m[2], m[3] + pc[3])
                    else:
                        merged.append(pc)

                if full:
                    for blk in (range(8) if 'noq' not in skip else []):
                        Wt = nextW()
                        load_w(Wt, w_in, C_Q + blk * 512, 512)
                        pbs = banks4()
                        fm_mm(Wt, 512, hT, ntok, pbs)
                        sg = evac_fm_to(pbs, 4, ntok, None, scale=128.0 ** -0.5)
                        dst = QT[blk * 512:(blk + 1) * 512, qc0:qc0 + ntok].rearrange("(s p) t -> p s t", p=128)
                        P.op("sp", lambda e, sg=sg, dst=dst: [e.dma_start(out=dst, in_=sg[:, :, 0:ntok])],
                             reads=[sg.b], dma=sg.b)
                for blk in (range(2) if 'nok' not in skip else []):
                    Wt = nextW()
                    load_w(Wt, w_in, C_K + blk * 512, 512)
                    pbs = banks4()
                    fm_mm(Wt, 512, hT, ntok, pbs)
                    sg = evac_fm_to(pbs, 4, ntok, None)
                    dl = []
                    for (sq, k0, tk0, n) in merged:
                        dl.append((KT[sq, blk * 512:(blk + 1) * 512, k0:k0 + n].rearrange("(s p) t -> p s t", p=128),
                                   tk0, n))
                    P.op("sp", lambda e, sg=sg, dl=dl: [e.dma_start(out=d, in_=sg[:, :, a:a + n]) for (d, a, n) in dl],
                         reads=[sg.b], dma=sg.b, ndma=len(dl))
                    if full:
                        pbs = banks4()
                        tm_mm(Wt, 512, hT, ntile, pbs)
                        sg = stm[cnt["tm"] % 2]
                        cnt["tm"] += 1
                        for j in range(ntile):
                            eng = alt_evac(j)
                            if eng == "act":
                                P.op("act", lambda e, j=j, sg=sg, pbs=pbs: e.activation(out=sg[:, j, :], in_=pbs[j].t[:, :], func=AF.Copy),
                                     reads=[pbs[j].b], writes=[sg.b])
                            else:
                                P.op("dve", lambda e, j=j, sg=sg, pbs=pbs: e.tensor_copy(out=sg[:, j, :], in_=pbs[j].t[:, :]),
                                     reads=[pbs[j].b], writes=[sg.b])
                        dst = k_o[t0 * 128:t0 * 128 + ntok, blk * 512:(blk + 1) * 512].rearrange("(j p) c -> p j c", p=128)
                        P.op("sp", lambda e, sg=sg, dst=dst: [e.dma_start(out=dst, in_=sg[:, 0:ntile, :])],
                             reads=[sg.b], dma=sg.b)
                for blk in (range(2) if 'nov' not in skip else []):
                    Wt = nextW()
                    load_w(Wt, w_in, C_V + blk * 512, 512)
                    pbs = banks4()
                    tm_mm(Wt, 512, hT, ntile, pbs)
                    sv = svb[cnt["vb"] % 2]
                    cnt["vb"] += 1
                    if full:
                        sg = stm[cnt["tm"] % 2]
                        cnt["tm"] += 1
                    for j in range(ntile):
                        if full:
                            P.op("dve", lambda e, j=j, sg=sg, pbs=pbs: e.tensor_copy(out=sg[:, j, :], in_=pbs[j].t[:, :]),
                                 reads=[pbs[j].b], writes=[sg.b])
                            P.op("act", lambda e, j=j, sv=sv, sg=sg: e.activation(out=sv[:, j, :], in_=sg[:, j, :], func=AF.Copy),
                                 reads=[sg.b], writes=[sv.b])
                        else:
                            P.op("act", lambda e, j=j, sv=sv, pbs=pbs: e.activation(out=sv[:, j, :], in_=pbs[j].t[:, :], func=AF.Copy),
                                 reads=[pbs[j].b], writes=[sv.b])
                    if full:
                        dst = v_o[t0 * 128:t0 * 128 + ntok, blk * 512:(blk + 1) * 512].rearrange("(j p) c -> p j c", p=128)
                        P.op("sp", lambda e, sg=sg, dst=dst: [e.dma_start(out=dst, in_=sg[:, 0:ntile, :])],
                             reads=[sg.b], dma=sg.b)
                    dl = []
                    for j in range(ntile):
                        if full:
                            pcs = key_dst(t0 + j)
                        else:
                            pcs = [(0, (t0 + j) * 128, 0, 128)]
                        for (sq, k0, tk0, n) in pcs:
                            c = k0 // 128
                            p0 = k0 % 128
                            d = VB[sq, blk * 4:(blk + 1) * 4, p0:p0 + n, c, :].rearrange("g p d -> p g d")
                            dl.append((d, j, tk0, n))
                    P.op("sp", lambda e, sv=sv, dl=dl: [
                        e.dma_start(out=d, in_=sv[tk0:tk0 + n, j, :].rearrange("p (g d) -> p g d", g=4))
                        for (d, j, tk0, n) in dl], reads=[sv.b], dma=sv.b, ndma=len(dl))
                if full:
                    for blk in (range(4) if 'noqi' not in skip else []):
                        Wt = nextW()
                        load_w(Wt, w_in, C_QI + blk * 512, 512)
                        pbs = banks4()
                        fm_mm(Wt, 512, hT, ntok, pbs)
                        sg = evac_fm_to(pbs, 4, ntok, None)
                        dst = QIT[blk * 512:(blk + 1) * 512, qc0:qc0 + ntok].rearrange("(s p) t -> p s t", p=128)
                        P.op("sp", lambda e, sg=sg, dst=dst: [e.dma_start(out=dst, in_=sg[:, :, 0:ntok])],
                             reads=[sg.b], dma=sg.b)
                if 'noki' in skip:
                    return
                Wt = nextW()
                load_w(Wt, w_in, C_KI, 128)
                pbs = banks4()
                fm_mm(Wt, 128, hT, ntok, pbs[0:1])
                sg = sfm[cnt["fm"] % 2]
                cnt["fm"] += 1
                P.op("act", lambda e, sg=sg, pbs=pbs: e.activation(out=sg[0:64, 0, 0:ntok], in_=pbs[0].t[0:64, 0:ntok], func=AF.Copy),
                     reads=[pbs[0].b], writes=[sg.b])
                dl = [(KIT[sq, :, k0:k0 + n], tk0, n) for (sq, k0, tk0, n) in merged]
                P.op("sp", lambda e, sg=sg, dl=dl: [e.dma_start(out=d, in_=sg[0:64, 0, a:a + n]) for (d, a, n) in dl],
                     reads=[sg.b], dma=sg.b, ndma=len(dl))
                if full:
                    pbs = banks4()
                    tm_mm(Wt, 128, hT, ntile, pbs)
                    sg = stm[cnt["tm"] % 2]
                    cnt["tm"] += 1
                    for j in range(ntile):
                        P.op("dve", lambda e, j=j, sg=sg, pbs=pbs: e.tensor_copy(out=sg[:, j, 0:64], in_=pbs[j].t[:, 0:64]),
                             reads=[pbs[j].b], writes=[sg.b])
                        P.op("dve", lambda e, j=j, sg=sg, pbs=pbs: e.tensor_scalar(out=sg[:, j, 64:96], in0=pbs[j].t[:, 64:96],
                                                                                    scalar1=1.0 / (8.0 * math.sqrt(32.0)), scalar2=None,
                                                                                    op0=ALU.mult),
                             reads=[pbs[j].b], writes=[sg.b])
                    d1 = ki_o[t0 * 128:t0 * 128 + ntok, :].rearrange("(j p) c -> p j c", p=128)
                    d2 = WI[t0 * 128:t0 * 128 + ntok, :].rearrange("(j p) c -> p j c", p=128)
                    P.op("sp", lambda e, sg=sg, d1=d1, d2=d2: [e.dma_start(out=d1, in_=sg[:, 0:ntile, 0:64]),
                                                              e.dma_start(out=d2, in_=sg[:, 0:ntile, 64:96])],
                         reads=[sg.b], dma=sg.b, ndma=2)
                    for blk in (range(8) if 'nogate' not in skip else []):
                        Wt = nextW()
                        load_w(Wt, w_in, C_G + blk * 512, 512)
                        pbs = banks4()
                        fm_mm(Wt, 512, hT, ntok, pbs)
                        sg = evac_fm_to(pbs, 4, ntok, None, func=AF.Silu)
                        dst = SGT[blk * 512:(blk + 1) * 512, qc0:qc0 + ntok].rearrange("(s p) t -> p s t", p=128)
                        P.op("sp", lambda e, sg=sg, dst=dst: [e.dma_start(out=dst, in_=sg[:, :, 0:ntok])],
                             reads=[sg.b], dma=sg.b)

            allg = [(xkv, t0, n, "kv") for (t0, n) in kv_groups] + [(xf, t0, n, "full") for (t0, n) in full_groups]
            for gi_, (src_, t0, n, kind_) in enumerate(allg):
                nxt_ = allg[gi_ + 1][0:3] if gi_ + 1 < len(allg) else None
                proj_group(src_, t0, n, kind_, do_norm=(gi_ == 0), nxt=nxt_)

            vdma = P.buf("vdma")
            ckt = Wts[0]
            ckit = Wts[1]
            for sb in (range(4) if 'S' not in skip else []):
                for g in range(8):
                    src = cv[sb, :, g * 128:(g + 1) * 128].rearrange("(c p) d -> p c d", p=128)
                    dst = VB[1 + sb, g, :, 0:32, :]
                    P.op("pool", lambda e, src=src, dst=dst: [e.dma_start(out=dst, in_=src)], dma=vdma)
                srck = cki[sb, :, :].rearrange("(c p) d -> p c d", p=128)
                P.op("pool", lambda e, srck=srck: [e.dma_start(out=ckit[:, 0:32, 0:64], in_=srck)], writes=[ckit.b], dma=ckit.b)
                for c8 in range(4):
                    pb = pbanks[c8 % 8]
                    pbv = pb.t[:, :].bitcast(BF16)

                    def em(e, c8=c8, pbv=pbv):
                        r = None
                        for i in range(8):
                            r = e.transpose(out=pbv[0:64, i * 128:(i + 1) * 128], in_=ckit[:, c8 * 8 + i, 0:64],
                                            identity=ident[:, :])
                        return r
                    P.op("pe", em, reads=[ckit.b, ident.b], writes=[pb.b])
                    sg = sfm[cnt["fm"] % 2]
                    cnt["fm"] += 1
                    P.op("act", lambda e, sg=sg, pbv=pbv: e.activation(out=sg[0:64, :, :].rearrange("p a b -> p (a b)")[:, 0:1024],
                                                                        in_=pbv[0:64, :], func=AF.Copy),
                         reads=[pb.b], writes=[sg.b])
                    dst = KIT[1 + sb, :, c8 * 1024:(c8 + 1) * 1024]
                    P.op("sp", lambda e, sg=sg, dst=dst: [e.dma_start(out=dst, in_=sg[0:64, :, :].rearrange("p a b -> p (a b)")[:, 0:1024])],
                         reads=[sg.b], dma=sg.b)
                for c4 in range(8):
                    srcc = ck[sb, c4 * 512:(c4 + 1) * 512, :].rearrange("(c p) n -> p c n", p=128)
                    ckv = ckt[:, :, :].rearrange("p a b -> p (a b)")
                    P.op("pool", lambda e, srcc=srcc, ckv=ckv: [e.dma_start(
                        out=ckv[:, 0:4096].rearrange("p (c n) -> p c n", c=4), in_=srcc)], writes=[ckt.b], dma=ckt.b)
                    for half in range(2):
                        sg = sfm[cnt["fm"] % 2]
                        cnt["fm"] += 1
                        for gg in range(4):
                            g = half * 4 + gg
                            pb = pbanks[(c4 * 8 + g) % 8]
                            pbv = pb.t[:, :].bitcast(BF16)

                            def em(e, g=g, pbv=pbv, ckv=ckv):
                                r = None
                                for c in range(4):
                                    r = e.transpose(out=pbv[:, c * 128:(c + 1) * 128],
                                                    in_=ckv[:, c * 1024 + g * 128:c * 1024 + (g + 1) * 128],
                                                    identity=ident[:, :])
                                return r
                            P.op("pe", em, reads=[ckt.b, ident.b], writes=[pb.b])
                            eng = alt_evac(gg)
                            if eng == "act":
                                P.op("act", lambda e, sg=sg, gg=gg, pbv=pbv: e.activation(out=sg[:, gg, :], in_=pbv[:, 0:512], func=AF.Copy),
                                     reads=[pb.b], writes=[sg.b])
                            else:
                                P.op("dve", lambda e, sg=sg, gg=gg, pbv=pbv: e.tensor_copy(out=sg[:, gg, :], in_=pbv[:, 0:512]),
                                     reads=[pb.b], writes=[sg.b])
                        dst = KT[1 + sb, half * 512:(half + 1) * 512, c4 * 512:(c4 + 1) * 512].rearrange("(g p) k -> p g k", p=128)
                        P.op("sp", lambda e, sg=sg, dst=dst: [e.dma_start(out=dst, in_=sg[:, :, :])], reads=[sg.b], dma=sg.b)
            P.drain("sp")
            P.emit_block()
            P.release([t.b for t in [xb, xn, hT] + Wts + sfm + stm + svb + [ident, nwcol]] + [vdma])
            if stop == 1:
                return nc

        with ExitStack() as st:
            SC = T(P, st, "SC", [128, NKEY], F32)
            MBs = [T(P, st, "MB%d" % i, [128, NKEY], BF16) for i in range(2)]
            KBI = T(P, st, "KBI", [128, NKEY], BF16)
            junk = T(P, st, "junk", [128, NKEY], BF16)
            QIs = [T(P, st, "QI%d" % i, [64, 32, 128], BF16) for i in range(2)]
            KIs = [T(P, st, "KI%d" % i, [64, NKEY], BF16) for i in range(2)]
            WIs = [T(P, st, "WIs%d" % i, [128, 32], F32) for i in range(2)]
            QTs = [T(P, st, "QTt%d" % i, [128, 32, 128], BF16) for i in range(2)]
            KTg = [T(P, st, "KTg%d" % i, [128, NKEY], BF16) for i in range(2)]
            Vg = [T(P, st, "Vg%d" % i, [128, 33, 129], BF16) for i in range(2)]
            rls = [T(P, st, "rl%d" % i, [128, 512], BF16) for i in range(4)]
            pTs = [T(P, st, "pT%d" % i, [128, 512], BF16) for i in range(3)]
            Ot = T(P, st, "Ot", [128, D], BF16)
            SGt = T(P, st, "SGt", [128, 32, 128], BF16)
            OGs = T(P, st, "OGs", [128, 32, 128], BF16)
            BT = T(P, st, "BT", [128, 2, 32, 128], BF16)
            ident = T(P, st, "ident2", [128, 128], BF16)
            i4 = T(P, st, "i4", [128, 512], BF16)
            i4s = T(P, st, "i4s", [32, 128], BF16)
            oh = T(P, st, "oh", [32, 384], F32)
            rb = T(P, st, "rb", [32, 32], F32)
            rb15 = T(P, st, "rb15", [32, 32], F32)
            adm = T(P, st, "adm", [128, 128], F32)
            sm = T(P, st, "sm", [128, 16], F32)
            den = T(P, st, "den", [128, 8], F32)
            pix = [T(P, st, "pix%d" % i, [128, 512], F32, psum=True) for i in range(2)]
            plg = [T(P, st, "plg%d" % i, [128, 512], F32, psum=True) for i in range(2)]
            poa = [T(P, st, "poa%d" % i, [128, 512], F32, psum=True) for i in range(2)]
            ptr = [T(P, st, "ptr%d" % i, [128, 512], F32, psum=True) for i in range(2)]

            for (t_, src) in ((ident, c_ident), (i4, c_i4), (i4s, c_i4s), (oh, c_oh), (rb, rel_bias), (adm, c_adm)):
                P.op("sp", lambda e, t_=t_, src=src: [e.dma_start(out=t_[:, :], in_=src[:, :])], writes=[t_.b], dma=t_.b)
            P.op("sp", lambda e: [e.dma_start(out=rb15[:, :], in_=rel_bias[15, :].partition_broadcast(32))],
                 writes=[rb15.b], dma=rb15.b)
            P.op("sp", lambda e: [e.dma_start(out=KBI[:, :], in_=c_kbias.partition_broadcast(128))],
                 writes=[KBI.b], dma=KBI.b)
            P.op("dve", lambda e: e.tensor_tensor(out=rb[:, :], in0=rb[:, :], in1=rb15[:, :], op=ALU.subtract),
                 reads=[rb15.b, rb.b], writes=[rb.b])
            P.op("dve", lambda e: e.memset(sm[:, 7:8], 0.5), writes=[sm.b])
            for v_ in Vg:
                P.op("dve", lambda e, v_=v_: e.memset(v_[:, :, 128:129], 1.0), writes=[v_.b])
            for dl in range(2):
                for q16 in range(8):
                    pb = ptr[(dl * 8 + q16) % 2]

                    def em(e, dl=dl, q16=q16, pb=pb):
                        r = None
                        for i in range(16):
                            ql = q16 * 16 + i
                            start = (-128 if dl == 1 else 0) - ql + 255
                            r = e.matmul(out=pb.t[:, i * 32:(i + 1) * 32], lhsT=oh[:, start:start + 128], rhs=rb[:, :],
                                         start=True, stop=True)
                        return r
                    P.op("pe", em, reads=[oh.b, rb.b], writes=[pb.b])
                    P.op("act", lambda e, dl=dl, q16=q16, pb=pb: e.activation(
                        out=BT[:, dl, :, q16 * 16:(q16 + 1) * 16],
                        in_=pb.t[:, :].rearrange("p (q h) -> p h q", h=32), func=AF.Copy),
                        reads=[pb.b], writes=[BT.b])

            tiles = []
            for ft in range(17):
                tiles.append(dict(seq=0, nq=128, qc=ft * 128, nch=16 + ft, last=128, prompt=True))
            for sb in range(4):
                tiles.append(dict(seq=1 + sb, nq=32, qc=17 * 128 + sb * 32, nch=33, last=32, prompt=False))
            cnt2 = {"rl": 0, "pix": 0, "pT": 0, "plg": 0, "kv": 0}
            def indexer(ti):
                tl = tiles[ti]
                nq, qc, nch, seq = tl["nq"], tl["qc"], tl["nch"], tl["seq"]
                nkeys = (nch - 1) * 128 + tl["last"]
                QI, KI, WIt, QTt, MB = QIs[ti % 2], KIs[ti % 2], WIs[ti % 2], QTs[ti % 2], MBs[ti % 2]
                P.op("sp", lambda e, QI=QI, qc=qc, nq=nq: [e.dma_start(
                    out=QI[:, :, 0:nq], in_=QIT[:, qc:qc + nq].rearrange("(h d) q -> d h q", d=64))],
                    writes=[QI.b], dma=QI.b)
                P.op("sp", lambda e, KI=KI, seq=seq, nkeys=nkeys: [e.dma_start(out=KI[:, 0:nkeys], in_=KIT[seq, :, 0:nkeys])],
                     writes=[KI.b], dma=KI.b)
                P.op("sp", lambda e, WIt=WIt, qc=qc, nq=nq: [e.dma_start(out=WIt[0:nq, :], in_=WI[qc:qc + nq, :])],
                     writes=[WIt.b], dma=WIt.b)
                P.op("sp", lambda e, QTt=QTt, qc=qc, nq=nq: [e.dma_start(
                    out=QTt[:, :, 0:nq], in_=QT[:, qc:qc + nq].rearrange("(h d) q -> d h q", d=128))],
                    writes=[QTt.b], dma=QTt.b)
                nblk = (nkeys + 511) // 512
                for kb in range(nblk):
                    k0 = kb * 512
                    bs = min(512, nkeys - k0)
                    for h in range(32):
                        pb = pix[cnt2["pix"] % 2]
                        cnt2["pix"] += 1
                        rl = rls[cnt2["rl"] % 4]
                        cnt2["rl"] += 1
                        P.op("pe", lambda e, pb=pb, QI=QI, KI=KI, h=h, k0=k0, bs=bs, nq=nq: e.matmul(
                            out=pb.t[0:nq, 0:bs], lhsT=QI[:, h, 0:nq], rhs=KI[:, k0:k0 + bs], start=True, stop=True),
                            reads=[QI.b, KI.b], writes=[pb.b])
                        P.op("act", lambda e, pb=pb, rl=rl, bs=bs, nq=nq: e.activation(
                            out=rl[0:nq, 0:bs], in_=pb.t[0:nq, 0:bs], func=AF.Relu), reads=[pb.b], writes=[rl.b])
                        if h == 0:
                            P.op("dve", lambda e, rl=rl, WIt=WIt, k0=k0, bs=bs, nq=nq: e.tensor_scalar(
                                out=SC[0:nq, k0:k0 + bs], in0=rl[0:nq, 0:bs], scalar1=WIt[0:nq, 0:1], scalar2=None,
                                op0=ALU.mult), reads=[rl.b, WIt.b], writes=[SC.b])
                        else:
                            P.op("dve", lambda e, rl=rl, WIt=WIt, h=h, k0=k0, bs=bs, nq=nq: e.scalar_tensor_tensor(
                                out=SC[0:nq, k0:k0 + bs], in0=rl[0:nq, 0:bs], scalar=WIt[0:nq, h:h + 1],
                                in1=SC[0:nq, k0:k0 + bs], op0=ALU.mult, op1=ALU.add),
                                reads=[rl.b, WIt.b, SC.b], writes=[SC.b])
                P.op("dve", lambda e, nq=nq, nkeys=nkeys: e.tensor_reduce(out=sm[0:nq, 1:2], in_=SC[0:nq, 0:nkeys],
                                                                          axis=AX.X, op=ALU.max),
                     reads=[SC.b], writes=[sm.b])
                P.op("dve", lambda e, nq=nq, nkeys=nkeys: e.tensor_reduce(out=sm[0:nq, 0:1], in_=SC[0:nq, 0:nkeys],
                                                                          axis=AX.X, op=ALU.min),
                     reads=[SC.b], writes=[sm.b])
                P.op("dve", lambda e, nq=nq: e.tensor_scalar(out=sm[0:nq, 1:2], in0=sm[0:nq, 1:2], scalar1=1.0, scalar2=None,
                                                              op0=ALU.add), reads=[sm.b], writes=[sm.b])
                P.op("dve", lambda e, nq=nq: e.tensor_scalar(out=sm[0:nq, 0:1], in0=sm[0:nq, 0:1], scalar1=-1.0, scalar2=None,
                                                              op0=ALU.add), reads=[sm.b], writes=[sm.b])
                if tl["prompt"]:
                    P.op("dve", lambda e, nkeys=nkeys: e.tensor_tensor(out=SC[:, 0:nkeys], in0=SC[:, 0:nkeys],
                                                                        in1=KBI[:, 0:nkeys], op=ALU.add),
                         reads=[SC.b, KBI.b], writes=[SC.b])
                    P.op("dve", lambda e, nkeys=nkeys: e.tensor_tensor(out=SC[:, nkeys - 128:nkeys], in0=SC[:, nkeys - 128:nkeys],
                                                                        in1=adm[:, :], op=ALU.add),
                         reads=[SC.b, adm.b], writes=[SC.b])
                for it in range(NIT):
                    P.op("dve", lambda e, nq=nq: e.scalar_tensor_tensor(out=sm[0:nq, 2:3], in0=sm[0:nq, 0:1], scalar=sm[0:nq, 1:2],
                                                                         in1=sm[0:nq, 7:8], op0=ALU.add, op1=ALU.mult),
                         reads=[sm.b], writes=[sm.b])
                    P.op("dve", lambda e, nq=nq, nkeys=nkeys: e.tensor_scalar(out=junk[0:nq, 0:nkeys], in0=SC[0:nq, 0:nkeys],
                                                                               scalar1=sm[0:nq, 2:3], scalar2=None, op0=ALU.is_ge,
                                                                               op1=ALU.add, accum_out=sm[0:nq, 3:4]),
                         reads=[sm.b, SC.b], writes=[sm.b, junk.b])
                    P.op("dve", lambda e, nq=nq: e.tensor_single_scalar(out=sm[0:nq, 4:5], in_=sm[0:nq, 3:4], scalar=255.5, op=ALU.is_ge),
                         reads=[sm.b], writes=[sm.b])
                    P.op("dve", lambda e, nq=nq: e.tensor_tensor(out=sm[0:nq, 5:6], in0=sm[0:nq, 2:3], in1=sm[0:nq, 0:1], op=ALU.subtract),
                         reads=[sm.b], writes=[sm.b])
                    P.op("dve", lambda e, nq=nq: e.tensor_tensor(out=sm[0:nq, 6:7], in0=sm[0:nq, 1:2], in1=sm[0:nq, 2:3], op=ALU.subtract),
                         reads=[sm.b], writes=[sm.b])
                    P.op("dve", lambda e, nq=nq: e.scalar_tensor_tensor(out=sm[0:nq, 0:1], in0=sm[0:nq, 5:6], scalar=sm[0:nq, 4:5],
                                                                         in1=sm[0:nq, 0:1], op0=ALU.mult, op1=ALU.add),
                         reads=[sm.b], writes=[sm.b])
                    P.op("dve", lambda e, nq=nq: e.scalar_tensor_tensor(out=sm[0:nq, 1:2], in0=sm[0:nq, 6:7], scalar=sm[0:nq, 4:5],
                                                                         in1=sm[0:nq, 2:3], op0=ALU.mult, op1=ALU.add),
                         reads=[sm.b], writes=[sm.b])
                P.op("dve", lambda e, nq=nq, nkeys=nkeys, MB=MB: e.tensor_scalar(out=MB[0:nq, 0:nkeys], in0=SC[0:nq, 0:nkeys],
                                                                                  scalar1=sm[0:nq, 0:1], scalar2=NEG, op0=ALU.is_lt,
                                                                                  op1=ALU.mult),
                     reads=[sm.b, SC.b], writes=[MB.b])
            def attend(ti):
                tl = tiles[ti]
                nq, qc, nch, seq = tl["nq"], tl["qc"], tl["nch"], tl["seq"]
                nkeys = (nch - 1) * 128 + tl["last"]
                QI, KI, WIt, QTt, MB = QIs[ti % 2], KIs[ti % 2], WIs[ti % 2], QTs[ti % 2], MBs[ti % 2]
                P.op("sp", lambda e, qc=qc, nq=nq: [e.dma_start(out=SGt[:, :, 0:nq],
                                                               in_=SGT[:, qc:qc + nq].rearrange("(kc p) q -> p kc q", p=128))],
                     writes=[SGt.b], dma=SGt.b)
                for g in range(8):
                    Kt, Vt = KTg[cnt2["kv"] % 2], Vg[cnt2["kv"] % 2]
                    cnt2["kv"] += 1
                    P.op("sp", lambda e, Kt=Kt, seq=seq, g=g, nkeys=nkeys: [e.dma_start(
                        out=Kt[:, 0:nkeys], in_=KT[seq, g * 128:(g + 1) * 128, 0:nkeys])], writes=[Kt.b], dma=Kt.b)
                    if tl["last"] == 128:
                        P.op("sp", lambda e, Vt=Vt, seq=seq, g=g, nch=nch: [e.dma_start(
                            out=Vt[:, 0:nch, 0:128], in_=VB[seq, g, :, 0:nch, :])], writes=[Vt.b], dma=Vt.b)
                    else:
                        P.op("sp", lambda e, Vt=Vt, seq=seq, g=g, nch=nch: [
                            e.dma_start(out=Vt[:, 0:nch - 1, 0:128], in_=VB[seq, g, :, 0:nch - 1, :]),
                            e.dma_start(out=Vt[0:32, nch - 1, 0:128], in_=VB[seq, g, 0:32, nch - 1, :])],
                            writes=[Vt.b], dma=Vt.b, ndma=2)
                    def issue_qk(c, Kt=Kt, g=g):
                            ksz = tl["last"] if c == nch - 1 else 128
                            near = c >= nch - 2
                            dl = 0 if c == nch - 1 else 1
                            lg = plg[cnt2["plg"] % 2]
                            cnt2["plg"] += 1
                            pT = pTs[cnt2["pT"] % 3]
                            cnt2["pT"] += 1
                            w4 = 4 * nq

                            def em(e, lg=lg, Kt=Kt, QTt=QTt, MB=MB, c=c, ksz=ksz, near=near, dl=dl, g=g, nq=nq, w4=w4):
                                e.matmul(out=lg.t[0:ksz, 0:w4], lhsT=Kt[:, c * 128:c * 128 + ksz],
                                         rhs=QTt[:, 4 * g:4 * g + 4, 0:nq], start=True, stop=False)
                                r = e.matmul(out=lg.t[0:ksz, 0:w4], lhsT=MB[0:nq, c * 128:c * 128 + ksz],
                                             rhs=(i4[:, :] if nq == 128 else i4s[:, :]), start=False, stop=(not near))
                                if near:
                                    r = e.matmul(out=lg.t[0:ksz, 0:w4], lhsT=ident[0:ksz, 0:ksz],
                                                 rhs=BT[0:ksz, dl, 4 * g:4 * g + 4, 0:nq], start=False, stop=True)
                                return r
                            P.op("pe", em, reads=[Kt.b, QTt.b, MB.b, i4.b, i4s.b, ident.b, BT.b], writes=[lg.b])
                            return (lg, pT, ksz, w4)

                    def issue_exp_pv(c, lg, pT, ksz, w4, Vt=Vt):
                            P.op("act", lambda e, lg=lg, pT=pT, ksz=ksz, w4=w4: e.activation(
                                out=pT[0:ksz, 0:w4], in_=lg.t[0:ksz, 0:w4], func=AF.Exp), reads=[lg.b], writes=[pT.b])

                            def em2(e, pT=pT, Vt=Vt, c=c, ksz=ksz, nq=nq, nch=nch):
                                r = None
                                for j in range(4):
                                    r = e.matmul(out=poa[j // 2].t[0:nq, (j % 2) * 129:(j % 2) * 129 + 129],
                                                 lhsT=pT[0:ksz, j * nq:(j + 1) * nq], rhs=Vt[0:ksz, c, 0:129],
                                                 start=(c == 0 and j % 2 == 0), stop=(c == nch - 1), skip_group_check=True)
                                return r
                            P.op("pe", em2, reads=[pT.b, Vt.b], writes=[poa[0].b, poa[1].b])

                    st_ = issue_qk(0)
                    for c in range(nch):
                        nx_ = issue_qk(c + 1) if c + 1 < nch else None
                        issue_exp_pv(c, *st_)
                        st_ = nx_
                    for j2 in range(2):
                        P.op("dve", lambda e, j2=j2, nq=nq: e.tensor_scalar(
                            out=den[0:nq, j2 * 2:j2 * 2 + 2], in0=poa[j2].t[0:nq, 0:258].rearrange("p (j d) -> p j d", d=129)[:, :, 128],
                            scalar1=1e-30, scalar2=None, op0=ALU.max), reads=[poa[j2].b], writes=[den.b])
                    P.op("dve", lambda e, nq=nq: e.reciprocal(out=den[0:nq, 0:4], in_=den[0:nq, 0:4]), reads=[den.b], writes=[den.b])
                    for j in range(4):
                        eng = "act" if j // 2 == 0 else "dve"
                        col = (4 * g + j) * 128
                        if eng == "act":
                            P.op("act", lambda e, j=j, col=col, nq=nq: e.activation(
                                out=Ot[0:nq, col:col + 128], in_=poa[j // 2].t[0:nq, (j % 2) * 129:(j % 2) * 129 + 128],
                                func=AF.Copy, scale=den[0:nq, j:j + 1]), reads=[poa[j // 2].b, den.b], writes=[Ot.b])
                        else:
                            P.op("dve", lambda e, j=j, col=col, nq=nq: e.tensor_scalar(
                                out=Ot[0:nq, col:col + 128], in0=poa[j // 2].t[0:nq, (j % 2) * 129:(j % 2) * 129 + 128],
                                scalar1=den[0:nq, j:j + 1], scalar2=None, op0=ALU.mult), reads=[poa[j // 2].b, den.b], writes=[Ot.b])
                for k8 in range(4):
                    pb = ptr[k8 % 2]
                    pbv = pb.t[:, :].bitcast(BF16)

                    def em(e, k8=k8, pbv=pbv, nq=nq):
                        r = None
                        for i in range(8):
                            kc = k8 * 8 + i
                            r = e.transpose(out=pbv[:, i * nq:(i + 1) * nq], in_=Ot[0:nq, kc * 128:(kc + 1) * 128],
                                            identity=ident[0:nq, 0:nq])
                        return r
                    P.op("pe", em, reads=[Ot.b, ident.b], writes=[pb.b])
                    P.op("dve", lambda e, k8=k8, pbv=pbv, nq=nq: e.tensor_tensor(
                        out=OGs[:, k8 * 8:(k8 + 1) * 8, 0:nq], in0=pbv[:, 0:8 * nq].rearrange("p (i q) -> p i q", q=nq),
                        in1=SGt[:, k8 * 8:(k8 + 1) * 8, 0:nq], op=ALU.mult), reads=[pb.b, SGt.b], writes=[OGs.b])
                P.op("sp", lambda e, qc=qc, nq=nq: [e.dma_start(out=OGT[:, qc:qc + nq].rearrange("(kc p) q -> p kc q", p=128),
                                                               in_=OGs[:, :, 0:nq])], reads=[OGs.b], dma=OGs.b)
            indexer(0)
            for ti in range(len(tiles)):
                if ti + 1 < len(tiles):
                    indexer(ti + 1)
                attend(ti)
            P.drain("sp")
            P.emit_block()
            P.release([t.b for t in [SC, KBI, junk, Ot, SGt, OGs, BT, ident, i4, i4s, oh, rb, rb15, adm] + MBs + QIs + KIs
                       + WIs + QTs + KTg + Vg + rls + pTs])
            if stop == 2:
                return nc

        with ExitStack() as st:
            OGt = T(P, st, "OGt", [128, KC, 512], BF16)
            Wts = [T(P, st, "w3t%d" % i, [128, KC, 512], BF16) for i in range(2)]
            xs = [T(P, st, "xs%d" % i, [128, 4, 512], F32) for i in range(2)]
            so = [T(P, st, "so%d" % i, [128, 4, 512], F32) for i in range(2)]
            pbanks = [T(P, st, "p3b%d" % i, [128, 512], F32, psum=True) for i in range(8)]
            c3d = {'n': 0}
            def grp_fn0(t0, ntile):
                ntok = ntile * 128
                P.op("sp", lambda e, t0=t0, ntok=ntok: [e.dma_start(
                    out=OGt[:, :, 0:ntok], in_=OGT[:, t0 * 128:t0 * 128 + ntok].rearrange("(kc p) q -> p kc q", p=128))],
                    writes=[OGt.b], dma=OGt.b)
                for blk in range(8):
                    Wt = Wts[c3d['n'] % 2]
                    xsb = xs[c3d['n'] % 2]
                    sob = so[c3d['n'] % 2]
                    pbs = pbanks[(c3d['n'] % 2) * 4:(c3d['n'] % 2) * 4 + 4]
                    c3d['n'] += 1
                    load_w(Wt, w_out, blk * 512, 512)
                    P.op("sp", lambda e, xsb=xsb, t0=t0, ntok=ntok, ntile=ntile, blk=blk: [e.dma_start(
                        out=xsb[:, 0:ntile, :], in_=xf[t0 * 128:t0 * 128 + ntok, blk * 512:(blk + 1) * 512].rearrange(
                            "(j p) c -> p j c", p=128))], writes=[xsb.b], dma=xsb.b)
                    tm_mm(Wt, 512, OGt, ntile, pbs)
                    for j in range(ntile):
                        P.op("dve", lambda e, j=j, sob=sob, xsb=xsb, pbs=pbs: e.tensor_tensor(
                            out=sob[:, j, :], in0=pbs[j].t[:, :], in1=xsb[:, j, :], op=ALU.add),
                            reads=[pbs[j].b, xsb.b], writes=[sob.b])
                    P.op("sp", lambda e, sob=sob, t0=t0, ntok=ntok, ntile=ntile, blk=blk: [e.dma_start(
                        out=X1[t0 * 128:t0 * 128 + ntok, blk * 512:(blk + 1) * 512].rearrange("(j p) c -> p j c", p=128),
                        in_=sob[:, 0:ntile, :])], reads=[sob.b], dma=sob.b)
            for (t0, ntile) in full_groups:
                grp_fn0(t0, ntile)
            P.drain("sp")
            P.emit_block()
            P.release([t.b for t in [OGt] + Wts + xs + so])
            if stop == 3:
                return nc

        with ExitStack() as st:
            xb = T(P, st, "xb4", [128, D], F32)
            xn = T(P, st, "xn4", [128, 6, D], BF16)
            hT = T(P, st, "hT4", [128, KC, 768], BF16)
            Wts = [T(P, st, "w4t%d" % i, [128, KC, 256], BF16) for i in range(2)]
            UF = T(P, st, "UF", [128, 2, 783], F32)
            T1 = T(P, st, "T1", [128, 2, 783], F32)
            T2 = T(P, st, "T2", [128, 2, 783], F32)
            CR = T(P, st, "CR", [128, KC, 15], F32)
            HS = T(P, st, "HS", [64, D], F32)
            HST = T(P, st, "HST", [128, KC, 4, 15], F32)
            smt = [T(P, st, "smt%d" % i, [128, 2, 768], BF16) for i in range(2)]
            stm = [T(P, st, "stm4%d" % i, [128, 2, 256], F32) for i in range(2)]
            ss = T(P, st, "ss4", [128, 8], F32)
            rstd = T(P, st, "rstd4", [128, 8], F32)
            nwcol = T(P, st, "nwcol4", [128, KC], F32)
            epst = T(P, st, "epst4", [128, 1], F32)
            ident = T(P, st, "ident4", [128, 128], BF16)
            identf = T(P, st, "identf4", [128, 128], F32)
            icn = T(P, st, "icn", [128, 4, 128], F32)
            pbanks = [T(P, st, "p4b%d" % i, [128, 512], F32, psum=True) for i in range(8)]
            c4 = {"w": 0, "blk": 0, "smt": 0, "tm": 0}

            P.op("sp", lambda e: [e.dma_start(out=ident[:, :], in_=c_ident[:, :])], writes=[ident.b], dma=ident.b)
            P.op("sp", lambda e: [e.dma_start(out=identf[:, :], in_=c_identf[:, :])], writes=[identf.b], dma=identf.b)
            P.op("sp", lambda e: [e.dma_start(out=nwcol[:, :], in_=norm_w[1, :, :])],
                 writes=[nwcol.b], dma=nwcol.b)
            P.op("sp", lambda e: [e.dma_start(out=icn[:, :, :].rearrange("p a b -> p (a b)"),
                                              in_=c_invcnt.rearrange("a b -> (a b)").partition_broadcast(128))],
                 writes=[icn.b], dma=icn.b)
            P.op("sp", lambda e: [e.dma_start(out=HS[0:60, :], in_=spool.rearrange("s i d -> (s i) d"))], writes=[HS.b], dma=HS.b)
            P.op("dve", lambda e: e.memset(epst[:, :], EPS), writes=[epst.b])
            P.op("dve", lambda e: e.memset(CR[:, :, :], 0.0), writes=[CR.b])
            for k4 in range(8):
                pb = pbanks[k4 % 8]

                def em(e, k4=k4, pb=pb):
                    r = None
                    for i in range(4):
                        kc = k4 * 4 + i
                        r = e.transpose(out=pb.t[:, i * 60:(i + 1) * 60], in_=HS[0:60, kc * 128:(kc + 1) * 128],
                                        identity=identf[0:60, 0:60])
                    return r
                P.op("pe", em, reads=[HS.b, identf.b], writes=[pb.b])
                P.op("dve", lambda e, k4=k4, pb=pb: e.tensor_copy(
                    out=HST[:, k4 * 4:(k4 + 1) * 4, :, :].rearrange("p k s i -> p k (s i)"),
                    in_=pb.t[:, 0:240].rearrange("p (k x) -> p k x", k=4)), reads=[pb.b], writes=[HST.b])

            def pool_mix(seg_views, w, widx, ntokseg, invc_first):
                V = seg_views
                L = ntokseg
                tot = L + 15
                cur = UF
                bufs = [T1, T2]
                bi = 0
                sh = 1
                lo = 0
                while sh < w:
                    dst = bufs[bi % 2]
                    lo2 = lo + sh
                    P.op("dve", lambda e, cur=cur, dst=dst, lo2=lo2, sh=sh, tot=tot: e.tensor_tensor(
                        out=V(dst, lo2, tot), in0=V(cur, lo2, tot), in1=V(cur, lo2 - sh, tot - sh), op=ALU.add),
                        reads=[cur.b], writes=[dst.b])
                    cur = dst
                    bi += 1
                    lo = lo2
                    sh *= 2
                return cur

            groups6 = [(0, 6), (6, 6), (12, 6)]

            def transpose6(xn_, ntile):
                ntok = ntile * 128
                for kc in range(KC):
                    pb = pbanks[kc % 8]
                    pbv = pb.t[:, :].bitcast(BF16)

                    def em(e, kc=kc, pbv=pbv):
                        r = None
                        for j in range(ntile):
                            r = e.transpose(out=pbv[:, j * 128:(j + 1) * 128], in_=xn_[:, j, kc * 128:(kc + 1) * 128],
                                            identity=ident[:, :])
                        return r
                    P.op("pe", em, reads=[xn_.b, ident.b], writes=[pb.b])
                    if kc % 2 == 0:
                        P.op("act", lambda e, kc=kc, pbv=pbv: e.activation(out=hT[:, kc, 0:ntok], in_=pbv[:, 0:ntok], func=AF.Copy,
                                                                          scale=nwcol[:, kc:kc + 1]),
                             reads=[pb.b, nwcol.b], writes=[hT.b])
                    else:
                        P.op("dve", lambda e, kc=kc, pbv=pbv: e.tensor_scalar(out=hT[:, kc, 0:ntok], in0=pbv[:, 0:ntok],
                                                                             scalar1=nwcol[:, kc:kc + 1], scalar2=None, op0=ALU.mult),
                             reads=[pb.b, nwcol.b], writes=[hT.b])

            def group4a(gi, t0, ntile):
                ntok = ntile * 128
                HW = ntok // 2
                has_sample = (t0 + ntile - 1) == 17
                npt = ntile - (1 if has_sample else 0)
                Lp = npt * 128
                if gi == 0:
                    norm_tiles(X1, t0 * 128, ntile, xb, xn, ss, rstd)
                transpose6(xn, ntile)
                if gi + 1 < len(groups6):
                    norm_tiles(X1, groups6[gi + 1][0] * 128, groups6[gi + 1][1], xb, xn, ss, rstd)
                for blk in range(32):
                    isu = blk < 16
                    Wt = Wts[c4["w"] % 2]
                    c4["w"] += 1
                    load_w(Wt, pw_in, blk * 256, 256)
                    quad = pbanks[(c4["blk"] % 2) * 4:(c4["blk"] % 2) * 4 + 4]
                    c4["blk"] += 1
                    for s in range(2):
                        for th in range(2):
                            pb = quad[s * 2 + th]

                            def em(e, s=s, th=th, pb=pb, Wt=Wt):
                                r = None
                                for kc in range(KC):
                                    r = e.matmul(out=pb.t[:, 0:HW], lhsT=Wt[:, kc, s * 128:(s + 1) * 128],
                                                 rhs=hT[:, kc, th * HW:(th + 1) * HW], start=(kc == 0), stop=(kc == KC - 1))
                                return r
                            P.op("pe", em, reads=[Wt.b, hT.b], writes=[pb.b])
                    sg = smt[c4["smt"] % 2]
                    c4["smt"] += 1
                    if not isu:
                        for s in range(2):
                            for th in range(2):
                                pb = quad[s * 2 + th]
                                P.op("act", lambda e, s=s, th=th, sg=sg, pb=pb: e.activation(
                                    out=sg[:, s, th * HW:(th + 1) * HW], in_=pb.t[:, 0:HW], func=AF.Silu),
                                    reads=[pb.b], writes=[sg.b])
                        r0 = (blk - 16) * 256
                        dst = SG1T[r0:r0 + 256, t0 * 128:t0 * 128 + ntok].rearrange("(s p) t -> p s t", p=128)
                        P.op("sp", lambda e, sg=sg, dst=dst: [e.dma_start(out=dst, in_=sg[:, :, 0:ntok])], reads=[sg.b], dma=sg.b)
                        continue
                    fc0 = blk * 2
                    widx = blk // 4
                    w = POOL_WINDOWS[widx]
                    P.op("dve", lambda e, fc0=fc0: e.tensor_copy(out=UF[:, :, 0:15], in_=CR[:, fc0:fc0 + 2, :]),
                         reads=[CR.b], writes=[UF.b])
                    for s in range(2):
                        for th in range(2):
                            lo_, hi_ = th * HW, min((th + 1) * HW, Lp)
                            if hi_ <= lo_:
                                continue
                            pb = quad[s * 2 + th]
                            P.op("act", lambda e, s=s, pb=pb, lo_=lo_, hi_=hi_: e.activation(
                                out=UF[:, s, 15 + lo_:15 + hi_], in_=pb.t[:, 0:hi_ - lo_], func=AF.Copy),
                                reads=[pb.b], writes=[UF.b])
                    P.op("dve", lambda e, fc0=fc0: e.tensor_copy(out=CR[:, fc0:fc0 + 2, :], in_=UF[:, :, Lp:Lp + 15]),
                         reads=[UF.b], writes=[CR.b])
                    Vp = lambda b, lo, hi: b[:, :, lo:hi]
                    S = pool_mix(Vp, w, widx, Lp, None)
                    if gi == 0:
                        for s in range(2):
                            P.op("dve", lambda e, S=S, widx=widx, s=s: e.tensor_tensor(
                                out=S[:, s, 15 + 128:15 + 256], in0=S[:, s, 15 + 128:15 + 256],
                                in1=icn[:, widx, :], op=ALU.mult), reads=[S.b, icn.b], writes=[S.b])
                        P.op("dve", lambda e, S=S, w=w: e.tensor_scalar(
                            out=S[:, :, 15:15 + 128], in0=S[:, :, 15:15 + 128], scalar1=1.0 / w, scalar2=None, op0=ALU.mult),
                            reads=[S.b], writes=[S.b])
                        P.op("dve", lambda e, S=S, w=w: e.tensor_scalar(
                            out=S[:, :, 15 + 256:15 + Lp], in0=S[:, :, 15 + 256:15 + Lp], scalar1=1.0 / w, scalar2=None,
                            op0=ALU.mult), reads=[S.b], writes=[S.b])
                        P.op("dve", lambda e, S=S, sg=sg: e.tensor_tensor(
                            out=sg[:, :, 0:Lp], in0=S[:, :, 15:15 + Lp], in1=UF[:, :, 15:15 + Lp], op=ALU.subtract),
                            reads=[S.b, UF.b], writes=[sg.b])
                    else:
                        P.op("dve", lambda e, S=S, sg=sg, w=w: e.scalar_tensor_tensor(
                            out=sg[:, :, 0:Lp], in0=S[:, :, 15:15 + Lp], scalar=1.0 / w, in1=UF[:, :, 15:15 + Lp],
                            op0=ALU.mult, op1=ALU.subtract), reads=[S.b, UF.b], writes=[sg.b])
                    if has_sample:
                        o0 = Lp - HW
                        U4 = lambda b, lo, hi: b[:, :, 0:188].rearrange("p s (b x) -> p s b x", x=47)[:, :, :, lo:hi]
                        P.op("dve", lambda e, fc0=fc0, U4=U4: e.tensor_copy(out=U4(UF, 0, 15), in_=HST[:, fc0:fc0 + 2, :, :]),
                             reads=[HST.b], writes=[UF.b])
                        for s in range(2):
                            pb = quad[s * 2 + 1]
                            P.op("act", lambda e, s=s, pb=pb, o0=o0: e.activation(
                                out=UF[:, s, 0:188].rearrange("p (b x) -> p b x", x=47)[:, :, 15:47],
                                in_=pb.t[:, o0:o0 + 128].rearrange("p (b x) -> p b x", x=32), func=AF.Copy),
                                reads=[pb.b], writes=[UF.b])
                        cur = UF
                        bufs = [T1, T2]
                        bi = 0
                        sh = 1
                        lo = 0
                        while sh < w:
                            dstb = bufs[bi % 2]
                            lo2 = lo + sh
                            P.op("dve", lambda e, cur=cur, dstb=dstb, lo2=lo2, sh=sh, U4=U4: e.tensor_tensor(
                                out=U4(dstb, lo2, 47), in0=U4(cur, lo2, 47), in1=U4(cur, lo2 - sh, 47 - sh), op=ALU.add),
                                reads=[cur.b], writes=[dstb.b])
                            cur = dstb
                            bi += 1
                            lo = lo2
                            sh *= 2
                        for s in range(2):
                            P.op("dve", lambda e, s=s, cur=cur, sg=sg, w=w: e.scalar_tensor_tensor(
                                out=sg[:, s, Lp:Lp + 128].rearrange("p (b x) -> p b x", x=32),
                                in0=cur[:, s, 0:188].rearrange("p (b x) -> p b x", x=47)[:, :, 15:47], scalar=1.0 / w,
                                in1=UF[:, s, 0:188].rearrange("p (b x) -> p b x", x=47)[:, :, 15:47],
                                op0=ALU.mult, op1=ALU.subtract), reads=[cur.b, UF.b], writes=[sg.b])
                    r0 = blk * 256
                    dst = M1T[r0:r0 + 256, t0 * 128:t0 * 128 + ntok].rearrange("(s p) t -> p s t", p=128)
                    P.op("sp", lambda e, sg=sg, dst=dst: [e.dma_start(out=dst, in_=sg[:, :, 0:ntok])], reads=[sg.b], dma=sg.b)
                    if has_sample:
                        pb2 = pbanks[(c4["blk"] % 2) * 4:(c4["blk"] % 2) * 4 + 2]
                        c4["blk"] += 1
                        for j in range(2):
                            def em(e, j=j, Wt=Wt, pb2=pb2):
                                r = None
                                jj = ntile - 2 + j
                                for kc in range(KC):
                                    r = e.matmul(out=pb2[j].t[:, 0:256], lhsT=hT[:, kc, jj * 128:(jj + 1) * 128],
                                                 rhs=Wt[:, kc, 0:256], start=(kc == 0), stop=(kc == KC - 1))
                                return r
                            P.op("pe", em, reads=[Wt.b, hT.b], writes=[pb2[j].b])
                        so_ = stm[c4["tm"] % 2]
                        c4["tm"] += 1
                        for j in range(2):
                            P.op("act", lambda e, j=j, so_=so_, pb2=pb2: e.activation(out=so_[:, j, :], in_=pb2[j].t[:, 0:256], func=AF.Copy),
                                 reads=[pb2[j].b], writes=[so_.b])
                        P.op("sp", lambda e, so_=so_, blk=blk: [e.dma_start(
                            out=u_o[:, blk * 256:(blk + 1) * 256].rearrange("(j p) c -> p j c", p=128), in_=so_[:, :, :])],
                            reads=[so_.b], dma=so_.b)
            for gi, (t0, ntile) in enumerate(groups6):
                group4a(gi, t0, ntile)
            P.drain("sp")
            P.emit_block()
            P.release([t.b for t in [xb, xn, hT, UF, T1, T2, CR, HS, HST, ident, identf, icn, nwcol] + Wts + smt + stm])
            if stop == 4:
                return nc

        with ExitStack() as st:
            MT = T(P, st, "MT", [128, KC, 512], BF16)
            SG1 = T(P, st, "SG1", [128, KC, 512], BF16)
            Wts = [T(P, st, "w5t%d" % i, [128, KC, 512], BF16) for i in range(2)]
            GWt = [T(P, st, "gwt%d" % i, [128, 8, 512], BF16) for i in range(2)]
            xs = [T(P, st, "xs5%d" % i, [128, 4, 512], F32) for i in range(2)]
            so = [T(P, st, "so5%d" % i, [128, 4, 512], F32) for i in range(2)]
            psc = T(P, st, "psc", [128, KC], F32)
            pbanks = [T(P, st, "p5b%d" % i, [128, 512], F32, psum=True) for i in range(8)]
            c5 = {"g": 0, "w": 0}
            P.op("sp", lambda e: [e.dma_start(out=psc[:, :], in_=pscale[:, :])],
                 writes=[psc.b], dma=psc.b)
            def grp_fn1(t0, ntile):
                ntok = ntile * 128
                P.op("sp", lambda e, t0=t0, ntok=ntok: [e.dma_start(
                    out=MT[:, :, 0:ntok], in_=M1T[:, t0 * 128:t0 * 128 + ntok].rearrange("(kc p) q -> p kc q", p=128))],
                    writes=[MT.b], dma=MT.b)
                P.op("sp", lambda e, t0=t0, ntok=ntok: [e.dma_start(
                    out=SG1[:, :, 0:ntok], in_=SG1T[:, t0 * 128:t0 * 128 + ntok].rearrange("(kc p) q -> p kc q", p=128))],
                    writes=[SG1.b], dma=SG1.b)
                for g in range(4):
                    for half in range(2):
                        Gt = GWt[c5["g"] % 2]
                        pbs = pbanks[(c5["g"] % 2) * 4:(c5["g"] % 2) * 4 + 4]
                        c5["g"] += 1
                        src = gw[g, :, half * 512:(half + 1) * 512].rearrange("(kc p) n -> p kc n", p=128)
                        P.op("pool", lambda e, Gt=Gt, src=src: [e.dma_start(out=Gt[:, :, :], in_=src)], writes=[Gt.b], dma=Gt.b)
                        fm_mm(Gt, 512, MT, ntok, pbs, nk=8, kofs=g * 8)
                        for s in range(4):
                            fc = g * 8 + half * 4 + s
                            P.op("dve", lambda e, s=s, fc=fc, pbs=pbs, ntok=ntok: e.scalar_tensor_tensor(
                                out=SG1[:, fc, 0:ntok], in0=pbs[s].t[:, 0:ntok], scalar=psc[:, fc:fc + 1],
                                in1=SG1[:, fc, 0:ntok], op0=ALU.mult, op1=ALU.mult),
                                reads=[pbs[s].b, psc.b, SG1.b], writes=[SG1.b])
                for blk in range(8):
                    Wt = Wts[c5["w"] % 2]
                    xsb = xs[c5["w"] % 2]
                    sob = so[c5["w"] % 2]
                    pbs = pbanks[(c5["w"] % 2) * 4:(c5["w"] % 2) * 4 + 4]
                    c5["w"] += 1
                    load_w(Wt, pw_out, blk * 512, 512)
                    P.op("sp", lambda e, xsb=xsb, t0=t0, ntok=ntok, ntile=ntile, blk=blk: [e.dma_start(
                        out=xsb[:, 0:ntile, :], in_=X1[t0 * 128:t0 * 128 + ntok, blk * 512:(blk + 1) * 512].rearrange(
                            "(j p) c -> p j c", p=128))], writes=[xsb.b], dma=xsb.b)
                    tm_mm(Wt, 512, SG1, ntile, pbs)
                    for j in range(ntile):
                        P.op("dve", lambda e, j=j, sob=sob, xsb=xsb, pbs=pbs: e.tensor_tensor(
                            out=sob[:, j, :], in0=pbs[j].t[:, :], in1=xsb[:, j, :], op=ALU.add),
                            reads=[pbs[j].b, xsb.b], writes=[sob.b])
                    P.op("sp", lambda e, sob=sob, t0=t0, ntok=ntok, ntile=ntile, blk=blk: [e.dma_start(
                        out=X2[t0 * 128:t0 * 128 + ntok, blk * 512:(blk + 1) * 512].rearrange("(j p) c -> p j c", p=128),
                        in_=sob[:, 0:ntile, :])], reads=[sob.b], dma=sob.b)
            for (t0, ntile) in full_groups:
                grp_fn1(t0, ntile)
            P.drain("sp")
            P.emit_block()
            P.release([t.b for t in [MT, SG1, psc] + Wts + GWt + xs + so])
            if stop == 5:
                return nc

        with ExitStack() as st:
            xbs = [T(P, st, "x6b%d" % i, [128, D], F32) for i in range(2)]
            ybs = [T(P, st, "y6b%d" % i, [128, D], F32) for i in range(2)]
            fw = T(P, st, "fw", [128, D], F32)
            jk = T(P, st, "jk6", [128, D], BF16)
            ss = T(P, st, "ss6", [128, 2], F32)
            rs = T(P, st, "rs6", [128, 2], F32)
            epst = T(P, st, "epst6", [128, 1], F32)
            P.op("sp", lambda e: [e.dma_start(out=fw[:, :], in_=fnorm_w.partition_broadcast(128))], writes=[fw.b], dma=fw.b)
            P.op("dve", lambda e: e.memset(epst[:, :], EPS), writes=[epst.b])
            for t in range(NFULL):
                xb_, yb_ = xbs[t % 2], ybs[t % 2]
                j = t % 2
                P.op("sp", lambda e, xb_=xb_, t=t: [e.dma_start(out=xb_[:, :], in_=X2[t * 128:(t + 1) * 128, :])],
                     writes=[xb_.b], dma=xb_.b)
                P.op("act", lambda e, xb_=xb_, j=j: e.activation(out=jk[:, :], in_=xb_[:, :], func=AF.Square, accum_out=ss[:, j:j + 1]),
                     reads=[xb_.b], writes=[jk.b, ss.b])
                P.op("act", lambda e, j=j: e.activation(out=rs[:, j:j + 1], in_=ss[:, j:j + 1], func=AF.Sqrt, scale=1.0 / D,
                                                         bias=epst[:, 0:1]), reads=[ss.b, epst.b], writes=[rs.b])
                P.op("dve", lambda e, j=j: e.reciprocal(out=rs[:, j:j + 1], in_=rs[:, j:j + 1]), reads=[rs.b], writes=[rs.b])
                P.op("dve", lambda e, xb_=xb_, yb_=yb_, j=j: e.scalar_tensor_tensor(
                    out=yb_[:, :], in0=xb_[:, :], scalar=rs[:, j:j + 1], in1=fw[:, :], op0=ALU.mult, op1=ALU.mult),
                    reads=[xb_.b, rs.b, fw.b], writes=[yb_.b])
                P.op("sp", lambda e, yb_=yb_, t=t: [e.dma_start(out=y_o[t * 128:(t + 1) * 128, :], in_=yb_[:, :])],
                     reads=[yb_.b], dma=yb_.b)
            P.drain("sp")
            P.emit_block()
    return nc


def _rel_bucket_np(rel):
    nb = 16
    ret = np.where(rel > 0, nb, 0)
    n = np.abs(rel)
    max_exact = nb // 2
    nf = np.maximum(n, 1).astype(np.float32)
    large = max_exact + (np.log(nf / np.float32(max_exact)) / np.float32(math.log(128 / max_exact))
                         * np.float32(nb - max_exact)).astype(np.int32)
    large = np.minimum(large, nb - 1)
    return ret + np.where(n < max_exact, n, large)


def _consts():
    bf = ml_dtypes.bfloat16
    ident = np.eye(128, dtype=np.float32)
    i4 = np.concatenate([ident] * 4, axis=1)
    i4s = np.concatenate([np.eye(32, dtype=np.float32)] * 4, axis=1)
    rel = np.arange(384) - 255
    bk = _rel_bucket_np(rel.astype(np.int32))
    oh = (bk[None, :] == np.arange(32)[:, None]).astype(np.float32)
    adm = np.zeros((128, 128), np.float32)
    adm[:64, 64:] = BIGNEG
    return dict(c_ident=ident.astype(bf), c_identf=ident, c_i4=i4.astype(bf), c_i4s=i4s.astype(bf), c_oh=oh, c_adm=adm)


_NC_CACHE = {}


def make_in_maps(inputs, cores=range(8)):
    bf = ml_dtypes.bfloat16
    cs = _consts()
    x_prompt = inputs["x_prompt"]
    x_sample = inputs["x_sample"]
    maps = []
    for c in cores:
        b, h = c // 2, c % 2
        xf = np.zeros((TOKF, D), np.float32)
        xkv = np.zeros((NKV * 128, D), np.float32)
        if h == 1:
            xf[0:128] = x_prompt[b, 1920:2048]
            xkv[:] = x_prompt[b, 0:1920]
        xf[128:128 + 2048] = x_prompt[b, h * 2048:(h + 1) * 2048]
        xf[17 * 128:] = x_sample[4 * c:4 * c + 4].reshape(128, D)
        kb = np.zeros((NKEY,), np.float32)
        if h == 0:
            kb[0:2048] = BIGNEG
        ic = np.zeros((4, 128), np.float32)
        pos = h * 2048 + np.arange(128)
        for wi_, w in enumerate(POOL_WINDOWS):
            ic[wi_] = 1.0 / np.minimum(pos + 1, w)
        m = dict(
            xf=xf, xkv=xkv,
            w_in=inputs["attn_w_in"][0], w_out=inputs["attn_w_out"][0], pw_in=inputs["pool_w_in"][0],
            gw=inputs["pool_group_w"][0], pw_out=inputs["pool_w_out"][0],
            norm_w=inputs["norm_w"].reshape(2, KC, 128).transpose(0, 2, 1), fnorm_w=inputs["final_norm_w"],
            pscale=inputs["pool_scale"][0].reshape(KC, 128).T,
            rel_bias=inputs["rel_bias"],
            ck=inputs["cache_k"][0, 4 * c:4 * c + 4].reshape(4, 4096, 1024),
            cv=inputs["cache_v"][0, 4 * c:4 * c + 4].reshape(4, 4096, 1024),
            cki=inputs["cache_kidx"][0, 4 * c:4 * c + 4],
            spool=inputs["state_pool"][0, 4 * c:4 * c + 4],
            c_kbias=kb.astype(bf), c_invcnt=ic,
        )
        m.update(cs)
        maps.append({k: np.ascontiguousarray(v) for k, v in m.items()})
    return maps


def assemble(results):
    y_p = np.zeros((4, 4096, D), np.float32)
    y_s = np.zeros((32, 32, D), np.float32)
    k_p = np.zeros((1, 4, 4096, 8, 128), np.float32)
    v_p = np.zeros((1, 4, 4096, 8, 128), np.float32)
    ki_p = np.zeros((1, 4, 4096, 64), np.float32)
    pool_p = np.zeros((1, 4, 15, D), np.float32)
    k_s = np.zeros((1, 32, 32, 8, 128), np.float32)
    v_s = np.zeros((1, 32, 32, 8, 128), np.float32)
    ki_s = np.zeros((1, 32, 32, 64), np.float32)
    pool_s = np.zeros((1, 32, 15, D), np.float32)
    for c, r in enumerate(results):
        b, h = c // 2, c % 2
        sl = slice(h * 2048, (h + 1) * 2048)
        y_p[b, sl] = r["y_o"][128:128 + 2048]
        y_s[4 * c:4 * c + 4] = r["y_o"][17 * 128:].reshape(4, 32, D)
        k_p[0, b, sl] = r["k_o"][128:128 + 2048].reshape(2048, 8, 128)
        v_p[0, b, sl] = r["v_o"][128:128 + 2048].reshape(2048, 8, 128)
        ki_p[0, b, sl] = r["ki_o"][128:128 + 2048]
        k_s[0, 4 * c:4 * c + 4] = r["k_o"][17 * 128:].reshape(4, 32, 8, 128)
        v_s[0, 4 * c:4 * c + 4] = r["v_o"][17 * 128:].reshape(4, 32, 8, 128)
        ki_s[0, 4 * c:4 * c + 4] = r["ki_o"][17 * 128:].reshape(4, 32, 64)
        if h == 1:
            pool_p[0, b] = r["u_o"][128 - 15:128]
        pool_s[0, 4 * c:4 * c + 4] = r["u_o"][128:].reshape(4, 32, D)[:, 17:]
    return (y_p, y_s, k_p, v_p, ki_p, pool_p, k_s, v_s, ki_s, pool_s)


def kernel(**inputs):
    inputs = {k: np.asarray(v) for k, v in inputs.items()}
    if "nc" not in _NC_CACHE:
        _NC_CACHE["nc"] = build_program()
    nc = _NC_CACHE["nc"]
    in_maps = make_in_maps(inputs)
    res = run_bass_kernel_spmd(nc, in_maps, core_ids=list(range(8)))
    return assemble(res.results)
```

```python
import math
from contextlib import ExitStack
import numpy as np
import ml_dtypes
import concourse.bass as bass
import concourse.mybir as mybir
from concourse.bass_utils import run_bass_kernel_spmd

F32 = mybir.dt.float32
BF16 = mybir.dt.bfloat16
AF = mybir.ActivationFunctionType
ALU = mybir.AluOpType
AX = mybir.AxisListType

D = 4096
KC = 32
NFULL = 18
NKV = 15
TOKF = NFULL * 128
NKEY = 4224
ATTN_IN = 12384
NIT = 18
NEG = -30000.0
BIGNEG = -1.0e30
EPS = 1e-6
C_Q, C_K, C_V, C_QI, C_KI, C_WI, C_G = 0, 4096, 5120, 6144, 8192, 8256, 8288
POOL_WINDOWS = (2, 4, 8, 16)


class Buf:
    def __init__(self, name):
        self.name = name
        self.last_w = None
        self.readers = []
        self.sem = None
        self.semcount = 0


class Prog:
    ENG = ("sp", "act", "pool", "dve", "pe")

    def __init__(self, nc, stack):
        self.nc = nc
        self.stack = stack
        self.engs = {"sp": nc.sync, "act": nc.scalar, "pool": nc.gpsimd, "dve": nc.vector, "pe": nc.tensor}
        self.esem = {k: stack.enter_context(nc.semaphore("es_" + k)) for k in self.ENG}
        self.ecount = {k: 0 for k in self.ENG}
        self.waited = {k: {} for k in self.ENG}
        self.q = {k: [] for k in self.ENG}
        self.dma_events = {}
        self.nbuf = 0
        self.free_sems = []

    def buf(self, name=None):
        self.nbuf += 1
        return Buf(name or "b%d" % self.nbuf)

    def _bufsem(self, b):
        if b.sem is None:
            if self.free_sems:
                b.sem, b.semcount = self.free_sems.pop()
            else:
                b.sem = self.stack.enter_context(self.nc.semaphore("bs%d" % self.nbuf + b.name))
                b.semcount = 0
        return b.sem

    def release(self, bufs):
        return

    def op(self, eng, emit, reads=(), writes=(), dma=None, ndma=1):
        waits = {}

        def need(ev):
            if ev is None:
                return
            s, v = ev
            if eng == "pe" and s is self.esem["pe"]:
                return
            k = id(s)
            if k not in waits or waits[k][1] < v:
                waits[k] = (s, v)

        for b in reads:
            need(b.last_w)
        for b in writes:
            need(b.last_w)
            for r in b.readers:
                need(r)
        wl = []
        for k, (s, v) in waits.items():
            if self.waited[eng].get(k, 0) >= v:
                continue
            self.waited[eng][k] = v
            wl.append((s, v))
        if dma is not None:
            s = self._bufsem(dma)
            dma.semcount += 16 * ndma
            ev = (s, dma.semcount)
            self.dma_events[id(s)] = ev
        else:
            self.ecount[eng] += 1
            ev = (self.esem[eng], self.ecount[eng])
        for b in writes:
            b.last_w = ev
            b.readers = []
        for b in reads:
            if b not in writes:
                b.readers.append(ev)
                if len(b.readers) > 24:
                    b.readers = b.readers[-24:]
        self.q[eng].append((wl, emit, dma is not None, ev, ndma))
        return ev

    def drain(self, eng="sp"):
        wl = list(self.dma_events.values())
        for k in self.ENG:
            if k != eng and self.ecount[k] > 0:
                wl.append((self.esem[k], self.ecount[k]))
        wl2 = []
        for (s, v) in wl:
            if self.waited[eng].get(id(s), 0) >= v:
                continue
            self.waited[eng][id(s)] = v
            wl2.append((s, v))
        self.q[eng].append((wl2, None, False, None, 0))
        self.dma_events = {}

    def emit_block(self):
        nc = self.nc
        with nc.Block() as block:
            for k in self.ENG:
                if not self.q[k]:
                    continue
                ql = self.q[k]

                def body(e, ql=ql):
                    for (wl, emit, isdma, ev, ndma) in ql:
                        for (s, v) in wl:
                            e.wait_ge(s, v)
                        if emit is None:
                            continue
                        r = emit(e)
                        if isdma:
                            if not isinstance(r, (list, tuple)):
                                r = [r]
                            assert len(r) == ndma, (len(r), ndma)
                            for ins in r:
                                ins.then_inc(ev[0], 16)
                        else:
                            r.then_inc(ev[0], 1)

                getattr(block, {"sp": "sync", "act": "scalar", "pool": "gpsimd", "dve": "vector", "pe": "tensor"}[k])(body)
        self.q = {k: [] for k in self.ENG}


class T:
    def __init__(self, P, st, name, shape, dt, psum=False):
        nc = P.nc
        self.t = st.enter_context((nc.psum_tensor if psum else nc.sbuf_tensor)(name, shape, dt))
        self.b = P.buf(name)

    def __getitem__(self, k):
        return self.t[k]


def alt_evac(i):
    return "act" if i % 2 == 0 else "dve"


def build_program(stop=99, skip=()):
    nc = bass.Bass("TRN2", target_bir_lowering=False)

    def din(name, shape, dt=F32):
        return nc.dram_tensor(name, list(shape), dt, kind="ExternalInput").ap()

    def dout(name, shape, dt=F32):
        return nc.dram_tensor(name, list(shape), dt, kind="ExternalOutput").ap()

    def dscr(name, shape, dt):
        return nc.dram_tensor(name, list(shape), dt, kind="Internal").ap()

    xf = din("xf", [TOKF, D])
    xkv = din("xkv", [NKV * 128, D])
    w_in = din("w_in", [D, ATTN_IN])
    w_out = din("w_out", [D, D])
    pw_in = din("pw_in", [D, 2 * D])
    gw = din("gw", [4, 1024, 1024])
    pw_out = din("pw_out", [D, D])
    norm_w = din("norm_w", [2, 128, KC])
    fnorm_w = din("fnorm_w", [D])
    pscale = din("pscale", [128, KC])
    rel_bias = din("rel_bias", [32, 32])
    ck = din("ck", [4, 4096, 1024])
    cv = din("cv", [4, 4096, 1024])
    cki = din("cki", [4, 4096, 64])
    spool = din("spool", [4, 15, D])
    c_ident = din("c_ident", [128, 128], BF16)
    c_identf = din("c_identf", [128, 128], F32)
    c_i4 = din("c_i4", [128, 512], BF16)
    c_i4s = din("c_i4s", [32, 128], BF16)
    c_oh = din("c_oh", [32, 384], F32)
    c_adm = din("c_adm", [128, 128], F32)
    c_kbias = din("c_kbias", [NKEY], BF16)
    c_invcnt = din("c_invcnt", [4, 128], F32)

    y_o = dout("y_o", [TOKF, D])
    k_o = dout("k_o", [TOKF, 1024])
    v_o = dout("v_o", [TOKF, 1024])
    ki_o = dout("ki_o", [TOKF, 64])
    u_o = dout("u_o", [256, D])

    QT = dscr("QT", [D, TOKF], BF16)
    KT = dscr("KT", [5, 1024, NKEY], BF16)
    VB = dscr("VB", [5, 8, 128, 33, 128], BF16)
    QIT = dscr("QIT", [2048, TOKF], BF16)
    KIT = dscr("KIT", [5, 64, NKEY], BF16)
    WI = dscr("WI", [TOKF, 32], F32)
    SGT = dscr("SGT", [D, TOKF], BF16)
    OGT = dscr("OGT", [D, TOKF], BF16)
    X1 = dscr("X1", [TOKF, D], F32)
    M1T = dscr("M1T", [D, TOKF], BF16)
    SG1T = dscr("SG1T", [D, TOKF], BF16)
    X2 = dscr("X2", [TOKF, D], F32)

    with ExitStack() as gst:
        P = Prog(nc, gst)

        def load_w(Wt, w_ap, c0, ncols, k0=0, nk=KC):
            src = w_ap[k0 * 128:(k0 + nk) * 128, c0:c0 + ncols].rearrange("(kc p) n -> p kc n", p=128)
            P.op("pool", lambda e: [e.dma_start(out=Wt[:, 0:nk, 0:ncols], in_=src)], writes=[Wt.b], dma=Wt.b)

        def norm_tiles(src_ap, row0, ntile, xb, xn, ss, rstd):
            for j in range(ntile):
                r0 = row0 + j * 128
                P.op("sp", lambda e, r0=r0: [e.dma_start(out=xb[:, :], in_=src_ap[r0:r0 + 128, :])],
                     writes=[xb.b], dma=xb.b)
                P.op("act", lambda e, j=j: e.activation(out=xn[:, j, :], in_=xb[:, :], func=AF.Square,
                                                         accum_out=ss[:, j:j + 1]),
                     reads=[xb.b], writes=[xn.b, ss.b])
                P.op("act", lambda e, j=j: e.activation(out=rstd[:, j:j + 1], in_=ss[:, j:j + 1], func=AF.Sqrt,
                                                         scale=1.0 / D, bias=epst[:, 0:1]),
                     reads=[ss.b, epst.b], writes=[rstd.b])
                P.op("dve", lambda e, j=j: e.reciprocal(out=rstd[:, j:j + 1], in_=rstd[:, j:j + 1]),
                     reads=[rstd.b], writes=[rstd.b])
                P.op("dve", lambda e, j=j: e.tensor_scalar(out=xn[:, j, :], in0=xb[:, :], scalar1=rstd[:, j:j + 1],
                                                            scalar2=None, op0=ALU.mult),
                     reads=[xb.b, rstd.b], writes=[xn.b])

        def transpose_to_hT(xn, ntile, hT, nwcol, pbanks, bank0):
            ntok = ntile * 128
            for kp in range(KC // 2):
                pb = pbanks[(bank0 + kp) % len(pbanks)]
                pbv = pb.t[:, :].bitcast(BF16)

                def em(e, kp=kp, pbv=pbv):
                    r = None
                    for i in range(2):
                        kc = kp * 2 + i
                        for j in range(ntile):
                            r = e.transpose(out=pbv[:, i * 512 + j * 128:i * 512 + (j + 1) * 128],
                                            in_=xn[:, j, kc * 128:(kc + 1) * 128], identity=ident[:, :])
                    return r
                P.op("pe", em, reads=[xn.b, ident.b], writes=[pb.b])
                for i in range(2):
                    kc = kp * 2 + i
                    eng = alt_evac(kp)
                    if eng == "act":
                        P.op("act", lambda e, kc=kc, i=i, pbv=pbv: e.activation(
                            out=hT[:, kc, 0:ntok], in_=pbv[:, i * 512:i * 512 + ntok], func=AF.Copy,
                            scale=nwcol[:, kc:kc + 1]), reads=[pb.b, nwcol.b], writes=[hT.b])
                    else:
                        P.op("dve", lambda e, kc=kc, i=i, pbv=pbv: e.tensor_scalar(
                            out=hT[:, kc, 0:ntok], in0=pbv[:, i * 512:i * 512 + ntok], scalar1=nwcol[:, kc:kc + 1],
                            scalar2=None, op0=ALU.mult), reads=[pb.b, nwcol.b], writes=[hT.b])

        def fm_mm(Wt, ncols, act, ntok, pbs, nk=KC, kofs=0):
            nsub = (ncols + 127) // 128
            for s in range(nsub):
                cw = min(128, ncols - s * 128)

                def em(e, s=s, cw=cw):
                    r = None
                    for kc in range(nk):
                        r = e.matmul(out=pbs[s].t[0:cw, 0:ntok], lhsT=Wt[:, kc, s * 128:s * 128 + cw],
                                     rhs=act[:, kofs + kc, 0:ntok], start=(kc == 0), stop=(kc == nk - 1))
                    return r
                P.op("pe", em, reads=[Wt.b, act.b], writes=[pbs[s].b])

        def tm_mm(Wt, ncols, act, ntile, pbs):
            for j in range(ntile):
                def em(e, j=j):
                    r = None
                    for kc in range(KC):
                        r = e.matmul(out=pbs[j].t[:, 0:ncols], lhsT=act[:, kc, j * 128:(j + 1) * 128],
                                     rhs=Wt[:, kc, 0:ncols], start=(kc == 0), stop=(kc == KC - 1))
                    return r
                P.op("pe", em, reads=[Wt.b, act.b], writes=[pbs[j].b])

        def tm_resid_pass(act, w_ap, resid_ap, dst_ap, t0, ntile, Wts_, xs_, so_, pbanks_, cn):
            ntok = ntile * 128
            nb = (ntile + 1) // 2
            for blk in range(16):
                Wt = Wts_[cn['n'] % 2]
                xsb = xs_[cn['n'] % 2]
                sob = so_[cn['n'] % 2]
                pbs = pbanks_[(cn['n'] % 2) * 4:(cn['n'] % 2) * 4 + nb]
                cn['n'] += 1
                load_w(Wt, w_ap, blk * 256, 256)
                P.op("sp", lambda e, xsb=xsb, blk=blk: [e.dma_start(
                    out=xsb[:, 0:ntile, :], in_=resid_ap[t0 * 128:t0 * 128 + ntok, blk * 256:(blk + 1) * 256].rearrange(
                        "(j p) c -> p j c", p=128))], writes=[xsb.b], dma=xsb.b)
                for j in range(ntile):
                    def em(e, j=j, Wt=Wt, pbs=pbs):
                        r = None
                        for kc in range(KC):
                            r = e.matmul(out=pbs[j // 2].t[:, (j % 2) * 256:(j % 2) * 256 + 256],
                                         lhsT=act[:, kc, j * 128:(j + 1) * 128], rhs=Wt[:, kc, 0:256],
                                         start=(kc == 0 and j % 2 == 0), stop=(kc == KC - 1), skip_group_check=True)
                        return r
                    P.op("pe", em, reads=[Wt.b, act.b], writes=[pbs[j // 2].b])
                for jb in range(nb):
                    nj = min(2, ntile - jb * 2)
                    P.op("dve", lambda e, jb=jb, nj=nj, sob=sob, xsb=xsb, pbs=pbs: e.tensor_tensor(
                        out=sob[:, jb * 2:jb * 2 + nj, :], in0=pbs[jb].t[:, 0:nj * 256].rearrange("p (j c) -> p j c", c=256),
                        in1=xsb[:, jb * 2:jb * 2 + nj, :], op=ALU.add),
                        reads=[pbs[jb].b, xsb.b], writes=[sob.b])
                P.op("sp", lambda e, sob=sob, blk=blk: [e.dma_start(
                    out=dst_ap[t0 * 128:t0 * 128 + ntok, blk * 256:(blk + 1) * 256].rearrange("(j p) c -> p j c", p=128),
                    in_=sob[:, 0:ntile, :])], reads=[sob.b], dma=sob.b)

        groups6g = [(0, 6), (6, 6), (12, 6)]

        full_groups = [(0, 4), (4, 4), (8, 4), (12, 4), (16, 2)]
        kv_groups = [(0, 4), (4, 4), (8, 4), (12, 3)]

        def key_dst(ft):
            if ft < 17:
                return [(0, (15 + ft) * 128, 0, 128)]
            return [(1 + sb, 4096, sb * 32, 32) for sb in range(4)]

        with ExitStack() as st:
            xb = T(P, st, "xb", [128, D], F32)
            xn = T(P, st, "xn", [128, 4, D], BF16)
            hT = T(P, st, "hT", [128, KC, 512], BF16)
            Wts = [T(P, st, "wt%d" % i, [128, KC, 512], BF16) for i in range(2)]
            sfm = [T(P, st, "sfm%d" % i, [128, 4, 512], BF16) for i in range(2)]
            stm = [T(P, st, "stm%d" % i, [128, 4, 512], F32) for i in range(2)]
            svb = [T(P, st, "svb%d" % i, [128, 4, 512], BF16) for i in range(2)]
            ss = T(P, st, "ss", [128, 4], F32)
            rstd = T(P, st, "rstd", [128, 4], F32)
            nwcol = T(P, st, "nwcol", [128, KC], F32)
            epst = T(P, st, "epst", [128, 1], F32)
            ident = T(P, st, "ident", [128, 128], BF16)
            pbanks = [T(P, st, "pb%d" % i, [128, 512], F32, psum=True) for i in range(8)]
            cnt = {"w": 0, "fm": 0, "tm": 0, "vb": 0, "blk": 0}

            P.op("sp", lambda e: [e.dma_start(out=ident[:, :], in_=c_ident[:, :])], writes=[ident.b], dma=ident.b)
            P.op("sp", lambda e: [e.dma_start(out=nwcol[:, :], in_=norm_w[0, :, :])],
                 writes=[nwcol.b], dma=nwcol.b)
            P.op("dve", lambda e: e.memset(epst[:, :], EPS), writes=[epst.b])

            def nextW():
                w = Wts[cnt["w"] % 2]
                cnt["w"] += 1
                return w

            def banks4():
                b = cnt["blk"] % 2
                cnt["blk"] += 1
                return pbanks[b * 4:(b + 1) * 4]

            def evac_fm_to(pbs, nsub, ntok, scr_rows_ap_fn, func=None, scale=None):
                sg = sfm[cnt["fm"] % 2]
                cnt["fm"] += 1
                for s in range(nsub):
                    eng = "act" if func is not None else alt_evac(s)
                    if eng == "act":
                        P.op("act", lambda e, s=s: e.activation(out=sg[:, s, 0:ntok], in_=pbs[s].t[:, 0:ntok],
                                                                 func=(func or AF.Copy),
                                                                 scale=(scale if scale is not None else 1.0)),
                             reads=[pbs[s].b], writes=[sg.b])
                    else:
                        P.op("dve", lambda e, s=s: e.tensor_scalar(out=sg[:, s, 0:ntok], in0=pbs[s].t[:, 0:ntok],
                                                                    scalar1=(scale if scale is not None else 1.0),
                                                                    scalar2=None, op0=ALU.mult),
                             reads=[pbs[s].b], writes=[sg.b])
                return sg

            def proj_group(src_ap, t0, ntile, kind, do_norm=True, nxt=None):
                ntok = ntile * 128
                if do_norm:
                    norm_tiles(src_ap, t0 * 128, ntile, xb, xn, ss, rstd)
                if 'normonly' in skip:
                    return
                transpose_to_hT(xn, ntile, hT, nwcol, pbanks, 0)
                if 'tronly' in skip:
                    return
                if nxt is not None:
                    norm_tiles(nxt[0], nxt[1] * 128, nxt[2], xb, xn, ss, rstd)
                full = kind == "full"
                if full:
                    pieces = []
                    for j in range(ntile):
                        for (sq, k0, tk0, n) in key_dst(t0 + j):
                            pieces.append((sq, k0, j * 128 + tk0, n))
                    qc0 = t0 * 128
                else:
                    pieces = [(0, (t0 + j) * 128, j * 128, 128) for j in range(ntile)]
                merged = []
                for pc in pieces:
                    if merged and merged[-1][0] == pc[0] and merged[-1][1] + merged[-1][3] == pc[1] \
                            and merged[-1][2] + merged[-1][3] == pc[2]:
                        m = merged[-1]
                        merged[-1] = (m[0], m[1], m[2], m[3] + pc[3])
                    else:
                        merged.append(pc)

                if full:
                    for blk in (range(8) if 'noq' not in skip else []):
                        Wt = nextW()
                        load_w(Wt, w_in, C_Q + blk * 512, 512)
                        pbs = banks4()
                        fm_mm(Wt, 512, hT, ntok, pbs)
                        sg = evac_fm_to(pbs, 4, ntok, None, scale=128.0 ** -0.5)
                        dst = QT[blk * 512:(blk + 1) * 512, qc0:qc0 + ntok].rearrange("(s p) t -> p s t", p=128)
                        P.op("sp", lambda e, sg=sg, dst=dst: [e.dma_start(out=dst, in_=sg[:, :, 0:ntok])],
                             reads=[sg.b], dma=sg.b)
                for blk in (range(2) if 'nok' not in skip else []):
                    Wt = nextW()
                    load_w(Wt, w_in, C_K + blk * 512, 512)
                    pbs = banks4()
                    fm_mm(Wt, 512, hT, ntok, pbs)
                    sg = evac_fm_to(pbs, 4, ntok, None)
                    dl = []
                    for (sq, k0, tk0, n) in merged:
                        dl.append((KT[sq, blk * 512:(blk + 1) * 512, k0:k0 + n].rearrange("(s p) t -> p s t", p=128),
                                   tk0, n))
                    P.op("sp", lambda e, sg=sg, dl=dl: [e.dma_start(out=d, in_=sg[:, :, a:a + n]) for (d, a, n) in dl],
                         reads=[sg.b], dma=sg.b, ndma=len(dl))
                    if full:
                        pbs = banks4()
                        tm_mm(Wt, 512, hT, ntile, pbs)
                        sg = stm[cnt["tm"] % 2]
                        cnt["tm"] += 1
                        for j in range(ntile):
                            eng = alt_evac(j)
                            if eng == "act":
                                P.op("act", lambda e, j=j, sg=sg, pbs=pbs: e.activation(out=sg[:, j, :], in_=pbs[j].t[:, :], func=AF.Copy),
                                     reads=[pbs[j].b], writes=[sg.b])
                            else:
                                P.op("dve", lambda e, j=j, sg=sg, pbs=pbs: e.tensor_copy(out=sg[:, j, :], in_=pbs[j].t[:, :]),
                                     reads=[pbs[j].b], writes=[sg.b])
                        dst = k_o[t0 * 128:t0 * 128 + ntok, blk * 512:(blk + 1) * 512].rearrange("(j p) c -> p j c", p=128)
                        P.op("sp", lambda e, sg=sg, dst=dst: [e.dma_start(out=dst, in_=sg[:, 0:ntile, :])],
                             reads=[sg.b], dma=sg.b)
                for blk in (range(2) if 'nov' not in skip else []):
                    Wt = nextW()
                    load_w(Wt, w_in, C_V + blk * 512, 512)
                    pbs = banks4()
                    tm_mm(Wt, 512, hT, ntile, pbs)
                    sv = svb[cnt["vb"] % 2]
                    cnt["vb"] += 1
                    if full:
                        sg = stm[cnt["tm"] % 2]
                        cnt["tm"] += 1
                    for j in range(ntile):
                        if full:
                            P.op("dve", lambda e, j=j, sg=sg, pbs=pbs: e.tensor_copy(out=sg[:, j, :], in_=pbs[j].t[:, :]),
                                 reads=[pbs[j].b], writes=[sg.b])
                            P.op("act", lambda e, j=j, sv=sv, sg=sg: e.activation(out=sv[:, j, :], in_=sg[:, j, :], func=AF.Copy),
                                 reads=[sg.b], writes=[sv.b])
                        else:
                            P.op("act", lambda e, j=j, sv=sv, pbs=pbs: e.activation(out=sv[:, j, :], in_=pbs[j].t[:, :], func=AF.Copy),
                                 reads=[pbs[j].b], writes=[sv.b])
                    if full:
                        dst = v_o[t0 * 128:t0 * 128 + ntok, blk * 512:(blk + 1) * 512].rearrange("(j p) c -> p j c", p=128)
                        P.op("sp", lambda e, sg=sg, dst=dst: [e.dma_start(out=dst, in_=sg[:, 0:ntile, :])],
                             reads=[sg.b], dma=sg.b)
                    dl = []
                    for j in range(ntile):
                        if full:
                            pcs = key_dst(t0 + j)
                        else:
                            pcs = [(0, (t0 + j) * 128, 0, 128)]
                        for (sq, k0, tk0, n) in pcs:
                            c = k0 // 128
                            p0 = k0 % 128
                            d = VB[sq, blk * 4:(blk + 1) * 4, p0:p0 + n, c, :].rearrange("g p d -> p g d")
                            dl.append((d, j, tk0, n))
                    P.op("sp", lambda e, sv=sv, dl=dl: [
                        e.dma_start(out=d, in_=sv[tk0:tk0 + n, j, :].rearrange("p (g d) -> p g d", g=4))
                        for (d, j, tk0, n) in dl], reads=[sv.b], dma=sv.b, ndma=len(dl))
                if full:
                    for blk in (range(4) if 'noqi' not in skip else []):
                        Wt = nextW()
                        load_w(Wt, w_in, C_QI + blk * 512, 512)
                        pbs = banks4()
                        fm_mm(Wt, 512, hT, ntok, pbs)
                        sg = evac_fm_to(pbs, 4, ntok, None)
                        dst = QIT[blk * 512:(blk + 1) * 512, qc0:qc0 + ntok].rearrange("(s p) t -> p s t", p=128)
                        P.op("sp", lambda e, sg=sg, dst=dst: [e.dma_start(out=dst, in_=sg[:, :, 0:ntok])],
                             reads=[sg.b], dma=sg.b)
                if 'noki' in skip:
                    return
                Wt = nextW()
                load_w(Wt, w_in, C_KI, 128)
                pbs = banks4()
                fm_mm(Wt, 128, hT, ntok, pbs[0:1])
                sg = sfm[cnt["fm"] % 2]
                cnt["fm"] += 1
                P.op("act", lambda e, sg=sg, pbs=pbs: e.activation(out=sg[0:64, 0, 0:ntok], in_=pbs[0].t[0:64, 0:ntok], func=AF.Copy),
                     reads=[pbs[0].b], writes=[sg.b])
                dl = [(KIT[sq, :, k0:k0 + n], tk0, n) for (sq, k0, tk0, n) in merged]
                P.op("sp", lambda e, sg=sg, dl=dl: [e.dma_start(out=d, in_=sg[0:64, 0, a:a + n]) for (d, a, n) in dl],
                     reads=[sg.b], dma=sg.b, ndma=len(dl))
                if full:
                    pbs = banks4()
                    tm_mm(Wt, 128, hT, ntile, pbs)
                    sg = stm[cnt["tm"] % 2]
                    cnt["tm"] += 1
                    for j in range(ntile):
                        P.op("dve", lambda e, j=j, sg=sg, pbs=pbs: e.tensor_copy(out=sg[:, j, 0:64], in_=pbs[j].t[:, 0:64]),
                             reads=[pbs[j].b], writes=[sg.b])
                        P.op("dve", lambda e, j=j, sg=sg, pbs=pbs: e.tensor_scalar(out=sg[:, j, 64:96], in0=pbs[j].t[:, 64:96],
                                                                                    scalar1=1.0 / (8.0 * math.sqrt(32.0)), scalar2=None,
                                                                                    op0=ALU.mult),
                             reads=[pbs[j].b], writes=[sg.b])
                    d1 = ki_o[t0 * 128:t0 * 128 + ntok, :].rearrange("(j p) c -> p j c", p=128)
                    d2 = WI[t0 * 128:t0 * 128 + ntok, :].rearrange("(j p) c -> p j c", p=128)
                    P.op("sp", lambda e, sg=sg, d1=d1, d2=d2: [e.dma_start(out=d1, in_=sg[:, 0:ntile, 0:64]),
                                                              e.dma_start(out=d2, in_=sg[:, 0:ntile, 64:96])],
                         reads=[sg.b], dma=sg.b, ndma=2)
                    for blk in (range(8) if 'nogate' not in skip else []):
                        Wt = nextW()
                        load_w(Wt, w_in, C_G + blk * 512, 512)
                        pbs = banks4()
                        fm_mm(Wt, 512, hT, ntok, pbs)
                        sg = evac_fm_to(pbs, 4, ntok, None, func=AF.Silu)
                        dst = SGT[blk * 512:(blk + 1) * 512, qc0:qc0 + ntok].rearrange("(s p) t -> p s t", p=128)
                        P.op("sp", lambda e, sg=sg, dst=dst: [e.dma_start(out=dst, in_=sg[:, :, 0:ntok])],
                             reads=[sg.b], dma=sg.b)

            allg = [(xkv, t0, n, "kv") for (t0, n) in kv_groups] + [(xf, t0, n, "full") for (t0, n) in full_groups]
            for gi_, (src_, t0, n, kind_) in enumerate(allg):
                nxt_ = allg[gi_ + 1][0:3] if gi_ + 1 < len(allg) else None
                proj_group(src_, t0, n, kind_, do_norm=(gi_ == 0), nxt=nxt_)

            vdma = P.buf("vdma")
            ckt = Wts[0]
            ckit = Wts[1]
            for sb in (range(4) if 'S' not in skip else []):
                for g in range(8):
                    src = cv[sb, :, g * 128:(g + 1) * 128].rearrange("(c p) d -> p c d", p=128)
                    dst = VB[1 + sb, g, :, 0:32, :]
                    P.op("pool", lambda e, src=src, dst=dst: [e.dma_start(out=dst, in_=src)], dma=vdma)
                srck = cki[sb, :, :].rearrange("(c p) d -> p c d", p=128)
                P.op("pool", lambda e, srck=srck: [e.dma_start(out=ckit[:, 0:32, 0:64], in_=srck)], writes=[ckit.b], dma=ckit.b)
                for c8 in range(4):
                    pb = pbanks[c8 % 8]
                    pbv = pb.t[:, :].bitcast(BF16)

                    def em(e, c8=c8, pbv=pbv):
                        r = None
                        for i in range(8):
                            r = e.transpose(out=pbv[0:64, i * 128:(i + 1) * 128], in_=ckit[:, c8 * 8 + i, 0:64],
                                            identity=ident[:, :])
                        return r
                    P.op("pe", em, reads=[ckit.b, ident.b], writes=[pb.b])
                    sg = sfm[cnt["fm"] % 2]
                    cnt["fm"] += 1
                    P.op("act", lambda e, sg=sg, pbv=pbv: e.activation(out=sg[0:64, :, :].rearrange("p a b -> p (a b)")[:, 0:1024],
                                                                        in_=pbv[0:64, :], func=AF.Copy),
                         reads=[pb.b], writes=[sg.b])
                    dst = KIT[1 + sb, :, c8 * 1024:(c8 + 1) * 1024]
                    P.op("sp", lambda e, sg=sg, dst=dst: [e.dma_start(out=dst, in_=sg[0:64, :, :].rearrange("p a b -> p (a b)")[:, 0:1024])],
                         reads=[sg.b], dma=sg.b)
                for c4 in range(8):
                    srcc = ck[sb, c4 * 512:(c4 + 1) * 512, :].rearrange("(c p) n -> p c n", p=128)
                    ckv = ckt[:, :, :].rearrange("p a b -> p (a b)")
                    P.op("pool", lambda e, srcc=srcc, ckv=ckv: [e.dma_start(
                        out=ckv[:, 0:4096].rearrange("p (c n) -> p c n", c=4), in_=srcc)], writes=[ckt.b], dma=ckt.b)
                    for half in range(2):
                        sg = sfm[cnt["fm"] % 2]
                        cnt["fm"] += 1
                        for gg in range(4):
                            g = half * 4 + gg
                            pb = pbanks[(c4 * 8 + g) % 8]
                            pbv = pb.t[:, :].bitcast(BF16)

                            def em(e, g=g, pbv=pbv, ckv=ckv):
                                r = None
                                for c in range(4):
                                    r = e.transpose(out=pbv[:, c * 128:(c + 1) * 128],
                                                    in_=ckv[:, c * 1024 + g * 128:c * 1024 + (g + 1) * 128],
                                                    identity=ident[:, :])
                                return r
                            P.op("pe", em, reads=[ckt.b, ident.b], writes=[pb.b])
                            eng = alt_evac(gg)
                            if eng == "act":
                                P.op("act", lambda e, sg=sg, gg=gg, pbv=pbv: e.activation(out=sg[:, gg, :], in_=pbv[:, 0:512], func=AF.Copy),
                                     reads=[pb.b], writes=[sg.b])
                            else:
                                P.op("dve", lambda e, sg=sg, gg=gg, pbv=pbv: e.tensor_copy(out=sg[:, gg, :], in_=pbv[:, 0:512]),
                                     reads=[pb.b], writes=[sg.b])
                        dst = KT[1 + sb, half * 512:(half + 1) * 512, c4 * 512:(c4 + 1) * 512].rearrange("(g p) k -> p g k", p=128)
                        P.op("sp", lambda e, sg=sg, dst=dst: [e.dma_start(out=dst, in_=sg[:, :, :])], reads=[sg.b], dma=sg.b)
            P.drain("sp")
            P.emit_block()
            P.release([t.b for t in [xb, xn, hT] + Wts + sfm + stm + svb + [ident, nwcol]] + [vdma])
            if stop == 1:
                return nc

        with ExitStack() as st:
            SC = T(P, st, "SC", [128, NKEY], F32)
            MBs = [T(P, st, "MB%d" % i, [128, NKEY], BF16) for i in range(2)]
            KBI = T(P, st, "KBI", [128, NKEY], BF16)
            junk = T(P, st, "junk", [128, NKEY], BF16)
            QIs = [T(P, st, "QI%d" % i, [64, 32, 128], BF16) for i in range(2)]
            KIs = [T(P, st, "KI%d" % i, [64, NKEY], BF16) for i in range(2)]
            WIs = [T(P, st, "WIs%d" % i, [128, 32], F32) for i in range(2)]
            QTs = [T(P, st, "QTt%d" % i, [128, 32, 128], BF16) for i in range(2)]
            KTg = [T(P, st, "KTg%d" % i, [128, NKEY], BF16) for i in range(2)]
            Vg = [T(P, st, "Vg%d" % i, [128, 33, 129], BF16) for i in range(2)]
            rls = [T(P, st, "rl%d" % i, [128, 512], BF16) for i in range(4)]
            pTs = [T(P, st, "pT%d" % i, [128, 512], BF16) for i in range(3)]
            Ot = T(P, st, "Ot", [128, D], BF16)
            SGt = T(P, st, "SGt", [128, 32, 128], BF16)
            OGs = T(P, st, "OGs", [128, 32, 128], BF16)
            BT = T(P, st, "BT", [128, 2, 32, 128], BF16)
            ident = T(P, st, "ident2", [128, 128], BF16)
            i4 = T(P, st, "i4", [128, 512], BF16)
            i4s = T(P, st, "i4s", [32, 128], BF16)
            oh = T(P, st, "oh", [32, 384], F32)
            rb = T(P, st, "rb", [32, 32], F32)
            rb15 = T(P, st, "rb15", [32, 32], F32)
            adm = T(P, st, "adm", [128, 128], F32)
            sm = T(P, st, "sm", [128, 16], F32)
            den = T(P, st, "den", [128, 8], F32)
            pix = [T(P, st, "pix%d" % i, [128, 512], F32, psum=True) for i in range(2)]
            plg = [T(P, st, "plg%d" % i, [128, 512], F32, psum=True) for i in range(2)]
            poa = [T(P, st, "poa%d" % i, [128, 512], F32, psum=True) for i in range(2)]
            ptr = [T(P, st, "ptr%d" % i, [128, 512], F32, psum=True) for i in range(2)]

            for (t_, src) in ((ident, c_ident), (i4, c_i4), (i4s, c_i4s), (oh, c_oh), (rb, rel_bias), (adm, c_adm)):
                P.op("sp", lambda e, t_=t_, src=src: [e.dma_start(out=t_[:, :], in_=src[:, :])], writes=[t_.b], dma=t_.b)
            P.op("sp", lambda e: [e.dma_start(out=rb15[:, :], in_=rel_bias[15, :].partition_broadcast(32))],
                 writes=[rb15.b], dma=rb15.b)
            P.op("sp", lambda e: [e.dma_start(out=KBI[:, :], in_=c_kbias.partition_broadcast(128))],
                 writes=[KBI.b], dma=KBI.b)
            P.op("dve", lambda e: e.tensor_tensor(out=rb[:, :], in0=rb[:, :], in1=rb15[:, :], op=ALU.subtract),
                 reads=[rb15.b, rb.b], writes=[rb.b])
            P.op("dve", lambda e: e.memset(sm[:, 7:8], 0.5), writes=[sm.b])
            for v_ in Vg:
                P.op("dve", lambda e, v_=v_: e.memset(v_[:, :, 128:129], 1.0), writes=[v_.b])
            for dl in range(2):
                for q16 in range(8):
                    pb = ptr[(dl * 8 + q16) % 2]

                    def em(e, dl=dl, q16=q16, pb=pb):
                        r = None
                        for i in range(16):
                            ql = q16 * 16 + i
                            start = (-128 if dl == 1 else 0) - ql + 255
                            r = e.matmul(out=pb.t[:, i * 32:(i + 1) * 32], lhsT=oh[:, start:start + 128], rhs=rb[:, :],
                                         start=True, stop=True)
                        return r
                    P.op("pe", em, reads=[oh.b, rb.b], writes=[pb.b])
                    P.op("act", lambda e, dl=dl, q16=q16, pb=pb: e.activation(
                        out=BT[:, dl, :, q16 * 16:(q16 + 1) * 16],
                        in_=pb.t[:, :].rearrange("p (q h) -> p h q", h=32), func=AF.Copy),
                        reads=[pb.b], writes=[BT.b])

            tiles = []
            for ft in range(17):
                tiles.append(dict(seq=0, nq=128, qc=ft * 128, nch=16 + ft, last=128, prompt=True))
            for sb in range(4):
                tiles.append(dict(seq=1 + sb, nq=32, qc=17 * 128 + sb * 32, nch=33, last=32, prompt=False))
            cnt2 = {"rl": 0, "pix": 0, "pT": 0, "plg": 0, "kv": 0}
            def indexer(ti):
                tl = tiles[ti]
                nq, qc, nch, seq = tl["nq"], tl["qc"], tl["nch"], tl["seq"]
                nkeys = (nch - 1) * 128 + tl["last"]
                QI, KI, WIt, QTt, MB = QIs[ti % 2], KIs[ti % 2], WIs[ti % 2], QTs[ti % 2], MBs[ti % 2]
                P.op("sp", lambda e, QI=QI, qc=qc, nq=nq: [e.dma_start(
                    out=QI[:, :, 0:nq], in_=QIT[:, qc:qc + nq].rearrange("(h d) q -> d h q", d=64))],
                    writes=[QI.b], dma=QI.b)
                P.op("sp", lambda e, KI=KI, seq=seq, nkeys=nkeys: [e.dma_start(out=KI[:, 0:nkeys], in_=KIT[seq, :, 0:nkeys])],
                     writes=[KI.b], dma=KI.b)
                P.op("sp", lambda e, WIt=WIt, qc=qc, nq=nq: [e.dma_start(out=WIt[0:nq, :], in_=WI[qc:qc + nq, :])],
                     writes=[WIt.b], dma=WIt.b)
                P.op("sp", lambda e, QTt=QTt, qc=qc, nq=nq: [e.dma_start(
                    out=QTt[:, :, 0:nq], in_=QT[:, qc:qc + nq].rearrange("(h d) q -> d h q", d=128))],
                    writes=[QTt.b], dma=QTt.b)
                nblk = (nkeys + 511) // 512
                for kb in range(nblk):
                    k0 = kb * 512
                    bs = min(512, nkeys - k0)
                    for h in range(32):
                        pb = pix[cnt2["pix"] % 2]
                        cnt2["pix"] += 1
                        rl = rls[cnt2["rl"] % 4]
                        cnt2["rl"] += 1
                        P.op("pe", lambda e, pb=pb, QI=QI, KI=KI, h=h, k0=k0, bs=bs, nq=nq: e.matmul(
                            out=pb.t[0:nq, 0:bs], lhsT=QI[:, h, 0:nq], rhs=KI[:, k0:k0 + bs], start=True, stop=True),
                            reads=[QI.b, KI.b], writes=[pb.b])
                        P.op("act", lambda e, pb=pb, rl=rl, bs=bs, nq=nq: e.activation(
                            out=rl[0:nq, 0:bs], in_=pb.t[0:nq, 0:bs], func=AF.Relu), reads=[pb.b], writes=[rl.b])
                        if h == 0:
                            P.op("dve", lambda e, rl=rl, WIt=WIt, k0=k0, bs=bs, nq=nq: e.tensor_scalar(
                                out=SC[0:nq, k0:k0 + bs], in0=rl[0:nq, 0:bs], scalar1=WIt[0:nq, 0:1], scalar2=None,
                                op0=ALU.mult), reads=[rl.b, WIt.b], writes=[SC.b])
                        else:
                            P.op("dve", lambda e, rl=rl, WIt=WIt, h=h, k0=k0, bs=bs, nq=nq: e.scalar_tensor_tensor(
                                out=SC[0:nq, k0:k0 + bs], in0=rl[0:nq, 0:bs], scalar=WIt[0:nq, h:h + 1],
                                in1=SC[0:nq, k0:k0 + bs], op0=ALU.mult, op1=ALU.add),
                                reads=[rl.b, WIt.b, SC.b], writes=[SC.b])
                P.op("dve", lambda e, nq=nq, nkeys=nkeys: e.tensor_reduce(out=sm[0:nq, 1:2], in_=SC[0:nq, 0:nkeys],
                                                                          axis=AX.X, op=ALU.max),
                     reads=[SC.b], writes=[sm.b])
                P.op("dve", lambda e, nq=nq, nkeys=nkeys: e.tensor_reduce(out=sm[0:nq, 0:1], in_=SC[0:nq, 0:nkeys],
                                                                          axis=AX.X, op=ALU.min),
                     reads=[SC.b], writes=[sm.b])
                P.op("dve", lambda e, nq=nq: e.tensor_scalar(out=sm[0:nq, 1:2], in0=sm[0:nq, 1:2], scalar1=1.0, scalar2=None,
                                                              op0=ALU.add), reads=[sm.b], writes=[sm.b])
                P.op("dve", lambda e, nq=nq: e.tensor_scalar(out=sm[0:nq, 0:1], in0=sm[0:nq, 0:1], scalar1=-1.0, scalar2=None,
                                                              op0=ALU.add), reads=[sm.b], writes=[sm.b])
                if tl["prompt"]:
                    P.op("dve", lambda e, nkeys=nkeys: e.tensor_tensor(out=SC[:, 0:nkeys], in0=SC[:, 0:nkeys],
                                                                        in1=KBI[:, 0:nkeys], op=ALU.add),
                         reads=[SC.b, KBI.b], writes=[SC.b])
                    P.op("dve", lambda e, nkeys=nkeys: e.tensor_tensor(out=SC[:, nkeys - 128:nkeys], in0=SC[:, nkeys - 128:nkeys],
                                                                        in1=adm[:, :], op=ALU.add),
                         reads=[SC.b, adm.b], writes=[SC.b])
                for it in range(NIT):
                    P.op("dve", lambda e, nq=nq: e.scalar_tensor_tensor(out=sm[0:nq, 2:3], in0=sm[0:nq, 0:1], scalar=sm[0:nq, 1:2],
                                                                         in1=sm[0:nq, 7:8], op0=ALU.add, op1=ALU.mult),
                         reads=[sm.b], writes=[sm.b])
                    P.op("dve", lambda e, nq=nq, nkeys=nkeys: e.tensor_scalar(out=junk[0:nq, 0:nkeys], in0=SC[0:nq, 0:nkeys],
                                                                               scalar1=sm[0:nq, 2:3], scalar2=None, op0=ALU.is_ge,
                                                                               op1=ALU.add, accum_out=sm[0:nq, 3:4]),
                         reads=[sm.b, SC.b], writes=[sm.b, junk.b])
                    P.op("dve", lambda e, nq=nq: e.tensor_single_scalar(out=sm[0:nq, 4:5], in_=sm[0:nq, 3:4], scalar=255.5, op=ALU.is_ge),
                         reads=[sm.b], writes=[sm.b])
                    P.op("dve", lambda e, nq=nq: e.tensor_tensor(out=sm[0:nq, 5:6], in0=sm[0:nq, 2:3], in1=sm[0:nq, 0:1], op=ALU.subtract),
                         reads=[sm.b], writes=[sm.b])
                    P.op("dve", lambda e, nq=nq: e.tensor_tensor(out=sm[0:nq, 6:7], in0=sm[0:nq, 1:2], in1=sm[0:nq, 2:3], op=ALU.subtract),
                         reads=[sm.b], writes=[sm.b])
                    P.op("dve", lambda e, nq=nq: e.scalar_tensor_tensor(out=sm[0:nq, 0:1], in0=sm[0:nq, 5:6], scalar=sm[0:nq, 4:5],
                                                                         in1=sm[0:nq, 0:1], op0=ALU.mult, op1=ALU.add),
                         reads=[sm.b], writes=[sm.b])
                    P.op("dve", lambda e, nq=nq: e.scalar_tensor_tensor(out=sm[0:nq, 1:2], in0=sm[0:nq, 6:7], scalar=sm[0:nq, 4:5],
                                                                         in1=sm[0:nq, 2:3], op0=ALU.mult, op1=ALU.add),
                         reads=[sm.b], writes=[sm.b])
                P.op("dve", lambda e, nq=nq, nkeys=nkeys, MB=MB: e.tensor_scalar(out=MB[0:nq, 0:nkeys], in0=SC[0:nq, 0:nkeys],
                                                                                  scalar1=sm[0:nq, 0:1], scalar2=NEG, op0=ALU.is_lt,
                                                                                  op1=ALU.mult),
                     reads=[sm.b, SC.b], writes=[MB.b])
            def attend(ti):
                tl = tiles[ti]
                nq, qc, nch, seq = tl["nq"], tl["qc"], tl["nch"], tl["seq"]
                nkeys = (nch - 1) * 128 + tl["last"]
                QI, KI, WIt, QTt, MB = QIs[ti % 2], KIs[ti % 2], WIs[ti % 2], QTs[ti % 2], MBs[ti % 2]
                P.op("sp", lambda e, qc=qc, nq=nq: [e.dma_start(out=SGt[:, :, 0:nq],
                                                               in_=SGT[:, qc:qc + nq].rearrange("(kc p) q -> p kc q", p=128))],
                     writes=[SGt.b], dma=SGt.b)
                for g in range(8):
                    Kt, Vt = KTg[cnt2["kv"] % 2], Vg[cnt2["kv"] % 2]
                    cnt2["kv"] += 1
                    P.op("sp", lambda e, Kt=Kt, seq=seq, g=g, nkeys=nkeys: [e.dma_start(
                        out=Kt[:, 0:nkeys], in_=KT[seq, g * 128:(g + 1) * 128, 0:nkeys])], writes=[Kt.b], dma=Kt.b)
                    if tl["last"] == 128:
                        P.op("sp", lambda e, Vt=Vt, seq=seq, g=g, nch=nch: [e.dma_start(
                            out=Vt[:, 0:nch, 0:128], in_=VB[seq, g, :, 0:nch, :])], writes=[Vt.b], dma=Vt.b)
                    else:
                        P.op("sp", lambda e, Vt=Vt, seq=seq, g=g, nch=nch: [
                            e.dma_start(out=Vt[:, 0:nch - 1, 0:128], in_=VB[seq, g, :, 0:nch - 1, :]),
                            e.dma_start(out=Vt[0:32, nch - 1, 0:128], in_=VB[seq, g, 0:32, nch - 1, :])],
                            writes=[Vt.b], dma=Vt.b, ndma=2)
                    def issue_qk(c, Kt=Kt, g=g):
                            ksz = tl["last"] if c == nch - 1 else 128
                            near = c >= nch - 2
                            dl = 0 if c == nch - 1 else 1
                            lg = plg[cnt2["plg"] % 2]
                            cnt2["plg"] += 1
                            pT = pTs[cnt2["pT"] % 3]
                            cnt2["pT"] += 1
                            w4 = 4 * nq

                            def em(e, lg=lg, Kt=Kt, QTt=QTt, MB=MB, c=c, ksz=ksz, near=near, dl=dl, g=g, nq=nq, w4=w4):
                                e.matmul(out=lg.t[0:ksz, 0:w4], lhsT=Kt[:, c * 128:c * 128 + ksz],
                                         rhs=QTt[:, 4 * g:4 * g + 4, 0:nq], start=True, stop=False)
                                r = e.matmul(out=lg.t[0:ksz, 0:w4], lhsT=MB[0:nq, c * 128:c * 128 + ksz],
                                             rhs=(i4[:, :] if nq == 128 else i4s[:, :]), start=False, stop=(not near))
                                if near:
                                    r = e.matmul(out=lg.t[0:ksz, 0:w4], lhsT=ident[0:ksz, 0:ksz],
                                                 rhs=BT[0:ksz, dl, 4 * g:4 * g + 4, 0:nq], start=False, stop=True)
                                return r
                            P.op("pe", em, reads=[Kt.b, QTt.b, MB.b, i4.b, i4s.b, ident.b, BT.b], writes=[lg.b])
                            return (lg, pT, ksz, w4)

                    def issue_exp_pv(c, lg, pT, ksz, w4, Vt=Vt):
                            P.op("act", lambda e, lg=lg, pT=pT, ksz=ksz, w4=w4: e.activation(
                                out=pT[0:ksz, 0:w4], in_=lg.t[0:ksz, 0:w4], func=AF.Exp), reads=[lg.b], writes=[pT.b])

                            def em2(e, pT=pT, Vt=Vt, c=c, ksz=ksz, nq=nq, nch=nch):
                                r = None
                                for j in range(4):
                                    r = e.matmul(out=poa[j // 2].t[0:nq, (j % 2) * 129:(j % 2) * 129 + 129],
                                                 lhsT=pT[0:ksz, j * nq:(j + 1) * nq], rhs=Vt[0:ksz, c, 0:129],
                                                 start=(c == 0 and j % 2 == 0), stop=(c == nch - 1), skip_group_check=True)
                                return r
                            P.op("pe", em2, reads=[pT.b, Vt.b], writes=[poa[0].b, poa[1].b])

                    st_ = issue_qk(0)
                    for c in range(nch):
                        nx_ = issue_qk(c + 1) if c + 1 < nch else None
                        issue_exp_pv(c, *st_)
                        st_ = nx_
                    for j2 in range(2):
                        P.op("dve", lambda e, j2=j2, nq=nq: e.tensor_scalar(
                            out=den[0:nq, j2 * 2:j2 * 2 + 2], in0=poa[j2].t[0:nq, 0:258].rearrange("p (j d) -> p j d", d=129)[:, :, 128],
                            scalar1=1e-30, scalar2=None, op0=ALU.max), reads=[poa[j2].b], writes=[den.b])
                    P.op("dve", lambda e, nq=nq: e.reciprocal(out=den[0:nq, 0:4], in_=den[0:nq, 0:4]), reads=[den.b], writes=[den.b])
                    for j in range(4):
                        eng = "act" if j // 2 == 0 else "dve"
                        col = (4 * g + j) * 128
                        if eng == "act":
                            P.op("act", lambda e, j=j, col=col, nq=nq: e.activation(
                                out=Ot[0:nq, col:col + 128], in_=poa[j // 2].t[0:nq, (j % 2) * 129:(j % 2) * 129 + 128],
                                func=AF.Copy, scale=den[0:nq, j:j + 1]), reads=[poa[j // 2].b, den.b], writes=[Ot.b])
                        else:
                            P.op("dve", lambda e, j=j, col=col, nq=nq: e.tensor_scalar(
                                out=Ot[0:nq, col:col + 128], in0=poa[j // 2].t[0:nq, (j % 2) * 129:(j % 2) * 129 + 128],
                                scalar1=den[0:nq, j:j + 1], scalar2=None, op0=ALU.mult), reads=[poa[j // 2].b, den.b], writes=[Ot.b])
                for k8 in range(4):
                    pb = ptr[k8 % 2]
                    pbv = pb.t[:, :].bitcast(BF16)

                    def em(e, k8=k8, pbv=pbv, nq=nq):
                        r = None
                        for i in range(8):
                            kc = k8 * 8 + i
                            r = e.transpose(out=pbv[:, i * nq:(i + 1) * nq], in_=Ot[0:nq, kc * 128:(kc + 1) * 128],
                                            identity=ident[0:nq, 0:nq])
                        return r
                    P.op("pe", em, reads=[Ot.b, ident.b], writes=[pb.b])
                    P.op("dve", lambda e, k8=k8, pbv=pbv, nq=nq: e.tensor_tensor(
                        out=OGs[:, k8 * 8:(k8 + 1) * 8, 0:nq], in0=pbv[:, 0:8 * nq].rearrange("p (i q) -> p i q", q=nq),
                        in1=SGt[:, k8 * 8:(k8 + 1) * 8, 0:nq], op=ALU.mult), reads=[pb.b, SGt.b], writes=[OGs.b])
                P.op("sp", lambda e, qc=qc, nq=nq: [e.dma_start(out=OGT[:, qc:qc + nq].rearrange("(kc p) q -> p kc q", p=128),
                                                               in_=OGs[:, :, 0:nq])], reads=[OGs.b], dma=OGs.b)
            indexer(0)
            for ti in range(len(tiles)):
                if ti + 1 < len(tiles):
                    indexer(ti + 1)
                attend(ti)
            P.drain("sp")
            P.emit_block()
            P.release([t.b for t in [SC, KBI, junk, Ot, SGt, OGs, BT, ident, i4, i4s, oh, rb, rb15, adm] + MBs + QIs + KIs
                       + WIs + QTs + KTg + Vg + rls + pTs])
            if stop == 2:
                return nc

        with ExitStack() as st:
            OGt = T(P, st, "OGt", [128, KC, 768], BF16)
            Wts = [T(P, st, "w3t%d" % i, [128, KC, 256], BF16) for i in range(2)]
            xs = [T(P, st, "xs%d" % i, [128, 6, 256], F32) for i in range(2)]
            so = [T(P, st, "so%d" % i, [128, 6, 256], F32) for i in range(2)]
            pbanks = [T(P, st, "p3b%d" % i, [128, 512], F32, psum=True) for i in range(8)]
            c3d = {'n': 0}
            def grp_fn0(t0, ntile):
                ntok = ntile * 128
                P.op("sp", lambda e, t0=t0, ntok=ntok: [e.dma_start(
                    out=OGt[:, :, 0:ntok], in_=OGT[:, t0 * 128:t0 * 128 + ntok].rearrange("(kc p) q -> p kc q", p=128))],
                    writes=[OGt.b], dma=OGt.b)
                tm_resid_pass(OGt, w_out, xf, X1, t0, ntile, Wts, xs, so, pbanks, c3d)
            for (t0, ntile) in groups6g:
                grp_fn0(t0, ntile)
            P.drain("sp")
            P.emit_block()
            P.release([t.b for t in [OGt] + Wts + xs + so])
            if stop == 3:
                return nc

        with ExitStack() as st:
            xb = T(P, st, "xb4", [128, D], F32)
            xn = T(P, st, "xn4", [128, 6, D], BF16)
            hT = T(P, st, "hT4", [128, KC, 768], BF16)
            Wts = [T(P, st, "w4t%d" % i, [128, KC, 256], BF16) for i in range(2)]
            UF = T(P, st, "UF", [128, 2, 783], F32)
            T1 = T(P, st, "T1", [128, 2, 783], F32)
            T2 = T(P, st, "T2", [128, 2, 783], F32)
            CR = T(P, st, "CR", [128, KC, 15], F32)
            HS = T(P, st, "HS", [64, D], F32)
            HST = T(P, st, "HST", [128, KC, 4, 15], F32)
            smt = [T(P, st, "smt%d" % i, [128, 2, 768], BF16) for i in range(2)]
            stm = [T(P, st, "stm4%d" % i, [128, 2, 256], F32) for i in range(2)]
            ss = T(P, st, "ss4", [128, 8], F32)
            rstd = T(P, st, "rstd4", [128, 8], F32)
            nwcol = T(P, st, "nwcol4", [128, KC], F32)
            epst = T(P, st, "epst4", [128, 1], F32)
            ident = T(P, st, "ident4", [128, 128], BF16)
            identf = T(P, st, "identf4", [128, 128], F32)
            icn = T(P, st, "icn", [128, 4, 128], F32)
            pbanks = [T(P, st, "p4b%d" % i, [128, 512], F32, psum=True) for i in range(8)]
            c4 = {"w": 0, "blk": 0, "smt": 0, "tm": 0}

            P.op("sp", lambda e: [e.dma_start(out=ident[:, :], in_=c_ident[:, :])], writes=[ident.b], dma=ident.b)
            P.op("sp", lambda e: [e.dma_start(out=identf[:, :], in_=c_identf[:, :])], writes=[identf.b], dma=identf.b)
            P.op("sp", lambda e: [e.dma_start(out=nwcol[:, :], in_=norm_w[1, :, :])],
                 writes=[nwcol.b], dma=nwcol.b)
            P.op("sp", lambda e: [e.dma_start(out=icn[:, :, :].rearrange("p a b -> p (a b)"),
                                              in_=c_invcnt.rearrange("a b -> (a b)").partition_broadcast(128))],
                 writes=[icn.b], dma=icn.b)
            P.op("sp", lambda e: [e.dma_start(out=HS[0:60, :], in_=spool.rearrange("s i d -> (s i) d"))], writes=[HS.b], dma=HS.b)
            P.op("dve", lambda e: e.memset(epst[:, :], EPS), writes=[epst.b])
            P.op("dve", lambda e: e.memset(CR[:, :, :], 0.0), writes=[CR.b])
            for k4 in range(8):
                pb = pbanks[k4 % 8]

                def em(e, k4=k4, pb=pb):
                    r = None
                    for i in range(4):
                        kc = k4 * 4 + i
                        r = e.transpose(out=pb.t[:, i * 60:(i + 1) * 60], in_=HS[0:60, kc * 128:(kc + 1) * 128],
                                        identity=identf[0:60, 0:60])
                    return r
                P.op("pe", em, reads=[HS.b, identf.b], writes=[pb.b])
                P.op("dve", lambda e, k4=k4, pb=pb: e.tensor_copy(
                    out=HST[:, k4 * 4:(k4 + 1) * 4, :, :].rearrange("p k s i -> p k (s i)"),
                    in_=pb.t[:, 0:240].rearrange("p (k x) -> p k x", k=4)), reads=[pb.b], writes=[HST.b])

            def pool_mix(seg_views, w, widx, ntokseg, invc_first):
                V = seg_views
                L = ntokseg
                tot = L + 15
                cur = UF
                bufs = [T1, T2]
                bi = 0
                sh = 1
                lo = 0
                while sh < w:
                    dst = bufs[bi % 2]
                    lo2 = lo + sh
                    P.op("dve", lambda e, cur=cur, dst=dst, lo2=lo2, sh=sh, tot=tot: e.tensor_tensor(
                        out=V(dst, lo2, tot), in0=V(cur, lo2, tot), in1=V(cur, lo2 - sh, tot - sh), op=ALU.add),
                        reads=[cur.b], writes=[dst.b])
                    cur = dst
                    bi += 1
                    lo = lo2
                    sh *= 2
                return cur

            groups6 = [(0, 6), (6, 6), (12, 6)]

            def transpose6(xn_, ntile):
                ntok = ntile * 128
                for kc in range(KC):
                    pb = pbanks[kc % 8]
                    pbv = pb.t[:, :].bitcast(BF16)

                    def em(e, kc=kc, pbv=pbv):
                        r = None
                        for j in range(ntile):
                            r = e.transpose(out=pbv[:, j * 128:(j + 1) * 128], in_=xn_[:, j, kc * 128:(kc + 1) * 128],
                                            identity=ident[:, :])
                        return r
                    P.op("pe", em, reads=[xn_.b, ident.b], writes=[pb.b])
                    if kc % 2 == 0:
                        P.op("act", lambda e, kc=kc, pbv=pbv: e.activation(out=hT[:, kc, 0:ntok], in_=pbv[:, 0:ntok], func=AF.Copy,
                                                                          scale=nwcol[:, kc:kc + 1]),
                             reads=[pb.b, nwcol.b], writes=[hT.b])
                    else:
                        P.op("dve", lambda e, kc=kc, pbv=pbv: e.tensor_scalar(out=hT[:, kc, 0:ntok], in0=pbv[:, 0:ntok],
                                                                             scalar1=nwcol[:, kc:kc + 1], scalar2=None, op0=ALU.mult),
                             reads=[pb.b, nwcol.b], writes=[hT.b])

            def group4a(gi, t0, ntile):
                ntok = ntile * 128
                HW = ntok // 2
                has_sample = (t0 + ntile - 1) == 17
                npt = ntile - (1 if has_sample else 0)
                Lp = npt * 128
                if gi == 0:
                    norm_tiles(X1, t0 * 128, ntile, xb, xn, ss, rstd)
                transpose6(xn, ntile)
                if gi + 1 < len(groups6):
                    norm_tiles(X1, groups6[gi + 1][0] * 128, groups6[gi + 1][1], xb, xn, ss, rstd)
                for blk in range(32):
                    isu = blk < 16
                    Wt = Wts[c4["w"] % 2]
                    c4["w"] += 1
                    load_w(Wt, pw_in, blk * 256, 256)
                    quad = pbanks[(c4["blk"] % 2) * 4:(c4["blk"] % 2) * 4 + 4]
                    c4["blk"] += 1
                    for s in range(2):
                        for th in range(2):
                            pb = quad[s * 2 + th]

                            def em(e, s=s, th=th, pb=pb, Wt=Wt):
                                r = None
                                for kc in range(KC):
                                    r = e.matmul(out=pb.t[:, 0:HW], lhsT=Wt[:, kc, s * 128:(s + 1) * 128],
                                                 rhs=hT[:, kc, th * HW:(th + 1) * HW], start=(kc == 0), stop=(kc == KC - 1))
                                return r
                            P.op("pe", em, reads=[Wt.b, hT.b], writes=[pb.b])
                    sg = smt[c4["smt"] % 2]
                    c4["smt"] += 1
                    if not isu:
                        for s in range(2):
                            for th in range(2):
                                pb = quad[s * 2 + th]
                                P.op("act", lambda e, s=s, th=th, sg=sg, pb=pb: e.activation(
                                    out=sg[:, s, th * HW:(th + 1) * HW], in_=pb.t[:, 0:HW], func=AF.Silu),
                                    reads=[pb.b], writes=[sg.b])
                        r0 = (blk - 16) * 256
                        dst = SG1T[r0:r0 + 256, t0 * 128:t0 * 128 + ntok].rearrange("(s p) t -> p s t", p=128)
                        P.op("sp", lambda e, sg=sg, dst=dst: [e.dma_start(out=dst, in_=sg[:, :, 0:ntok])], reads=[sg.b], dma=sg.b)
                        continue
                    fc0 = blk * 2
                    widx = blk // 4
                    w = POOL_WINDOWS[widx]
                    P.op("dve", lambda e, fc0=fc0: e.tensor_copy(out=UF[:, :, 0:15], in_=CR[:, fc0:fc0 + 2, :]),
                         reads=[CR.b], writes=[UF.b])
                    for s in range(2):
                        for th in range(2):
                            lo_, hi_ = th * HW, min((th + 1) * HW, Lp)
                            if hi_ <= lo_:
                                continue
                            pb = quad[s * 2 + th]
                            P.op("act", lambda e, s=s, pb=pb, lo_=lo_, hi_=hi_: e.activation(
                                out=UF[:, s, 15 + lo_:15 + hi_], in_=pb.t[:, 0:hi_ - lo_], func=AF.Copy),
                                reads=[pb.b], writes=[UF.b])
                    P.op("dve", lambda e, fc0=fc0: e.tensor_copy(out=CR[:, fc0:fc0 + 2, :], in_=UF[:, :, Lp:Lp + 15]),
                         reads=[UF.b], writes=[CR.b])
                    Vp = lambda b, lo, hi: b[:, :, lo:hi]
                    S = pool_mix(Vp, w, widx, Lp, None)
                    if gi == 0:
                        for s in range(2):
                            P.op("dve", lambda e, S=S, widx=widx, s=s: e.tensor_tensor(
                                out=S[:, s, 15 + 128:15 + 256], in0=S[:, s, 15 + 128:15 + 256],
                                in1=icn[:, widx, :], op=ALU.mult), reads=[S.b, icn.b], writes=[S.b])
                        P.op("dve", lambda e, S=S, w=w: e.tensor_scalar(
                            out=S[:, :, 15:15 + 128], in0=S[:, :, 15:15 + 128], scalar1=1.0 / w, scalar2=None, op0=ALU.mult),
                            reads=[S.b], writes=[S.b])
                        P.op("dve", lambda e, S=S, w=w: e.tensor_scalar(
                            out=S[:, :, 15 + 256:15 + Lp], in0=S[:, :, 15 + 256:15 + Lp], scalar1=1.0 / w, scalar2=None,
                            op0=ALU.mult), reads=[S.b], writes=[S.b])
                        P.op("dve", lambda e, S=S, sg=sg: e.tensor_tensor(
                            out=sg[:, :, 0:Lp], in0=S[:, :, 15:15 + Lp], in1=UF[:, :, 15:15 + Lp], op=ALU.subtract),
                            reads=[S.b, UF.b], writes=[sg.b])
                    else:
                        P.op("dve", lambda e, S=S, sg=sg, w=w: e.scalar_tensor_tensor(
                            out=sg[:, :, 0:Lp], in0=S[:, :, 15:15 + Lp], scalar=1.0 / w, in1=UF[:, :, 15:15 + Lp],
                            op0=ALU.mult, op1=ALU.subtract), reads=[S.b, UF.b], writes=[sg.b])
                    if has_sample:
                        o0 = Lp - HW
                        U4 = lambda b, lo, hi: b[:, :, 0:188].rearrange("p s (b x) -> p s b x", x=47)[:, :, :, lo:hi]
                        P.op("dve", lambda e, fc0=fc0, U4=U4: e.tensor_copy(out=U4(UF, 0, 15), in_=HST[:, fc0:fc0 + 2, :, :]),
                             reads=[HST.b], writes=[UF.b])
                        for s in range(2):
                            pb = quad[s * 2 + 1]
                            P.op("act", lambda e, s=s, pb=pb, o0=o0: e.activation(
                                out=UF[:, s, 0:188].rearrange("p (b x) -> p b x", x=47)[:, :, 15:47],
                                in_=pb.t[:, o0:o0 + 128].rearrange("p (b x) -> p b x", x=32), func=AF.Copy),
                                reads=[pb.b], writes=[UF.b])
                        cur = UF
                        bufs = [T1, T2]
                        bi = 0
                        sh = 1
                        lo = 0
                        while sh < w:
                            dstb = bufs[bi % 2]
                            lo2 = lo + sh
                            P.op("dve", lambda e, cur=cur, dstb=dstb, lo2=lo2, sh=sh, U4=U4: e.tensor_tensor(
                                out=U4(dstb, lo2, 47), in0=U4(cur, lo2, 47), in1=U4(cur, lo2 - sh, 47 - sh), op=ALU.add),
                                reads=[cur.b], writes=[dstb.b])
                            cur = dstb
                            bi += 1
                            lo = lo2
                            sh *= 2
                        for s in range(2):
                            P.op("dve", lambda e, s=s, cur=cur, sg=sg, w=w: e.scalar_tensor_tensor(
                                out=sg[:, s, Lp:Lp + 128].rearrange("p (b x) -> p b x", x=32),
                                in0=cur[:, s, 0:188].rearrange("p (b x) -> p b x", x=47)[:, :, 15:47], scalar=1.0 / w,
                                in1=UF[:, s, 0:188].rearrange("p (b x) -> p b x", x=47)[:, :, 15:47],
                                op0=ALU.mult, op1=ALU.subtract), reads=[cur.b, UF.b], writes=[sg.b])
                    r0 = blk * 256
                    dst = M1T[r0:r0 + 256, t0 * 128:t0 * 128 + ntok].rearrange("(s p) t -> p s t", p=128)
                    P.op("sp", lambda e, sg=sg, dst=dst: [e.dma_start(out=dst, in_=sg[:, :, 0:ntok])], reads=[sg.b], dma=sg.b)
                    if has_sample:
                        pb2 = pbanks[(c4["blk"] % 2) * 4:(c4["blk"] % 2) * 4 + 2]
                        c4["blk"] += 1
                        for j in range(2):
                            def em(e, j=j, Wt=Wt, pb2=pb2):
                                r = None
                                jj = ntile - 2 + j
                                for kc in range(KC):
                                    r = e.matmul(out=pb2[j].t[:, 0:256], lhsT=hT[:, kc, jj * 128:(jj + 1) * 128],
                                                 rhs=Wt[:, kc, 0:256], start=(kc == 0), stop=(kc == KC - 1))
                                return r
                            P.op("pe", em, reads=[Wt.b, hT.b], writes=[pb2[j].b])
                        so_ = stm[c4["tm"] % 2]
                        c4["tm"] += 1
                        for j in range(2):
                            P.op("act", lambda e, j=j, so_=so_, pb2=pb2: e.activation(out=so_[:, j, :], in_=pb2[j].t[:, 0:256], func=AF.Copy),
                                 reads=[pb2[j].b], writes=[so_.b])
                        P.op("sp", lambda e, so_=so_, blk=blk: [e.dma_start(
                            out=u_o[:, blk * 256:(blk + 1) * 256].rearrange("(j p) c -> p j c", p=128), in_=so_[:, :, :])],
                            reads=[so_.b], dma=so_.b)
            for gi, (t0, ntile) in enumerate(groups6):
                group4a(gi, t0, ntile)
            P.drain("sp")
            P.emit_block()
            P.release([t.b for t in [xb, xn, hT, UF, T1, T2, CR, HS, HST, ident, identf, icn, nwcol] + Wts + smt + stm])
            if stop == 4:
                return nc

        with ExitStack() as st:
            MT = T(P, st, "MT", [128, KC, 768], BF16)
            SG1 = T(P, st, "SG1", [128, KC, 768], BF16)
            Wts = [T(P, st, "w5t%d" % i, [128, KC, 256], BF16) for i in range(2)]
            GWt = [T(P, st, "gwt%d" % i, [128, 8, 512], BF16) for i in range(2)]
            xs = [T(P, st, "xs5%d" % i, [128, 6, 256], F32) for i in range(2)]
            so = [T(P, st, "so5%d" % i, [128, 6, 256], F32) for i in range(2)]
            psc = T(P, st, "psc", [128, KC], F32)
            pbanks = [T(P, st, "p5b%d" % i, [128, 512], F32, psum=True) for i in range(8)]
            c5 = {"g": 0, "w": 0}
            c5w = {"n": 0}
            P.op("sp", lambda e: [e.dma_start(out=psc[:, :], in_=pscale[:, :])],
                 writes=[psc.b], dma=psc.b)
            def grp_fn1(t0, ntile):
                ntok = ntile * 128
                HW = ntok // 2
                P.op("sp", lambda e, t0=t0, ntok=ntok: [e.dma_start(
                    out=MT[:, :, 0:ntok], in_=M1T[:, t0 * 128:t0 * 128 + ntok].rearrange("(kc p) q -> p kc q", p=128))],
                    writes=[MT.b], dma=MT.b)
                P.op("sp", lambda e, t0=t0, ntok=ntok: [e.dma_start(
                    out=SG1[:, :, 0:ntok], in_=SG1T[:, t0 * 128:t0 * 128 + ntok].rearrange("(kc p) q -> p kc q", p=128))],
                    writes=[SG1.b], dma=SG1.b)
                for g in range(4):
                    for qt in range(4):
                        Gt = GWt[c5["g"] % 2]
                        quad = pbanks[(c5["g"] % 2) * 4:(c5["g"] % 2) * 4 + 4]
                        c5["g"] += 1
                        src = gw[g, :, qt * 256:(qt + 1) * 256].rearrange("(kc p) n -> p kc n", p=128)
                        P.op("pool", lambda e, Gt=Gt, src=src: [e.dma_start(out=Gt[:, :, 0:256], in_=src)], writes=[Gt.b], dma=Gt.b)
                        for s_ in range(2):
                            for th in range(2):
                                pb = quad[s_ * 2 + th]

                                def em(e, s_=s_, th=th, pb=pb, Gt=Gt, g=g):
                                    r = None
                                    for kc in range(8):
                                        r = e.matmul(out=pb.t[:, 0:HW], lhsT=Gt[:, kc, s_ * 128:(s_ + 1) * 128],
                                                     rhs=MT[:, g * 8 + kc, th * HW:(th + 1) * HW], start=(kc == 0), stop=(kc == 7))
                                    return r
                                P.op("pe", em, reads=[Gt.b, MT.b], writes=[pb.b])
                                fc = g * 8 + qt * 2 + s_
                                P.op("dve", lambda e, th=th, fc=fc, pb=pb: e.scalar_tensor_tensor(
                                    out=SG1[:, fc, th * HW:(th + 1) * HW], in0=pb.t[:, 0:HW], scalar=psc[:, fc:fc + 1],
                                    in1=SG1[:, fc, th * HW:(th + 1) * HW], op0=ALU.mult, op1=ALU.mult),
                                    reads=[pb.b, psc.b, SG1.b], writes=[SG1.b])
                tm_resid_pass(SG1, pw_out, X1, X2, t0, ntile, Wts, xs, so, pbanks, c5w)
            for (t0, ntile) in groups6g:
                grp_fn1(t0, ntile)
            P.drain("sp")
            P.emit_block()
            P.release([t.b for t in [MT, SG1, psc] + Wts + GWt + xs + so])
            if stop == 5:
                return nc

        with ExitStack() as st:
            xbs = [T(P, st, "x6b%d" % i, [128, D], F32) for i in range(2)]
            ybs = [T(P, st, "y6b%d" % i, [128, D], F32) for i in range(2)]
            fw = T(P, st, "fw", [128, D], F32)
            jk = T(P, st, "jk6", [128, D], BF16)
            ss = T(P, st, "ss6", [128, 2], F32)
            rs = T(P, st, "rs6", [128, 2], F32)
            epst = T(P, st, "epst6", [128, 1], F32)
            P.op("sp", lambda e: [e.dma_start(out=fw[:, :], in_=fnorm_w.partition_broadcast(128))], writes=[fw.b], dma=fw.b)
            P.op("dve", lambda e: e.memset(epst[:, :], EPS), writes=[epst.b])
            for t in range(NFULL):
                xb_, yb_ = xbs[t % 2], ybs[t % 2]
                j = t % 2
                P.op("sp", lambda e, xb_=xb_, t=t: [e.dma_start(out=xb_[:, :], in_=X2[t * 128:(t + 1) * 128, :])],
                     writes=[xb_.b], dma=xb_.b)
                P.op("act", lambda e, xb_=xb_, j=j: e.activation(out=jk[:, :], in_=xb_[:, :], func=AF.Square, accum_out=ss[:, j:j + 1]),
                     reads=[xb_.b], writes=[jk.b, ss.b])
                P.op("act", lambda e, j=j: e.activation(out=rs[:, j:j + 1], in_=ss[:, j:j + 1], func=AF.Sqrt, scale=1.0 / D,
                                                         bias=epst[:, 0:1]), reads=[ss.b, epst.b], writes=[rs.b])
                P.op("dve", lambda e, j=j: e.reciprocal(out=rs[:, j:j + 1], in_=rs[:, j:j + 1]), reads=[rs.b], writes=[rs.b])
                P.op("dve", lambda e, xb_=xb_, yb_=yb_, j=j: e.scalar_tensor_tensor(
                    out=yb_[:, :], in0=xb_[:, :], scalar=rs[:, j:j + 1], in1=fw[:, :], op0=ALU.mult, op1=ALU.mult),
                    reads=[xb_.b, rs.b, fw.b], writes=[yb_.b])
                P.op("sp", lambda e, yb_=yb_, t=t: [e.dma_start(out=y_o[t * 128:(t + 1) * 128, :], in_=yb_[:, :])],
                     reads=[yb_.b], dma=yb_.b)
            P.drain("sp")
            P.emit_block()
    return nc


def _rel_bucket_np(rel):
    nb = 16
    ret = np.where(rel > 0, nb, 0)
    n = np.abs(rel)
    max_exact = nb // 2
    nf = np.maximum(n, 1).astype(np.float32)
    large = max_exact + (np.log(nf / np.float32(max_exact)) / np.float32(math.log(128 / max_exact))
                         * np.float32(nb - max_exact)).astype(np.int32)
    large = np.minimum(large, nb - 1)
    return ret + np.where(n < max_exact, n, large)


def _consts():
    bf = ml_dtypes.bfloat16
    ident = np.eye(128, dtype=np.float32)
    i4 = np.concatenate([ident] * 4, axis=1)
    i4s = np.concatenate([np.eye(32, dtype=np.float32)] * 4, axis=1)
    rel = np.arange(384) - 255
    bk = _rel_bucket_np(rel.astype(np.int32))
    oh = (bk[None, :] == np.arange(32)[:, None]).astype(np.float32)
    adm = np.zeros((128, 128), np.float32)
    adm[:64, 64:] = BIGNEG
    return dict(c_ident=ident.astype(bf), c_identf=ident, c_i4=i4.astype(bf), c_i4s=i4s.astype(bf), c_oh=oh, c_adm=adm)


_NC_CACHE = {}


def make_in_maps(inputs, cores=range(8)):
    bf = ml_dtypes.bfloat16
    cs = _consts()
    x_prompt = inputs["x_prompt"]
    x_sample = inputs["x_sample"]
    maps = []
    for c in cores:
        b, h = c // 2, c % 2
        xf = np.zeros((TOKF, D), np.float32)
        xkv = np.zeros((NKV * 128, D), np.float32)
        if h == 1:
            xf[0:128] = x_prompt[b, 1920:2048]
            xkv[:] = x_prompt[b, 0:1920]
        xf[128:128 + 2048] = x_prompt[b, h * 2048:(h + 1) * 2048]
        xf[17 * 128:] = x_sample[4 * c:4 * c + 4].reshape(128, D)
        kb = np.zeros((NKEY,), np.float32)
        if h == 0:
            kb[0:2048] = BIGNEG
        ic = np.zeros((4, 128), np.float32)
        pos = h * 2048 + np.arange(128)
        for wi_, w in enumerate(POOL_WINDOWS):
            ic[wi_] = 1.0 / np.minimum(pos + 1, w)
        m = dict(
            xf=xf, xkv=xkv,
            w_in=inputs["attn_w_in"][0], w_out=inputs["attn_w_out"][0], pw_in=inputs["pool_w_in"][0],
            gw=inputs["pool_group_w"][0], pw_out=inputs["pool_w_out"][0],
            norm_w=inputs["norm_w"].reshape(2, KC, 128).transpose(0, 2, 1), fnorm_w=inputs["final_norm_w"],
            pscale=inputs["pool_scale"][0].reshape(KC, 128).T,
            rel_bias=inputs["rel_bias"],
            ck=inputs["cache_k"][0, 4 * c:4 * c + 4].reshape(4, 4096, 1024),
            cv=inputs["cache_v"][0, 4 * c:4 * c + 4].reshape(4, 4096, 1024),
            cki=inputs["cache_kidx"][0, 4 * c:4 * c + 4],
            spool=inputs["state_pool"][0, 4 * c:4 * c + 4],
            c_kbias=kb.astype(bf), c_invcnt=ic,
        )
        m.update(cs)
        maps.append({k: np.ascontiguousarray(v) for k, v in m.items()})
    return maps


def assemble(results):
    y_p = np.zeros((4, 4096, D), np.float32)
    y_s = np.zeros((32, 32, D), np.float32)
    k_p = np.zeros((1, 4, 4096, 8, 128), np.float32)
    v_p = np.zeros((1, 4, 4096, 8, 128), np.float32)
    ki_p = np.zeros((1, 4, 4096, 64), np.float32)
    pool_p = np.zeros((1, 4, 15, D), np.float32)
    k_s = np.zeros((1, 32, 32, 8, 128), np.float32)
    v_s = np.zeros((1, 32, 32, 8, 128), np.float32)
    ki_s = np.zeros((1, 32, 32, 64), np.float32)
    pool_s = np.zeros((1, 32, 15, D), np.float32)
    for c, r in enumerate(results):
        b, h = c // 2, c % 2
        sl = slice(h * 2048, (h + 1) * 2048)
        y_p[b, sl] = r["y_o"][128:128 + 2048]
        y_s[4 * c:4 * c + 4] = r["y_o"][17 * 128:].reshape(4, 32, D)
        k_p[0, b, sl] = r["k_o"][128:128 + 2048].reshape(2048, 8, 128)
        v_p[0, b, sl] = r["v_o"][128:128 + 2048].reshape(2048, 8, 128)
        ki_p[0, b, sl] = r["ki_o"][128:128 + 2048]
        k_s[0, 4 * c:4 * c + 4] = r["k_o"][17 * 128:].reshape(4, 32, 8, 128)
        v_s[0, 4 * c:4 * c + 4] = r["v_o"][17 * 128:].reshape(4, 32, 8, 128)
        ki_s[0, 4 * c:4 * c + 4] = r["ki_o"][17 * 128:].reshape(4, 32, 64)
        if h == 1:
            pool_p[0, b] = r["u_o"][128 - 15:128]
        pool_s[0, 4 * c:4 * c + 4] = r["u_o"][128:].reshape(4, 32, D)[:, 17:]
    return (y_p, y_s, k_p, v_p, ki_p, pool_p, k_s, v_s, ki_s, pool_s)


def kernel(**inputs):
    inputs = {k: np.asarray(v) for k, v in inputs.items()}
    if "nc" not in _NC_CACHE:
        _NC_CACHE["nc"] = build_program()
    nc = _NC_CACHE["nc"]
    in_maps = make_in_maps(inputs)
    res = run_bass_kernel_spmd(nc, in_maps, core_ids=list(range(8)))
    return assemble(res.results)
```
